# Optimizing a Trainium2 kernel written in Bass

```python
import math, functools
import jax, jax.numpy as jnp
from jax import lax
import numpy as np

D_MODEL = 2048
BATCH = 4
SEQ = 8192
DEPTH = 1
DEC_BATCH = 8
DEC_SEQ = 16
PAST_LEN = 1024

CHUNK = 64
Q_BLOCK = 128
EPS = 1e-6
NEG_INF = -1e30

MLA_HEADS = 8
MLA_Q_LORA = 512
MLA_KV_LORA = 256
MLA_NOPE = 128
MLA_ROPE = 64
MLA_V = 128
MLA_THETA = 10000.0

DIFF_HEADS = 8
DIFF_DH = 64
DIFF_VD = 2 * DIFF_DH
DIFF_ROT = DIFF_DH // 4
ROPE_THETA = 500000.0

D_FF = -(-(8 * D_MODEL) // (3 * 256)) * 256

DIFF_QK = DIFF_HEADS * 2 * DIFF_DH
C_QLAT = 0
C_KVLAT = C_QLAT + MLA_Q_LORA
C_KROPE = C_KVLAT + MLA_KV_LORA
C_DQ = C_KROPE + MLA_ROPE
C_DK = C_DQ + DIFF_QK
C_DV = C_DK + DIFF_QK
C_GA = C_DV + DIFF_HEADS * DIFF_VD
C_GB = C_GA + D_MODEL
IN_COLS = C_GB + D_MODEL

kernel_name = "hybrid_mla_diffattn_gated_streaming_step"


def _rmsnorm(x, g):
    xf = x.astype(jnp.float32)
    y = xf * lax.rsqrt(jnp.mean(xf * xf, axis=-1, keepdims=True) + EPS)
    return (y * g.astype(jnp.float32)).astype(x.dtype)


def _rope(x, pos, theta, rot_dim):
    half = rot_dim // 2
    inv = 1.0 / (jnp.float32(theta) ** (jnp.arange(half, dtype=jnp.float32) / half))
    ang = pos.astype(jnp.float32)[:, None] * inv[None, :]
    shape = (pos.shape[0],) + (1,) * (x.ndim - 3) + (half,)
    cos = jnp.cos(ang).reshape(shape)
    sin = jnp.sin(ang).reshape(shape)
    xf = x.astype(jnp.float32)
    x1 = xf[..., :half]
    x2 = xf[..., half:rot_dim]
    out = jnp.concatenate([x1 * cos - x2 * sin, x2 * cos + x1 * sin, xf[..., rot_dim:]], axis=-1)
    return out.astype(x.dtype)


def _chunk_mask(q_pos, k_pos):
    return (k_pos[None, :] // CHUNK) <= (q_pos[:, None] // CHUNK)


def _masked_softmax(s, mask):
    s = jnp.where(mask[None, None], s.astype(jnp.float32), NEG_INF)
    return jax.nn.softmax(s, axis=-1)


def _mla_attend(q_nope, q_rope, q_pos, k_nope, k_rope, v, k_pos):
    scale = (MLA_NOPE + MLA_ROPE) ** -0.5
    s = jnp.einsum('bqhd,bkhd->bhqk', q_nope, k_nope) + jnp.einsum('bqhr,bkr->bhqk', q_rope, k_rope)
    p = _masked_softmax(s * scale, _chunk_mask(q_pos, k_pos))
    return jnp.einsum('bhqk,bkhd->bqhd', p.astype(v.dtype), v)


def _diff_attend(q1, q2, q_pos, k1, k2, v, lam, k_pos):
    scale = DIFF_DH ** -0.5
    mask = _chunk_mask(q_pos, k_pos)
    p1 = _masked_softmax(jnp.einsum('bqhd,bkhd->bhqk', q1, k1) * scale, mask)
    p2 = _masked_softmax(jnp.einsum('bqhd,bkhd->bhqk', q2, k2) * scale, mask)
    a = p1 - lam * p2
    return jnp.einsum('bhqk,bkhe->bqhe', a.astype(v.dtype), v)


def _blockwise(fn, qs, q_pos):
    L = q_pos.shape[0]
    nb = L // Q_BLOCK

    def split(a):
        return jnp.moveaxis(a.reshape((a.shape[0], nb, Q_BLOCK) + a.shape[2:]), 1, 0)

    out = lax.map(lambda args: fn(*args), tuple(split(q) for q in qs) + (q_pos.reshape(nb, Q_BLOCK),))
    out = jnp.moveaxis(out, 0, 1)
    return out.reshape((out.shape[0], L) + out.shape[3:])


def _layer(x, pos, past, lw, lambda_init, blockwise):
    B, L, _ = x.shape
    h = _rmsnorm(x, lw['norm_mix'])
    proj = h @ lw['w_in']

    q_lat = _rmsnorm(proj[..., C_QLAT:C_KVLAT], lw['mla_q_norm'])
    q = (q_lat @ lw['mla_w_uq']).reshape(B, L, MLA_HEADS, MLA_NOPE + MLA_ROPE)
    q_nope = q[..., :MLA_NOPE]
    q_rope = _rope(q[..., MLA_NOPE:], pos, MLA_THETA, MLA_ROPE)
    ckv_new = _rmsnorm(proj[..., C_KVLAT:C_KROPE], lw['mla_kv_norm'])
    krope_new = _rope(proj[..., C_KROPE:C_DQ], pos, MLA_THETA, MLA_ROPE)

    dq = _rope(proj[..., C_DQ:C_DK].reshape(B, L, DIFF_HEADS, 2, DIFF_DH), pos, ROPE_THETA, DIFF_ROT)
    dk_new = _rope(proj[..., C_DK:C_DV].reshape(B, L, DIFF_HEADS, 2, DIFF_DH), pos, ROPE_THETA,
                   DIFF_ROT).reshape(B, L, DIFF_HEADS, 2 * DIFF_DH)
    dv_new = proj[..., C_DV:C_GA].reshape(B, L, DIFF_HEADS, DIFF_VD)

    gate_a = jax.nn.sigmoid(proj[..., C_GA:C_GB])
    gate_b = jax.nn.sigmoid(proj[..., C_GB:IN_COLS])

    if past is None:
        ckv, krope, dk, dv = ckv_new, krope_new, dk_new, dv_new
        k_pos = pos
    else:
        ckv = jnp.concatenate([past[0], ckv_new], axis=1)
        krope = jnp.concatenate([past[1], krope_new], axis=1)
        dk = jnp.concatenate([past[2], dk_new], axis=1)
        dv = jnp.concatenate([past[3], dv_new], axis=1)
        k_pos = jnp.arange(ckv.shape[1])
    T = ckv.shape[1]

    kv = (ckv @ lw['mla_w_ukv']).reshape(B, T, MLA_HEADS, MLA_NOPE + MLA_V)
    mla_fn = functools.partial(_mla_attend, k_nope=kv[..., :MLA_NOPE], k_rope=krope,
                               v=kv[..., MLA_NOPE:], k_pos=k_pos)

    lam = (jnp.exp(jnp.sum(lw['diff_lq1'].astype(jnp.float32) * lw['diff_lk1'].astype(jnp.float32)))
           - jnp.exp(jnp.sum(lw['diff_lq2'].astype(jnp.float32) * lw['diff_lk2'].astype(jnp.float32)))
           + lambda_init)
    dk5 = dk.reshape(B, T, DIFF_HEADS, 2, DIFF_DH)
    diff_fn = functools.partial(_diff_attend, k1=dk5[..., 0, :], k2=dk5[..., 1, :], v=dv,
                                lam=lam, k_pos=k_pos)

    qa = (q_nope, q_rope)
    qb = (dq[..., 0, :], dq[..., 1, :])
    if blockwise:
        out_a = _blockwise(mla_fn, qa, pos)
        out_b = _blockwise(diff_fn, qb, pos)
    else:
        out_a = mla_fn(*qa, pos)
        out_b = diff_fn(*qb, pos)
    out_b = _rmsnorm(out_b, lw['diff_subln']) * (1.0 - lambda_init)

    y_a = out_a.reshape(B, L, MLA_HEADS * MLA_V) @ lw['w_branch_a']
    y_b = out_b.reshape(B, L, DIFF_HEADS * DIFF_VD) @ lw['w_branch_b']
    x = x + (gate_a * y_a + gate_b * y_b) @ lw['w_out']

    h2 = _rmsnorm(x, lw['norm_ffn'])
    gu = h2 @ lw['w_ffn_in']
    x = x + (jax.nn.silu(gu[..., :D_FF]) * gu[..., D_FF:]) @ lw['w_ffn_out']
    return x, (ckv_new, krope_new, dk_new, dv_new)


def setup_inputs(seed: int = 0) -> dict:
    key = jax.random.key(seed)
    ks = jax.random.split(key, 24)
    f32 = jnp.float32

    def nrm(k, shape, scale=1.0):
        return jax.random.normal(k, shape, f32) * scale

    def gain(k, shape):
        return 1.0 + 0.02 * jax.random.normal(k, shape, f32)

    return {
        'x_prompt': nrm(ks[0], (BATCH, SEQ, D_MODEL)),
        'x_sample': nrm(ks[1], (DEC_BATCH, DEC_SEQ, D_MODEL)),
        'cache_mla_ckv': nrm(ks[2], (DEPTH, DEC_BATCH, PAST_LEN, MLA_KV_LORA)),
        'cache_mla_krope': nrm(ks[3], (DEPTH, DEC_BATCH, PAST_LEN, MLA_ROPE)),
        'cache_diff_k': nrm(ks[4], (DEPTH, DEC_BATCH, PAST_LEN, DIFF_HEADS, 2 * DIFF_DH)),
        'cache_diff_v': nrm(ks[5], (DEPTH, DEC_BATCH, PAST_LEN, DIFF_HEADS, DIFF_VD)),
        'norm_mix': gain(ks[6], (DEPTH, D_MODEL)),
        'w_in': nrm(ks[7], (DEPTH, D_MODEL, IN_COLS), D_MODEL ** -0.5),
        'mla_q_norm': gain(ks[8], (DEPTH, MLA_Q_LORA)),
        'mla_w_uq': nrm(ks[9], (DEPTH, MLA_Q_LORA, MLA_HEADS * (MLA_NOPE + MLA_ROPE)), MLA_Q_LORA ** -0.5),
        'mla_kv_norm': gain(ks[10], (DEPTH, MLA_KV_LORA)),
        'mla_w_ukv': nrm(ks[11], (DEPTH, MLA_KV_LORA, MLA_HEADS * (MLA_NOPE + MLA_V)), MLA_KV_LORA ** -0.5),
        'diff_lq1': nrm(ks[12], (DEPTH, DIFF_DH), 0.1),
        'diff_lk1': nrm(ks[13], (DEPTH, DIFF_DH), 0.1),
        'diff_lq2': nrm(ks[14], (DEPTH, DIFF_DH), 0.1),
        'diff_lk2': nrm(ks[15], (DEPTH, DIFF_DH), 0.1),
        'diff_subln': gain(ks[16], (DEPTH, DIFF_VD)),
        'w_branch_a': nrm(ks[17], (DEPTH, MLA_HEADS * MLA_V, D_MODEL), (MLA_HEADS * MLA_V) ** -0.5),
        'w_branch_b': nrm(ks[18], (DEPTH, DIFF_HEADS * DIFF_VD, D_MODEL), (DIFF_HEADS * DIFF_VD) ** -0.5),
        'w_out': nrm(ks[19], (DEPTH, D_MODEL, D_MODEL), D_MODEL ** -0.5),
        'norm_ffn': gain(ks[20], (DEPTH, D_MODEL)),
        'w_ffn_in': nrm(ks[21], (DEPTH, D_MODEL, 2 * D_FF), D_MODEL ** -0.5),
        'w_ffn_out': nrm(ks[22], (DEPTH, D_FF, D_MODEL), D_FF ** -0.5),
        'norm_final': gain(ks[23], (D_MODEL,)),
    }


def reference(x_prompt, x_sample, cache_mla_ckv, cache_mla_krope, cache_diff_k, cache_diff_v,
              norm_mix, w_in, mla_q_norm, mla_w_uq, mla_kv_norm, mla_w_ukv,
              diff_lq1, diff_lk1, diff_lq2, diff_lk2, diff_subln,
              w_branch_a, w_branch_b, w_out, norm_ffn, w_ffn_in, w_ffn_out, norm_final):
    pos_p = jnp.arange(x_prompt.shape[1])
    pos_s = cache_mla_ckv.shape[2] + jnp.arange(x_sample.shape[1])
    xp, xs = x_prompt, x_sample
    st_p = ([], [], [], [])
    st_s = ([], [], [], [])
    for l in range(DEPTH):
        lambda_init = 0.8 - 0.6 * math.exp(-0.3 * l)
        lw = {
            'norm_mix': norm_mix[l], 'w_in': w_in[l],
            'mla_q_norm': mla_q_norm[l], 'mla_w_uq': mla_w_uq[l],
            'mla_kv_norm': mla_kv_norm[l], 'mla_w_ukv': mla_w_ukv[l],
            'diff_lq1': diff_lq1[l], 'diff_lk1': diff_lk1[l],
            'diff_lq2': diff_lq2[l], 'diff_lk2': diff_lk2[l], 'diff_subln': diff_subln[l],
            'w_branch_a': w_branch_a[l], 'w_branch_b': w_branch_b[l], 'w_out': w_out[l],
            'norm_ffn': norm_ffn[l], 'w_ffn_in': w_ffn_in[l], 'w_ffn_out': w_ffn_out[l],
        }
        xp, rows_p = _layer(xp, pos_p, None, lw, lambda_init, True)
        past = (cache_mla_ckv[l], cache_mla_krope[l], cache_diff_k[l], cache_diff_v[l])
        xs, rows_s = _layer(xs, pos_s, past, lw, lambda_init, False)
        for i in range(4):
            st_p[i].append(rows_p[i])
            st_s[i].append(rows_s[i])
    y_prompt = _rmsnorm(xp, norm_final)
    y_sample = _rmsnorm(xs, norm_final)
    new_ckv_p = jnp.stack(st_p[0], axis=0)
    new_krope_p = jnp.stack(st_p[1], axis=0)
    new_dk_p = jnp.stack(st_p[2], axis=0)
    new_dv_p = jnp.stack(st_p[3], axis=0)
    new_ckv_s = jnp.stack(st_s[0], axis=0)
    new_krope_s = jnp.stack(st_s[1], axis=0)
    new_dk_s = jnp.stack(st_s[2], axis=0)
    new_dv_s = jnp.stack(st_s[3], axis=0)
    return (y_prompt, y_sample, new_ckv_p, new_krope_p, new_dk_p, new_dv_p,
            new_ckv_s, new_krope_s, new_dk_s, new_dv_s)
```

```python
import contextlib
import numpy as np
import ml_dtypes
import concourse.bass as bass
import concourse.mybir as mybir
from concourse.bass_utils import run_bass_kernel_spmd

F32 = mybir.dt.float32
BF16 = mybir.dt.bfloat16
AF = mybir.ActivationFunctionType
DBG = {"cut": 9}
ALU = mybir.AluOpType

D = 2048
T = 512
DFF = 5632
EPS = 1e-6
NEG = -30000.0
LAMBDA_INIT = 0.8 - 0.6 * 1.0
SC_MLA = float((128 + 64) ** -0.5)
SC_DIFF = float(64 ** -0.5)
NBLK = 19
NKT = NBLK * 4
RW = 640


class Res:
    __slots__ = ("name", "w", "r", "arena", "multi", "ws", "excl")

    def __init__(self, name, arena=False, multi=False, excl=False):
        self.name = name
        self.w = None
        self.r = {}
        self.arena = arena
        self.multi = multi
        self.ws = {}
        self.excl = excl


class Trk:
    SEM_LIMIT = 30000

    def __init__(self, nc, n_dma_sems=14):
        self.nc = nc
        self.engs = {"pe": nc.tensor, "act": nc.scalar, "dve": nc.vector, "pool": nc.gpsimd, "sp": nc.sync}
        self.sems = {}
        self.owner = {}
        self.cur = {}
        self.gen = {}
        self.waited = {e: {} for e in self.engs}
        self.pending = {e: {} for e in self.engs}
        self.arena_evs = {}
        self.nwaits = 0
        self.nops = 0
        self._stack = []
        for e in ("pe", "act", "dve", "pool"):
            self.gen[e] = 0
            self._new_sem(e)
        self.dq = {}
        for q in ("sp", "pool"):
            lst = []
            for i in range(n_dma_sems):
                k = f"d_{q}{i}"
                self.sems[k] = self._alloc(k)
                self.owner[k] = "dma"
                lst.append([k, 0])
            self.dq[q] = [lst, 0]

    def _alloc(self, name):
        cm = self.nc.semaphore(name)
        h = cm.__enter__()
        self._stack.append(cm)
        return h

    def _new_sem(self, e):
        k = f"s_{e}{self.gen[e]}"
        self.gen[e] += 1
        self.sems[k] = self._alloc(k)
        self.owner[k] = e
        self.cur[e] = [k, 0]

    def close(self):
        for cm in reversed(self._stack):
            cm.__exit__(None, None, None)

    def _need(self, eng, evs):
        for (k, v) in evs:
            if self.waited[eng].get(k, 0) < v:
                self.engs[eng].wait_ge(self.sems[k], v)
                self.waited[eng][k] = v
                self.nwaits += 1

    def _deps(self, eng, reads, writes):
        evs = []
        arena = False
        for r in reads:
            arena |= r.arena
            if r.multi:
                evs.extend(r.ws.items())
            elif r.w is not None and not (eng == "pe" and self.owner[r.w[0]] == "pe"):
                evs.append(r.w)
            if r.excl:
                for k, v in r.r.items():
                    if self.owner[k] != eng:
                        evs.append((k, v))
        for w in writes:
            arena |= w.arena
            if w.multi:
                pass
            elif w.w is not None and not (eng == "pe" and self.owner[w.w[0]] == "pe"):
                evs.append(w.w)
            for k, v in w.r.items():
                if not (eng == "pe" and self.owner[k] == "pe"):
                    evs.append((k, v))
        if arena and self.pending[eng]:
            evs.extend(self.pending[eng].items())
            self.pending[eng] = {}
        self._need(eng, evs)

    def _commit(self, ev, reads, writes):
        k, v = ev
        for r in reads:
            if r.r.get(k, 0) < v:
                r.r[k] = v
            if r.arena:
                self.arena_evs[k] = max(self.arena_evs.get(k, 0), v)
        for w in writes:
            if w.multi:
                w.ws[k] = max(w.ws.get(k, 0), v)
            else:
                w.w = ev
                w.r = {}
            if w.arena:
                self.arena_evs[k] = max(self.arena_evs.get(k, 0), v)

    def barrier(self):
        for e in self.engs:
            p = self.pending[e]
            for k, v in self.arena_evs.items():
                if p.get(k, 0) < v:
                    p[k] = v
        self.arena_evs = {}

    def op(self, eng, fn, reads=(), writes=()):
        self._deps(eng, reads, writes)
        c = self.cur[eng]
        if c[1] >= self.SEM_LIMIT:
            self._new_sem(eng)
            c = self.cur[eng]
        ins = fn()
        c[1] += 1
        ins.then_inc(self.sems[c[0]], 1)
        self.nops += 1
        self._commit((c[0], c[1]), reads, writes)

    def dma(self, q, out, in_, reads=(), writes=()):
        self._deps(q, reads, writes)
        lst, idx = self.dq[q]
        slot = lst[idx]
        self.dq[q][1] = (idx + 1) % len(lst)
        if slot[1] > 0:
            self._need(q, [(slot[0], slot[1])])
        slot[1] += 16
        self.engs[q].dma_start(out=out, in_=in_).then_inc(self.sems[slot[0]], 16)
        self.nops += 1
        self._commit((slot[0], slot[1]), reads, writes)

    def finish(self):
        evs = []
        for q in self.dq:
            for s in self.dq[q][0]:
                if s[1] > 0:
                    evs.append((s[0], s[1]))
        for e in ("pe", "act", "dve", "pool"):
            if self.cur[e][1] > 0:
                evs.append((self.cur[e][0], self.cur[e][1]))
        self._need("pool", evs)
        self._need("sp", evs)


class Ring:
    def __init__(self, items):
        self.items = items
        self.i = 0

    def next(self):
        it = self.items[self.i]
        self.i = (self.i + 1) % len(self.items)
        return it


def build_program(npair=8, do_sample=True, stage=9):
    nc = bass.Bass("TRN2", target_bir_lowering=False)
    tr = Trk(nc)
    V, S_, P_ = nc.vector, nc.scalar, nc.tensor

    def din(name, shape, dt=F32):
        return nc.dram_tensor(name, list(shape), dt, kind="ExternalInput").ap()

    def dout(name, shape, dt=F32):
        return nc.dram_tensor(name, list(shape), dt, kind="ExternalOutput").ap()

    def dscr(name, shape, dt=BF16):
        return nc.dram_tensor(name, list(shape), dt, kind="Internal").ap()

    xo = din("xo", [8, T, D]); xk = din("xk", [8, T, D]); xs = din("xs", [16, D])
    rope_o = din("rope_o", [8, T, RW]); rope_k = din("rope_k", [8, T, RW]); rope_s = din("rope_s", [16, RW])
    c_ckv = din("c_ckv", [1024, 256]); c_kr = din("c_kr", [1024, 64])
    c_dk = din("c_dk", [1024, 1024]); c_dv = din("c_dv", [1024, 1024])
    w_in = din("w_in", [D, 8000]); w_uq = din("w_uq", [512, 1536]); w_ukv = din("w_ukv", [256, 2048])
    w_ba = din("w_ba", [1024, D]); w_bb = din("w_bb", [1024, D]); w_out = din("w_out", [D, D])
    w_fi = din("w_fi", [D, 2 * DFF]); w_fo = din("w_fo", [DFF, D])
    g_mix = din("g_mix", [D]); g_q = din("g_q", [512]); g_kv = din("g_kv", [256])
    lq1 = din("lq1", [64]); lk1 = din("lk1", [64]); lq2 = din("lq2", [64]); lk2 = din("lk2", [64])
    g_sub = din("g_sub", [128]); g_ffn = din("g_ffn", [D]); g_fin = din("g_fin", [D])
    ident_d = din("ident", [128, 128], BF16); obias_d = din("obias", [128, 1])

    yo = dout("yo", [8, T, D]); ckv_o = dout("ckv_o", [8, T, 256]); kr_o = dout("kr_o", [8, T, 64])
    dk_o = dout("dk_o", [8, T, 1024]); dv_o = dout("dv_o", [8, T, 1024])
    ys = dout("ys", [16, D]); ckv_s = dout("ckv_s", [16, 256]); kr_s = dout("kr_s", [16, 64])
    dk_s = dout("dk_s", [16, 1024]); dv_s = dout("dv_s", [16, 1024])

    WA = [dscr(f"WA{i}", [128, 16, 512]) for i in range(8)]
    WG = [dscr(f"WG{i}", [128, 16, 512]) for i in range(8)]
    WUQ = dscr("WUQ", [128, 4, 1536])
    WUKV = dscr("WUKV", [128, 2, 2048])
    WBA = [dscr(f"WBA{i}", [128, 8, 512]) for i in range(4)]
    WBB = [dscr(f"WBB{i}", [128, 8, 512]) for i in range(4)]
    WO = [dscr(f"WO{i}", [128, 16, 512]) for i in range(4)]
    WFG = [dscr(f"WFG{i}", [128, 16, 512]) for i in range(11)]
    WFU = [dscr(f"WFU{i}", [128, 16, 512]) for i in range(11)]
    WFO = [dscr(f"WFO{i}", [128, 4, 2048]) for i in range(11)]
    KN = dscr("KN", [8, 128, NBLK * T]); KR = dscr("KR", [64, NBLK * T]); DK = dscr("DK", [8, 128, NBLK * T])
    VM = dscr("VM", [8, 128, NKT, 129]); DV = dscr("DV", [8, 128, NKT, 129])
    R_w = {}

    def wres(name):
        if name not in R_w:
            R_w[name] = Res(name, multi=name in ("KN", "KR", "DK", "VM", "DV", "WUQ", "WUKV"))
        return R_w[name]

    def prep(dst, src, name):
        tr.dma("pool", dst, src, writes=[wres(name)])

    def kc_view(w, c0, n):
        return w[:, c0:c0 + n].rearrange("(kc p) n -> p kc n", p=128)

    def emit_prep(part):
        if part == 2:
            return emit_prep2()
        cols = [(0, 512), (512, 320), (832, 512), (1344, 512), (1856, 512), (2368, 512), (2880, 512), (3392, 512)]
        for i, (c0, n) in enumerate(cols):
            prep(WA[i], kc_view(w_in, c0, 512), f"WA{i}")
        for kc in range(4):
            src = w_uq[kc * 128:(kc + 1) * 128, :].rearrange("p (h d) -> p h d", d=192)
            prep(WUQ[:, kc, 0:1024].rearrange("p (h d) -> p h d", d=128), src[:, :, 0:128], "WUQ")
            prep(WUQ[:, kc, 1024:1536].rearrange("p (h d) -> p h d", d=64), src[:, :, 128:192], "WUQ")
        for kc in range(2):
            src = w_ukv[kc * 128:(kc + 1) * 128, :].rearrange("p (h d) -> p h d", d=256)
            prep(WUKV[:, kc, 0:1024].rearrange("p (h d) -> p h d", d=128), src[:, :, 0:128], "WUKV")
            prep(WUKV[:, kc, 1024:2048].rearrange("p (h d) -> p h d", d=128), src[:, :, 128:256], "WUKV")
    def emit_prep2():
        for i in range(8):
            prep(WG[i], kc_view(w_in, 3904 + i * 512, 512), f"WG{i}")
        for i in range(4):
            prep(WBA[i], kc_view(w_ba, i * 512, 512), f"WBA{i}")
            prep(WBB[i], kc_view(w_bb, i * 512, 512), f"WBB{i}")
        for i in range(4):
            prep(WO[i], kc_view(w_out, i * 512, 512), f"WO{i}")
        for i in range(11):
            prep(WFG[i], kc_view(w_fi, i * 512, 512), f"WFG{i}")
            prep(WFU[i], kc_view(w_fi, DFF + i * 512, 512), f"WFU{i}")
            prep(WFO[i], w_fo[i * 512:(i + 1) * 512, :].rearrange("(kc p) n -> p kc n", p=128), f"WFO{i}")

    ctxs = []

    uid = [0]

    def sb(name, shape, dt):
        uid[0] += 1
        cm = nc.sbuf_tensor(f"sb{uid[0]}_{name}", list(shape), dt)
        t = cm.__enter__()
        ctxs.append(cm)
        return t

    class Scope:
        def __enter__(self):
            self.mark = len(ctxs)
            tr.barrier()
            return self

        def __exit__(self, *a):
            while len(ctxs) > self.mark:
                ctxs.pop().__exit__(None, None, None)
            tr.barrier()
            return False

    ident = sb("ident", [128, 128], BF16); R_ident = Res("ident")
    zcol = sb("zcol", [128, 1], F32); obias = sb("obias", [128, 1], F32); R_cst = Res("cst")
    gmix_c = sb("gmix_c", [128, 16], F32); gffn_c = sb("gffn_c", [128, 16], F32); gq_c = sb("gq_c", [128, 4], F32)
    gkv_b = sb("gkv_b", [128, 256], F32); gsub_b = sb("gsub_b", [128, 128], F32); gfin_b = sb("gfin_b", [128, D], F32)
    lam4 = sb("lam4", [128, 4, 64], F32); lamt = sb("lamt", [128, 4], F32); nlam = sb("nlam", [128, 1], F32)
    ones_bf = sb("ones_bf", [128, 8], BF16)
    hT = [sb(f"hT{i}", [128, 16, T], BF16) for i in range(2)]
    R_hT = [Res(f"hT{i}") for i in range(2)]
    hring = Ring([0, 1])
    QnT = sb("QnT", [128, 8, T], BF16); QrT = sb("QrT", [64, 8, T], BF16); Q1p = sb("Q1p", [128, 8, T], BF16); Q2p = sb("Q2p", [128, 8, T], BF16)
    R_QnT, R_QrT, R_Q12T = Res("QnT"), Res("QrT"), Res("Q12T")
    NW = 3
    wslot = [sb(f"wslot{i}", [128, 8192], BF16) for i in range(NW)]
    R_wslot = [Res(f"wslot{i}") for i in range(NW)]
    wring = Ring(list(range(NW)))
    small = sb("small", [128, 64], F32)
    R_small = [Res(f"small{i}") for i in range(64)]
    smring = Ring(list(range(64)))

    psb = []
    for i in range(8):
        cm = nc.psum_tensor(f"ps{i}", [128, 512], F32)
        psb.append(cm.__enter__())
        ctxs.append(cm)
    R_ps = [Res(f"ps{i}", excl=True) for i in range(8)]
    psring = Ring(list(range(8)))

    def set_psring(banks):
        psring.items = list(banks)
        psring.i = 0

    flip = [0]

    def evac_eng():
        flip[0] ^= 1
        return "act" if flip[0] else "dve"

    def copy(eng, out, in_, reads, writes):
        if eng == "act":
            tr.op("act", lambda: S_.activation(out=out, in_=in_, func=AF.Copy), reads, writes)
        else:
            tr.op("dve", lambda: V.tensor_copy(out, in_), reads, writes)

    def wload(dram_ap, rname, nelem):
        i = wring.next()
        view = wslot[i][:, 0:nelem]
        src = dram_ap if len(dram_ap.shape) == 2 else dram_ap.rearrange("p a b -> p (a b)")
        tr.dma("sp", view, src, reads=[wres(rname)], writes=[R_wslot[i]])
        return view, R_wslot[i]

    def wstream(items, depth=2):
        q = []
        it = iter(items)
        for _ in range(depth):
            x = next(it, None)
            if x is not None:
                q.append(wload(*x))
        while q:
            cur = q.pop(0)
            x = next(it, None)
            if x is not None:
                q.append(wload(*x))
            yield cur

    def smcol():
        i = smring.next()
        return small[:, i:i + 1], R_small[i]

    def rstd_from_ss(col, rcol, n, tp):
        tr.op("act", lambda: S_.activation(out=col[:tp], in_=col[:tp], func=AF.Ln, scale=1.0 / n, bias=EPS),
              reads=[rcol], writes=[rcol])
        tr.op("act", lambda: S_.activation(out=col[:tp], in_=col[:tp], func=AF.Exp, scale=-0.5),
              reads=[rcol], writes=[rcol])

    def emit_consts():
        tr.dma("sp", ident[:], ident_d, writes=[R_ident])
        tr.dma("sp", obias[:], obias_d, writes=[R_cst])
        with nc.allow_non_contiguous_dma(reason="tiny one-off per-partition gain columns"):
            for j in range(4):
                tr.dma("sp", gmix_c[:, 4 * j:4 * j + 4], g_mix[512 * j:512 * j + 512].rearrange("(c p) -> p c", p=128), writes=[R_cst])
                tr.dma("sp", gffn_c[:, 4 * j:4 * j + 4], g_ffn[512 * j:512 * j + 512].rearrange("(c p) -> p c", p=128), writes=[R_cst])
            tr.dma("sp", gq_c[:], g_q.rearrange("(c p) -> p c", p=128), writes=[R_cst])
        tr.dma("sp", gkv_b[:], g_kv.partition_broadcast(128), writes=[R_cst])
        tr.dma("sp", gsub_b[:], g_sub.partition_broadcast(128), writes=[R_cst])
        tr.dma("sp", gfin_b[:], g_fin.partition_broadcast(128), writes=[R_cst])
        for j, v in enumerate((lq1, lk1, lq2, lk2)):
            tr.dma("sp", lam4[:, j, :], v.partition_broadcast(128), writes=[R_cst])
        tr.op("dve", lambda: V.memset(zcol[:], 0.0), writes=[R_cst])
        tr.op("pool", lambda: nc.gpsimd.memset(Q1p[64:128, :, :], 0.0), writes=[R_Q12T])
        tr.op("pool", lambda: nc.gpsimd.memset(Q2p[0:64, :, :], 0.0), writes=[R_Q12T])
        tr.op("dve", lambda: V.memset(ones_bf[:], 1.0), writes=[R_cst])
        tr.op("dve", lambda: V.tensor_tensor(lam4[:, 0, :], lam4[:, 0, :], lam4[:, 1, :], ALU.mult), [R_cst], [R_cst])
        tr.op("dve", lambda: V.tensor_tensor(lam4[:, 2, :], lam4[:, 2, :], lam4[:, 3, :], ALU.mult), [R_cst], [R_cst])
        tr.op("dve", lambda: V.reduce_sum(lamt[:, 0:1], lam4[:, 0, :], mybir.AxisListType.X), [R_cst], [R_cst])
        tr.op("dve", lambda: V.reduce_sum(lamt[:, 1:2], lam4[:, 2, :], mybir.AxisListType.X), [R_cst], [R_cst])
        tr.op("act", lambda: S_.activation(out=lamt[:, 2:4], in_=lamt[:, 0:2], func=AF.Exp), [R_cst], [R_cst])
        tr.op("dve", lambda: V.scalar_tensor_tensor(nlam[:], lamt[:, 3:4], -LAMBDA_INIT, lamt[:, 2:3],
                                                    ALU.add, ALU.subtract), [R_cst], [R_cst])

    def front(kind, nt, blk, x_src=None, rope_src=None, outs=None, cache_rows=None):
        subs = [(s, min(128, nt - s * 128)) for s in range((nt + 127) // 128)]
        ns = len(subs)
        own = kind == "own"
        A = lambda n: Res(n, arena=True)
        g3 = lambda ap, w: ap.rearrange("p (g w) -> p g w", w=w)
        with Scope():
            set_psring(range(8))
            if kind != "cache":
                hi = hring.next()
                h_T, R_h = hT[hi], R_hT[hi]
                with Scope():
                    xt = [sb(f"xt{i}", [128, D], F32) for i in range(2)]; R_xt = [A("xt0"), A("xt1")]
                    xb4 = sb("xb4", [128, ns, D], BF16); R_xb4 = [A(f"xb4_{s}") for s in range(ns)]
                    for s, tp in subs:
                        b = s % 2
                        tr.dma("sp", xt[b][:tp, :], x_src[s * 128:s * 128 + tp, :], writes=[R_xt[b]])
                        ss, R_ss = smcol()
                        tr.op("act", lambda: S_.activation(out=xb4[:tp, s, :], in_=xt[b][:tp, :], func=AF.Square,
                                                           accum_out=ss[:tp]),
                              reads=[R_xt[b]], writes=[R_xb4[s], R_ss])
                        rstd_from_ss(ss, R_ss, D, tp)
                        tr.op("act", lambda: S_.activation(out=xb4[:tp, s, :], in_=xt[b][:tp, :], func=AF.Copy,
                                                           scale=ss[:tp]),
                              reads=[R_xt[b], R_ss], writes=[R_xb4[s]])
                    for dc2 in range(8):
                        bi = psring.next()
                        pbf = psb[bi][:].bitcast(BF16)

                        def trs(dc2=dc2, pbf=pbf):
                            ins = None
                            for u in range(2):
                                dc = dc2 * 2 + u
                                for s, tp in subs:
                                    ins = P_.transpose(pbf[:, u * 512 + s * 128:u * 512 + s * 128 + tp],
                                                       xb4[:tp, s, dc * 128:(dc + 1) * 128], ident[:tp, :tp])
                            return ins
                        tr.op("pe", trs, reads=R_xb4 + [R_ident], writes=[R_ps[bi]])
                        for u in range(2):
                            dc = dc2 * 2 + u
                            if dc2 % 2 == 0:
                                tr.op("act", lambda: S_.activation(out=h_T[:, dc, 0:nt], in_=pbf[:, u * 512:u * 512 + nt],
                                                                   func=AF.Copy, scale=gmix_c[:, dc:dc + 1]),
                                      reads=[R_ps[bi], R_cst], writes=[R_h])
                            else:
                                tr.op("dve", lambda: V.tensor_scalar(h_T[:, dc, 0:nt], pbf[:, u * 512:u * 512 + nt],
                                                                     gmix_c[:, dc:dc + 1], None, ALU.mult),
                                      reads=[R_ps[bi], R_cst], writes=[R_h])

            if DBG["cut"] <= 1:
                return
            ckvb = sb("ckvb", [128, ns, 256], BF16); R_ckvb = A("ckvb")
            krb = sb("krb", [128, ns, 64], BF16); R_krb = A("krb")
            dkb = sb("dkb", [128, ns, 1024], BF16); R_dkb = A("dkb")
            vst = [sb(f"vst{i}", [128, 8, 129], BF16) for i in range(2)]; R_vst = [A("vst0"), A("vst1")]
            dst = [sb(f"dst{i}", [128, 4, 129], BF16) for i in range(2)]; R_dst = [A("dst0"), A("dst1")]
            kst = [sb(f"kst{i}", [128, T], BF16) for i in range(4)]; R_kst = [A(f"kst{i}") for i in range(4)]
            kring = Ring([0, 1, 2, 3])
            dring = Ring([0, 1])
            ckvT = sb("ckvT", [128, 2, T], BF16); R_ckvT = A("ckvT")
            for i in range(2):
                if DBG.get("nomemset", 0):
                    break
                tr.op("dve", (lambda i=i: V.memset(vst[i][:, :, 128:129], 1.0)), writes=[R_vst[i]])
                tr.op("dve", (lambda i=i: V.memset(dst[i][:, :, 128:129], 1.0)), writes=[R_dst[i]])
            if kind != "cache":
                rt = sb("rt", [128, ns, RW], F32); R_rt = A("rt")
                o_ckv = [sb(f"o_ckv{i}", [128, 256], F32) for i in range(2)]; R_ockv = [A("ockv0"), A("ockv1")]
                o_kr = [sb(f"o_kr{i}", [128, 64], F32) for i in range(2)]; R_okr = [A("okr0"), A("okr1")]
                o_blk = [sb(f"o_blk{i}", [128, 512], F32) for i in range(3)]; R_oblk = [A(f"oblk{i}") for i in range(3)]
                oring = Ring([0, 1, 2])
                tmp = sb("ropetmp", [128, 4, 256], F32); R_tmp = A("ropetmp")
                if own:
                    qlb = sb("qlb", [128, ns, 512], BF16); R_qlb = A("qlb")
                    dqb = sb("dqb", [128, ns, 1024], BF16); R_dqb = A("dqb")
                    qrb = sb("qrb", [128, ns, 512], BF16); R_qrb = A("qrb")
                    qlatT = sb("qlatT", [128, 4, T], BF16); R_qlatT = A("qlatT")
                for s, tp in subs:
                    if DBG.get("nort", 0):
                        break
                    tr.dma("sp", rt[:tp, s, :], rope_src[s * 128:s * 128 + tp, :], writes=[R_rt])

                def rope(src3, dst3, C3, S3, tp, G, hw, rd, wr):
                    n = G * hw
                    t = [tmp[:tp, j, 0:n].rearrange("p (g w) -> p g w", w=hw) for j in range(4)]
                    x1, x2 = src3[:, :, 0:hw], src3[:, :, hw:2 * hw]
                    rm = DBG.get("ropemask", 3)
                    if rm & 1:
                        tr.op("dve", lambda: V.tensor_tensor(t[0], x1, C3, ALU.mult), rd, [R_tmp])
                        tr.op("dve", lambda: V.tensor_tensor(t[1], x2, S3, ALU.mult), rd, [R_tmp])
                        tr.op("dve", lambda: V.tensor_tensor(t[2], x2, C3, ALU.mult), rd, [R_tmp])
                        tr.op("dve", lambda: V.tensor_tensor(t[3], x1, S3, ALU.mult), rd, [R_tmp])
                    if rm & 2:
                        tr.op("dve", lambda: V.tensor_tensor(dst3[:, :, 0:hw], t[0], t[1], ALU.subtract), [R_tmp], wr)
                        tr.op("dve", lambda: V.tensor_tensor(dst3[:, :, hw:2 * hw], t[2], t[3], ALU.add), [R_tmp], wr)

                blocks = [0, 1, 2, 3, 4, 5, 6, 7] if own else [1, 4, 5, 6, 7]
                blocks = blocks[:DBG.get("nblk", 99)]
                ncols = {0: 512, 1: 320}
                ws_ = wstream([(WA[nb_], f"WA{nb_}", 8192) for nb_ in blocks])
                for bix, nb in enumerate(blocks):
                    wv, R_wv = next(ws_)
                    w3 = wv.rearrange("p (kc n) -> p kc n", n=512)
                    ncl = ncols.get(nb, 512)
                    if DBG.get("ncl512", 0):
                        ncl = 512
                    for s, tp in subs:
                        b = s % 2
                        r0 = s * 128
                        bi = psring.next()
                        bank = psb[bi]

                        def mm(s=s, tp=tp, bank=bank, w3=w3, ncl=ncl):
                            ins = None
                            for kc in range(16):
                                ins = P_.matmul(bank[:tp, 0:ncl], h_T[:, kc, s * 128:s * 128 + tp], w3[:, kc, 0:ncl],
                                                start=(kc == 0), stop=(kc == 15))
                            return ins
                        if not DBG.get("nomm", 0):
                            tr.op("pe", mm, reads=[R_h, R_wv], writes=[R_ps[bi]])
                        Rb = R_ps[bi]
                        if DBG.get("noev", 0):
                            continue
                        if nb == 0:
                            ss, R_ss = smcol()
                            tr.op("act", lambda: S_.activation(out=qlb[:tp, s, :], in_=bank[:tp, 0:512], func=AF.Square,
                                                               accum_out=ss[:tp]), [Rb], [R_qlb, R_ss])
                            rstd_from_ss(ss, R_ss, 512, tp)
                            tr.op("act", lambda: S_.activation(out=qlb[:tp, s, :], in_=bank[:tp, 0:512], func=AF.Copy,
                                                               scale=ss[:tp]), [Rb, R_ss], [R_qlb])
                        elif nb == 1:
                            ss, R_ss = smcol()
                            tr.op("act", lambda: S_.activation(out=o_ckv[b][:tp, :], in_=bank[:tp, 0:256], func=AF.Square,
                                                               accum_out=ss[:tp]), [Rb], [R_ockv[b], R_ss])
                            rstd_from_ss(ss, R_ss, 256, tp)
                            tr.op("dve", lambda: V.scalar_tensor_tensor(o_ckv[b][:tp, :], bank[:tp, 0:256], ss[:tp],
                                                                        gkv_b[:tp, :], ALU.mult, ALU.mult),
                                  [Rb, R_ss, R_cst], [R_ockv[b]])
                            tr.op("act", lambda: S_.activation(out=ckvb[:tp, s, :], in_=o_ckv[b][:tp, :], func=AF.Copy),
                                  [R_ockv[b]], [R_ckvb])
                            rope(g3(bank[:tp, 256:320], 64), g3(o_kr[b][:tp, :], 64),
                                 g3(rt[:tp, s, 0:32], 32), g3(rt[:tp, s, 256:288], 32), tp, 1, 32,
                                 [Rb, R_rt], [R_okr[b]])
                            tr.op("act", lambda: S_.activation(out=krb[:tp, s, :], in_=o_kr[b][:tp, :], func=AF.Copy),
                                  [R_okr[b]], [R_krb])
                            if outs is not None:
                                tr.dma("pool", outs["ckv"][r0:r0 + tp, :], o_ckv[b][:tp, :], reads=[R_ockv[b]])
                                tr.dma("pool", outs["kr"][r0:r0 + tp, :], o_kr[b][:tp, :], reads=[R_okr[b]])
                        elif nb in (2, 3):
                            c0 = (nb - 2) * 512
                            tr.op("act", lambda: S_.activation(out=dqb[:tp, s, c0:c0 + 512], in_=bank[:tp, 0:512],
                                                               func=AF.Copy), [Rb], [R_dqb])
                            rope(g3(bank[:tp, 0:512], 64), g3(dqb[:tp, s, c0:c0 + 512], 64),
                                 g3(rt[:tp, s, 512:576], 8), g3(rt[:tp, s, 576:640], 8), tp, 8, 8,
                                 [Rb, R_rt], [R_dqb])
                        elif nb in (4, 5):
                            c0 = (nb - 4) * 512
                            oi = oring.next()
                            tr.op("act", lambda: S_.activation(out=o_blk[oi][:tp, :], in_=bank[:tp, 0:512],
                                                               func=AF.Copy), [Rb], [R_oblk[oi]])
                            if not DBG.get("norope", 0):
                                rope(g3(bank[:tp, 0:512], 64), g3(o_blk[oi][:tp, :], 64),
                                     g3(rt[:tp, s, 512:576], 8), g3(rt[:tp, s, 576:640], 8), tp, 8, 8,
                                     [Rb, R_rt] + ([R_oblk[oi]] if DBG.get("ropeser", 0) else []), [R_oblk[oi]])
                            tr.op("act", lambda: S_.activation(out=dkb[:tp, s, c0:c0 + 512], in_=o_blk[oi][:tp, :],
                                                               func=AF.Copy), [R_oblk[oi]], [R_dkb])
                            if outs is not None:
                                tr.dma("pool", outs["dk"][r0:r0 + tp, c0:c0 + 512], o_blk[oi][:tp, :], reads=[R_oblk[oi]])
                        else:
                            c0 = (nb - 6) * 512
                            h0 = (nb - 6) * 4
                            di = dring.next()
                            tr.op("dve", lambda: V.tensor_copy(dst[di][:tp, :, 0:128], g3(bank[:tp, 0:512], 128)),
                                  [Rb], [R_dst[di]])
                            kt = blk * 4 + s
                            tr.dma("pool", DV[h0:h0 + 4, 0:tp, kt, :].rearrange("h p c -> p h c"), dst[di][:tp, :, :],
                                   reads=[R_dst[di]], writes=[wres("DV")])
                            if outs is not None:
                                oi = oring.next()
                                tr.op("act", lambda: S_.activation(out=o_blk[oi][:tp, :], in_=bank[:tp, 0:512],
                                                                   func=AF.Copy), [Rb], [R_oblk[oi]])
                                tr.dma("pool", outs["dv"][r0:r0 + tp, c0:c0 + 512], o_blk[oi][:tp, :], reads=[R_oblk[oi]])
            else:
                cst = [sb(f"cst32_{i}", [128, 1024], F32) for i in range(2)]; R_c32 = [A("cst32_0"), A("cst32_1")]
                cring = Ring([0, 1])
                for s, tp in subs:
                    r0 = cache_rows + s * 128
                    kt = blk * 4 + s
                    ci = cring.next()
                    tr.dma("sp", cst[ci][:tp, 0:256], c_ckv[r0:r0 + tp, :], writes=[R_c32[ci]])
                    tr.dma("sp", cst[ci][:tp, 256:320], c_kr[r0:r0 + tp, :], writes=[R_c32[ci]])
                    copy("dve", ckvb[:tp, s, :], cst[ci][:tp, 0:256], [R_c32[ci]], [R_ckvb])
                    copy("dve", krb[:tp, s, :], cst[ci][:tp, 256:320], [R_c32[ci]], [R_krb])
                    ci = cring.next()
                    tr.dma("sp", cst[ci][:tp, :], c_dk[r0:r0 + tp, :], writes=[R_c32[ci]])
                    copy("act", dkb[:tp, s, :], cst[ci][:tp, :], [R_c32[ci]], [R_dkb])
                    ci = cring.next()
                    tr.dma("sp", cst[ci][:tp, :], c_dv[r0:r0 + tp, :], writes=[R_c32[ci]])
                    for half in range(2):
                        di = dring.next()
                        copy("dve", dst[di][:tp, :, 0:128], g3(cst[ci][:tp, half * 512:(half + 1) * 512], 128),
                             [R_c32[ci]], [R_dst[di]])
                        tr.dma("pool", DV[half * 4:half * 4 + 4, 0:tp, kt, :].rearrange("h p c -> p h c"), dst[di][:tp, :, :],
                               reads=[R_dst[di]], writes=[wres("DV")])

            if DBG["cut"] <= 2:
                return
            def tr_group(src_of_s, width, emit_evac, rd):
                bi = psring.next()
                pbf = psb[bi][:].bitcast(BF16)

                def f():
                    ins = None
                    for s, tp in subs:
                        ins = P_.transpose(pbf[0:width, s * 128:s * 128 + tp], src_of_s(s, tp), ident[:tp, :tp])
                    return ins
                tr.op("pe", f, reads=rd + [R_ident], writes=[R_ps[bi]])
                emit_evac(pbf[0:width, 0:nt], R_ps[bi])

            for c in range(2):
                def ev(src, Rb, c=c):
                    copy(evac_eng(), ckvT[:, c, 0:nt], src, [Rb], [R_ckvT])
                tr_group(lambda s, tp, c=c: ckvb[:tp, s, c * 128:(c + 1) * 128], 128, ev, [R_ckvb])

            def ev_kr(src, Rb):
                ki = kring.next()
                copy(evac_eng(), kst[ki][0:64, 0:nt], src, [Rb], [R_kst[ki]])
                tr.dma("pool", KR[:, blk * T:blk * T + nt], kst[ki][0:64, 0:nt], reads=[R_kst[ki]], writes=[wres("KR")])
            tr_group(lambda s, tp: krb[:tp, s, :], 64, ev_kr, [R_krb])
            for h in range(8):
                def ev_dk(src, Rb, h=h):
                    ki = kring.next()
                    copy(evac_eng(), kst[ki][:, 0:nt], src, [Rb], [R_kst[ki]])
                    tr.dma("pool", DK[h, :, blk * T:blk * T + nt], kst[ki][:, 0:nt], reads=[R_kst[ki]], writes=[wres("DK")])
                tr_group(lambda s, tp, h=h: dkb[:tp, s, h * 128:(h + 1) * 128], 128, ev_dk, [R_dkb])
            if own:
                for h in range(8):
                    def ev_dq(src, Rb, h=h):
                        e = evac_eng()
                        copy(e, Q1p[0:64, h, 0:nt], src[0:64, :], [Rb], [R_Q12T])
                        copy(e, Q2p[64:128, h, 0:nt], src[64:128, :], [Rb], [R_Q12T])
                    tr_group(lambda s, tp, h=h: dqb[:tp, s, h * 128:(h + 1) * 128], 128, ev_dq, [R_dqb])
                for c in range(4):
                    def ev_ql(src, Rb, c=c):
                        tr.op("act", lambda: S_.activation(out=qlatT[:, c, 0:nt], in_=src, func=AF.Copy,
                                                           scale=gq_c[:, c:c + 1]), [Rb, R_cst], [R_qlatT])
                    tr_group(lambda s, tp, c=c: qlb[:tp, s, c * 128:(c + 1) * 128], 128, ev_ql, [R_qlb])

            if DBG["cut"] <= 3:
                return
            wv, R_wv = wload(WUKV, "WUKV", 4096)
            wk3 = wv.rearrange("p (kc n) -> p kc n", n=2048)
            for h in range(8):
                bi = psring.next()
                bank = psb[bi]

                def mmk(h=h, bank=bank):
                    ins = None
                    for kc in range(2):
                        ins = P_.matmul(bank[:, 0:nt], wk3[:, kc, h * 128:(h + 1) * 128], ckvT[:, kc, 0:nt],
                                        start=(kc == 0), stop=(kc == 1))
                    return ins
                tr.op("pe", mmk, reads=[R_wv, R_ckvT], writes=[R_ps[bi]])
                ki = kring.next()
                copy(evac_eng(), kst[ki][:, 0:nt], bank[:, 0:nt], [R_ps[bi]], [R_kst[ki]])
                tr.dma("pool", KN[h, :, blk * T:blk * T + nt], kst[ki][:, 0:nt], reads=[R_kst[ki]], writes=[wres("KN")])
            for s, tp in subs:
                b = s % 2
                for half in range(2):
                    bi = psring.next()
                    bank = psb[bi]

                    def mmv(s=s, tp=tp, half=half, bank=bank):
                        ins = None
                        for kc in range(2):
                            ins = P_.matmul(bank[:tp, 0:512], ckvT[:, kc, s * 128:s * 128 + tp],
                                            wk3[:, kc, 1024 + half * 512:1024 + (half + 1) * 512],
                                            start=(kc == 0), stop=(kc == 1))
                        return ins
                    tr.op("pe", mmv, reads=[R_wv, R_ckvT], writes=[R_ps[bi]])
                    copy(evac_eng(), vst[b][:tp, half * 4:half * 4 + 4, 0:128], g3(bank[:tp, 0:512], 128),
                         [R_ps[bi]], [R_vst[b]])
                kt = blk * 4 + s
                tr.dma("pool", VM[:, 0:tp, kt, :].rearrange("h p c -> p h c"), vst[b][:tp, :, :],
                       reads=[R_vst[b]], writes=[wres("VM")])

            if own:
                wv, R_wv = wload(WUQ, "WUQ", 6144)
                wq3 = wv.rearrange("p (kc n) -> p kc n", n=1536)
                for h in range(8):
                    bi = psring.next()
                    bank = psb[bi]

                    def mmq(h=h, bank=bank):
                        ins = None
                        for kc in range(4):
                            ins = P_.matmul(bank[:, 0:nt], wq3[:, kc, h * 128:(h + 1) * 128], qlatT[:, kc, 0:nt],
                                            start=(kc == 0), stop=(kc == 3))
                        return ins
                    tr.op("pe", mmq, reads=[R_wv, R_qlatT], writes=[R_ps[bi]])
                    copy(evac_eng(), QnT[:, h, 0:nt], bank[:, 0:nt], [R_ps[bi]], [R_QnT])
                for s, tp in subs:
                    bi = psring.next()
                    bank = psb[bi]

                    def mmr(s=s, tp=tp, bank=bank):
                        ins = None
                        for kc in range(4):
                            ins = P_.matmul(bank[:tp, 0:512], qlatT[:, kc, s * 128:s * 128 + tp], wq3[:, kc, 1024:1536],
                                            start=(kc == 0), stop=(kc == 3))
                        return ins
                    tr.op("pe", mmr, reads=[R_wv, R_qlatT], writes=[R_ps[bi]])
                    rope(g3(bank[:tp, 0:512], 64), g3(qrb[:tp, s, :], 64),
                         g3(rt[:tp, s, 0:256], 32), g3(rt[:tp, s, 256:512], 32), tp, 8, 32,
                         [R_ps[bi], R_rt], [R_qrb])
                for h in range(8):
                    def ev_qr(src, Rb, h=h):
                        copy(evac_eng(), QrT[:, h, 0:nt], src, [Rb], [R_QrT])
                    tr_group(lambda s, tp, h=h: qrb[:tp, s, h * 64:(h + 1) * 64], 64, ev_qr, [R_qrb])

    def attention(nt, ktiles, outAT, R_oA, outBT, R_oB):
        subs = [(s, min(128, nt - s * 128)) for s in range((nt + 127) // 128)]
        A = lambda n: Res(n, arena=True)
        NB = 3
        kn_s = [sb(f"kn_s{i}", [128, 1024], BF16) for i in range(NB)]
        kr_s = [sb(f"kr_s{i}", [64, 1024], BF16) for i in range(NB)]
        v_s = [sb(f"v_s{i}", [128, 8, 129], BF16) for i in range(NB)]
        R_kn = [A(f"skn{i}") for i in range(NB)]; R_kr = [A(f"skr{i}") for i in range(NB)]; R_v = [A(f"sv{i}") for i in range(NB)]
        sring = Ring(list(range(NB)))
        NPB = 4
        pt = [sb(f"pt{i}", [128, 2, T], BF16) for i in range(NPB)]
        R_pt = [A(f"pt{i}") for i in range(NPB)]
        pring = Ring(list(range(NPB)))
        otok = [sb(f"otok{i}", [128, 4, 128], BF16) for i in range(2)]; R_otok = [A("otok0"), A("otok1")]
        oring = Ring([0, 1])
        o1 = sb("o1", [128, 128], F32); R_o1 = A("o1")
        o2 = sb("o2", [128, 4, 128], F32); R_o2 = A("o2")
        rz = sb("rz", [128, 32], F32); R_rz = A("rz")
        rz2 = sb("rz2", [128, 8], F32); R_rz2 = A("rz2")
        accM = [(2, 0), (2, 129), (2, 258), (3, 0)]
        acc1 = [(4, 0), (4, 129), (4, 258), (5, 0)]
        acc2 = [(6, 0), (6, 129), (6, 258), (5, 129)]
        sbank = Ring([0, 1, 7])
        groups = []
        for kt in ktiles:
            if groups and len(groups[-1]) < 8 and groups[-1][-1][0] + 1 == kt[0] and kt[0] % 8 != 0:
                groups[-1].append(kt)
            else:
                groups.append([kt])

        def acc_banks(accs):
            return sorted(set(b for b, _ in accs[:len(subs)]))

        def zero_acc(accs):
            for b in acc_banks(accs):
                cols = [c for bb, c in accs[:len(subs)] if bb == b]
                c0, c1 = min(cols), max(cols) + 129
                tr.op("dve", lambda: V.memset(psb[b][:, c0:c1], 0.0), writes=[R_ps[b]])

        def zero_branch(branch):
            if branch == "M":
                zero_acc(accM)
            else:
                zero_acc(acc1)
                zero_acc(acc2)

        zero_branch("M")
        zero_branch("D")
        deferred = []
        pend = None
        for h in range(DBG.get("att_heads", 8)):
            for branch in ("M", "D")[:DBG.get("att_br", 2)]:
                accl = [accM] if branch == "M" else [acc1, acc2]
                R_accw = [R_ps[b] for accs in accl for b in acc_banks(accs)]
                nstep = 0
                for grp in groups:
                    kt0 = grp[0][0]
                    nk = len(grp)
                    si = sring.next()
                    key0 = kt0 * 128
                    nkeys = sum(g[1] for g in grp)
                    pl = max(g[1] for g in grp)
                    if branch == "M":
                        tr.dma("sp", kn_s[si][:, 0:nkeys], KN[h, :, key0:key0 + nkeys], reads=[wres("KN")], writes=[R_kn[si]])
                        tr.dma("sp", kr_s[si][:, 0:nkeys], KR[:, key0:key0 + nkeys], reads=[wres("KR")], writes=[R_kr[si]])
                        tr.dma("sp", v_s[si][:pl, 0:nk, :], VM[h, 0:pl, kt0:kt0 + nk, :], reads=[wres("VM")], writes=[R_v[si]])
                    else:
                        tr.dma("sp", kn_s[si][:, 0:nkeys], DK[h, :, key0:key0 + nkeys], reads=[wres("DK")], writes=[R_kn[si]])
                        tr.dma("sp", v_s[si][:pl, 0:nk, :], DV[h, 0:pl, kt0:kt0 + nk, :], reads=[wres("DV")], writes=[R_v[si]])
                    for gi, (kt, kp, mode, j) in enumerate(grp):
                        q0 = 128 * j if mode == "diag" else 0
                        ko = gi * 128
                        pi = pring.next()
                        bias = obias if mode == "bias" else zcol
                        if branch == "M":
                            b0 = sbank.next()

                            def mms(b0=b0, ko=ko, kp=kp, q0=q0, si=si):
                                P_.matmul(psb[b0][:kp, q0:nt], kn_s[si][:, ko:ko + kp], QnT[:, h, q0:nt], start=True, stop=False)
                                return P_.matmul(psb[b0][:kp, q0:nt], kr_s[si][:, ko:ko + kp], QrT[:, h, q0:nt],
                                                 start=False, stop=True)
                            tr.op("pe", mms, reads=[R_kn[si], R_kr[si], R_QnT, R_QrT], writes=[R_ps[b0]])
                            tr.op("act", lambda: S_.activation(out=pt[pi][:kp, 0, q0:nt], in_=psb[b0][:kp, q0:nt], func=AF.Exp,
                                                               scale=SC_MLA, bias=bias[:kp]),
                                  reads=[R_ps[b0], R_cst], writes=[R_pt[pi]])
                            nmap = 1
                        else:
                            for qh in range(2):
                                qa, qb = max(q0, qh * 256), min(nt, (qh + 1) * 256)
                                if qa >= qb:
                                    continue
                                w = qb - qa
                                b0 = sbank.next()

                                def mms(b0=b0, ko=ko, kp=kp, qa=qa, qb=qb, w=w, si=si):
                                    P_.matmul(psb[b0][:kp, 0:w], kn_s[si][:, ko:ko + kp], Q1p[:, h, qa:qb], start=True, stop=True)
                                    return P_.matmul(psb[b0][:kp, 256:256 + w], kn_s[si][:, ko:ko + kp], Q2p[:, h, qa:qb],
                                                     start=True, stop=True, skip_group_check=True)
                                tr.op("pe", mms, reads=[R_kn[si], R_Q12T], writes=[R_ps[b0]])
                                src = psb[b0][:kp, :].rearrange("p (m q) -> p m q", q=256)[:, :, 0:w]
                                tr.op("act", lambda: S_.activation(out=pt[pi][:kp, :, qa:qb], in_=src, func=AF.Exp,
                                                                   scale=SC_DIFF, bias=bias[:kp]),
                                      reads=[R_ps[b0], R_cst], writes=[R_pt[pi]])
                            nmap = 2
                        if mode == "diag":
                            tr.op("dve", lambda: V.memset(pt[pi][64:128, 0:nmap, q0:q0 + 64], 0.0), writes=[R_pt[pi]])

                        def pv(pi=pi, kp=kp, j=j, mode=mode, gi=gi, accl=accl, si=si, R_accw=R_accw):
                            def f():
                                ins = None
                                for m, accs in enumerate(accl):
                                    for s, tp in subs:
                                        if mode == "diag" and s < j:
                                            continue
                                        bk, c0 = accs[s]
                                        ins = P_.matmul(psb[bk][:tp, c0:c0 + 129], pt[pi][:kp, m, s * 128:s * 128 + tp],
                                                        v_s[si][:kp, gi, :], start=False, stop=False, skip_group_check=True)
                                return ins
                            tr.op("pe", f, reads=[R_pt[pi], R_v[si]], writes=R_accw)
                        if pend is not None:
                            pend()
                        pend = pv
                        nstep += 1
                        while deferred and deferred[0][0] <= nstep:
                            deferred.pop(0)[1]()
                while deferred:
                    deferred.pop(0)[1]()
                oi = oring.next()

                def ev1(h=h, branch=branch, oi=oi):
                    for s, tp in subs:
                        r = rz[:, s * 8:s * 8 + 8]
                        if branch == "M":
                            bk, c0 = accM[s]
                            tr.op("dve", lambda: V.reciprocal(r[:tp, 0:1], psb[bk][:tp, c0 + 128:c0 + 129]), [R_ps[bk]], [R_rz])
                            tr.op("dve", lambda: V.tensor_scalar(otok[oi][:tp, s, :], psb[bk][:tp, c0:c0 + 128], r[:tp, 0:1], None, ALU.mult),
                                  [R_ps[bk], R_rz], [R_otok[oi]])
                        else:
                            bk1, c1 = acc1[s]
                            bk2, c2 = acc2[s]
                            tr.op("dve", lambda: V.reciprocal(r[:tp, 1:2], psb[bk1][:tp, c1 + 128:c1 + 129]), [R_ps[bk1]], [R_rz])
                            tr.op("dve", lambda: V.reciprocal(r[:tp, 2:3], psb[bk2][:tp, c2 + 128:c2 + 129]), [R_ps[bk2]], [R_rz])
                            tr.op("dve", lambda: V.tensor_tensor(r[:tp, 3:4], r[:tp, 2:3], nlam[:tp, :], ALU.mult), [R_rz, R_cst], [R_rz])
                            tr.op("dve", lambda: V.tensor_scalar(o1[:tp, :], psb[bk1][:tp, c1:c1 + 128], r[:tp, 1:2], None, ALU.mult),
                                  [R_ps[bk1], R_rz], [R_o1])
                            tr.op("dve", lambda: V.scalar_tensor_tensor(o2[:tp, s, :], psb[bk2][:tp, c2:c2 + 128], r[:tp, 3:4],
                                                                        o1[:tp, :], ALU.mult, ALU.add),
                                  [R_ps[bk2], R_rz, R_o1], [R_o2])
                    if branch == "M" and h < 7:
                        zero_branch(branch)

                def ev2(h=h, branch=branch, oi=oi):
                    if branch == "M":
                        return
                    for s, tp in subs:
                        tr.op("act", lambda: S_.activation(out=o1[:tp, :], in_=o2[:tp, s, :], func=AF.Square,
                                                           accum_out=rz2[:tp, s:s + 1]), [R_o2], [R_o1, R_rz2])
                    tp0 = subs[0][1]
                    ns_ = len(subs)
                    tr.op("act", lambda: S_.activation(out=rz2[:tp0, 0:ns_], in_=rz2[:tp0, 0:ns_], func=AF.Ln, scale=1.0 / 128, bias=EPS),
                          [R_rz2], [R_rz2])
                    tr.op("act", lambda: S_.activation(out=rz2[:tp0, 0:ns_], in_=rz2[:tp0, 0:ns_], func=AF.Exp, scale=-0.5),
                          [R_rz2], [R_rz2])
                    tr.op("dve", lambda: V.tensor_scalar(rz2[:tp0, 4:4 + ns_], rz2[:tp0, 0:ns_], 1.0 - LAMBDA_INIT, None, ALU.mult),
                          [R_rz2], [R_rz2])
                    for s, tp in subs:
                        tr.op("dve", lambda: V.scalar_tensor_tensor(otok[oi][:tp, s, :], o2[:tp, s, :], rz2[:tp, 4 + s:5 + s], gsub_b[:tp, :],
                                                                    ALU.mult, ALU.mult), [R_o2, R_rz2, R_cst], [R_otok[oi]])
                    if h < 7:
                        zero_branch(branch)

                def ev3(h=h, branch=branch, oi=oi):
                    b0 = sbank.next()
                    pbf = psb[b0][:].bitcast(BF16)

                    def tro():
                        ins = None
                        for s, tp in subs:
                            ins = P_.transpose(pbf[:, s * 128:s * 128 + tp], otok[oi][:tp, s, :], ident[:tp, :tp])
                        return ins
                    tr.op("pe", tro, reads=[R_otok[oi], R_ident], writes=[R_ps[b0]])
                    if branch == "M":
                        copy("dve", outAT[:, h, 0:nt], pbf[:, 0:nt], [R_ps[b0]], [R_oA])
                    else:
                        copy("dve", outBT[:, h, 0:nt], pbf[:, 0:nt], [R_ps[b0]], [R_oB])
                if not DBG.get("att_noevac", 0):
                    deferred.extend([[2, ev1], [4, ev2], [6, ev3]])
        if pend is not None:
            pend()
        while deferred:
            deferred.pop(0)[1]()

    def back(nt, x_src, y_dst, outAT, R_oA, outBT, R_oB):
        subs = [(s, min(128, nt - s * 128)) for s in range((nt + 127) // 128)]
        ns = len(subs)
        A = lambda n: Res(n, arena=True)
        set_psring(range(8))
        x2 = sb("x2", [128, ns, D], F32); R_x2 = [A(f"x2_{s}") for s in range(ns)]

        def norm_T(gcol, h_T, R_h):
            with Scope():
                norm_T_(gcol, h_T, R_h)

        def norm_T_(gcol, h_T, R_h):
            xb4 = sb("xb4b", [128, ns, D], BF16); R_xb4 = [A(f"xb4b_{s}") for s in range(ns)]
            for s, tp in subs:
                ss, R_ss = smcol()
                tr.op("act", lambda: S_.activation(out=xb4[:tp, s, :], in_=x2[:tp, s, :], func=AF.Square, accum_out=ss[:tp]),
                      reads=[R_x2[s]], writes=[R_xb4[s], R_ss])
                rstd_from_ss(ss, R_ss, D, tp)
                tr.op("act", lambda: S_.activation(out=xb4[:tp, s, :], in_=x2[:tp, s, :], func=AF.Copy, scale=ss[:tp]),
                      reads=[R_x2[s], R_ss], writes=[R_xb4[s]])
            for dc2 in range(8):
                bi = psring.next()
                pbf = psb[bi][:].bitcast(BF16)

                def trs(dc2=dc2, pbf=pbf):
                    ins = None
                    for u in range(2):
                        dc = dc2 * 2 + u
                        for s, tp in subs:
                            ins = P_.transpose(pbf[:, u * 512 + s * 128:u * 512 + s * 128 + tp],
                                               xb4[:tp, s, dc * 128:(dc + 1) * 128], ident[:tp, :tp])
                    return ins
                tr.op("pe", trs, reads=R_xb4 + [R_ident], writes=[R_ps[bi]])
                for u in range(2):
                    dc = dc2 * 2 + u
                    if dc2 % 2 == 0:
                        tr.op("act", lambda: S_.activation(out=h_T[:, dc, 0:nt], in_=pbf[:, u * 512:u * 512 + nt], func=AF.Copy,
                                                           scale=gcol[:, dc:dc + 1]), [R_ps[bi], R_cst], [R_h])
                    else:
                        tr.op("dve", lambda: V.tensor_scalar(h_T[:, dc, 0:nt], pbf[:, u * 512:u * 512 + nt],
                                                             gcol[:, dc:dc + 1], None, ALU.mult), [R_ps[bi], R_cst], [R_h])

        for s, tp in subs:
            tr.dma("sp", x2[:tp, s, :], x_src[s * 128:s * 128 + tp, :], writes=[R_x2[s]])
        hi = hring.next()
        h_T, R_h = hT[hi], R_hT[hi]
        norm_T(gmix_c, h_T, R_h)

        with Scope():
            merged = sb("merged", [128, 16, T], BF16); R_mg = A("merged")
            sg = [sb(f"sg{i}", [128, T], F32) for i in range(4)]; R_sg = [A(f"sg{i}") for i in range(4)]
            order = []
            for g in range(4):
                order += [(WG[g], f"WG{g}", 8192, "ga", g), (WG[4 + g], f"WG{4 + g}", 8192, "gb", g),
                          (WBA[g], f"WBA{g}", 4096, "ya", g), (WBB[g], f"WBB{g}", 4096, "yb", g)]
            ws_ = wstream([o[:3] for o in order])
            pend = {}
            for oi, (wd, wn, ne, typ, g) in enumerate(order):
                wv, R_wv = next(ws_)
                nk = 16 if typ in ("ga", "gb") else 8
                w3 = wv.rearrange("p (kc n) -> p kc n", n=512)
                src, R_src = (h_T, R_h) if typ in ("ga", "gb") else ((outAT, R_oA) if typ == "ya" else (outBT, R_oB))
                for cc in range(4):
                    c = g * 4 + cc
                    bi = psring.next()
                    bank = psb[bi]

                    def mm(bank=bank, w3=w3, cc=cc, nk=nk, src=src):
                        ins = None
                        for kc in range(nk):
                            ins = P_.matmul(bank[:, 0:nt], w3[:, kc, cc * 128:(cc + 1) * 128], src[:, kc, 0:nt],
                                            start=(kc == 0), stop=(kc == nk - 1))
                        return ins
                    tr.op("pe", mm, reads=[R_wv, R_src], writes=[R_ps[bi]])
                    if typ == "ga":
                        tr.op("act", lambda: S_.activation(out=sg[cc][:, 0:nt], in_=bank[:, 0:nt], func=AF.Sigmoid),
                              [R_ps[bi]], [R_sg[cc]])
                        pend[("ga", cc)] = None
                    elif typ == "gb":
                        tr.op("act", lambda: S_.activation(out=merged[:, c, 0:nt], in_=bank[:, 0:nt], func=AF.Sigmoid),
                              [R_ps[bi]], [R_mg])
                    elif typ == "ya":
                        tr.op("dve", lambda: V.tensor_tensor(sg[cc][:, 0:nt], sg[cc][:, 0:nt], bank[:, 0:nt], ALU.mult),
                              [R_ps[bi], R_sg[cc]], [R_sg[cc]])
                    else:
                        tr.op("dve", lambda: V.tensor_tensor(merged[:, c, 0:nt], merged[:, c, 0:nt], bank[:, 0:nt], ALU.mult),
                              [R_ps[bi], R_mg], [R_mg])
                        tr.op("dve", lambda: V.tensor_tensor(merged[:, c, 0:nt], merged[:, c, 0:nt], sg[cc][:, 0:nt], ALU.add),
                              [R_mg, R_sg[cc]], [R_mg])
            ws_ = wstream([(WO[nb_], f"WO{nb_}", 8192) for nb_ in range(4)])
            for nb in range(4):
                wv, R_wv = next(ws_)
                w3 = wv.rearrange("p (kc n) -> p kc n", n=512)
                for s, tp in subs:
                    bi = psring.next()
                    bank = psb[bi]

                    def mm(bank=bank, w3=w3, s=s, tp=tp):
                        ins = None
                        for kc in range(16):
                            ins = P_.matmul(bank[:tp, 0:512], merged[:, kc, s * 128:s * 128 + tp], w3[:, kc, :],
                                            start=(kc == 0), stop=(kc == 15))
                        return ins
                    tr.op("pe", mm, reads=[R_wv, R_mg], writes=[R_ps[bi]])
                    tr.op("dve", lambda: V.tensor_tensor(x2[:tp, s, nb * 512:(nb + 1) * 512], x2[:tp, s, nb * 512:(nb + 1) * 512],
                                                         bank[:tp, 0:512], ALU.add), [R_ps[bi], R_x2[s]], [R_x2[s]])
        hi = hring.next()
        h2T, R_h2 = hT[hi], R_hT[hi]
        norm_T(gffn_c, h2T, R_h2)
        with Scope():
            actT = [sb(f"actT{i}", [128, 4, T], BF16) for i in range(2)]; R_act = [A("actT0"), A("actT1")]
            sl = [sb(f"silu{i}", [128, T], F32) for i in range(2)]; R_sl = [A("silu0"), A("silu1")]
            order = []
            for g in range(11):
                order += [(WFG[g], f"WFG{g}", 8192, "g", g), (WFU[g], f"WFU{g}", 8192, "u", g), (WFO[g], f"WFO{g}", 8192, "o", g)]
            ws_ = wstream([o[:3] for o in order])
            for oi, (wd, wn, ne, typ, g) in enumerate(order):
                wv, R_wv = next(ws_)
                ab = g % 2
                if typ in ("g", "u"):
                    w3 = wv.rearrange("p (kc n) -> p kc n", n=512)
                    for cc in range(4):
                        bi = psring.next()
                        bank = psb[bi]

                        def mm(bank=bank, w3=w3, cc=cc):
                            ins = None
                            for kc in range(16):
                                ins = P_.matmul(bank[:, 0:nt], w3[:, kc, cc * 128:(cc + 1) * 128], h2T[:, kc, 0:nt],
                                                start=(kc == 0), stop=(kc == 15))
                            return ins
                        tr.op("pe", mm, reads=[R_wv, R_h2], writes=[R_ps[bi]])
                        if typ == "g":
                            tr.op("act", lambda: S_.activation(out=actT[ab][:, cc, 0:nt], in_=bank[:, 0:nt], func=AF.Silu),
                                  [R_ps[bi]], [R_act[ab]])
                        else:
                            tr.op("dve", lambda: V.tensor_tensor(actT[ab][:, cc, 0:nt], actT[ab][:, cc, 0:nt], bank[:, 0:nt],
                                                                 ALU.mult), [R_ps[bi], R_act[ab]], [R_act[ab]])
                else:
                    w3 = wv.rearrange("p (kc n) -> p kc n", n=2048)
                    for s, tp in subs:
                        for nb in range(4):
                            bi = psring.next()
                            bank = psb[bi]

                            def mm(bank=bank, w3=w3, s=s, tp=tp, nb=nb):
                                ins = None
                                for kc in range(4):
                                    ins = P_.matmul(bank[:tp, 0:512], actT[ab][:, kc, s * 128:s * 128 + tp],
                                                    w3[:, kc, nb * 512:(nb + 1) * 512], start=(kc == 0), stop=(kc == 3))
                                return ins
                            tr.op("pe", mm, reads=[R_wv, R_act[ab]], writes=[R_ps[bi]])
                            tr.op("dve", lambda: V.tensor_tensor(x2[:tp, s, nb * 512:(nb + 1) * 512],
                                                                 x2[:tp, s, nb * 512:(nb + 1) * 512], bank[:tp, 0:512], ALU.add),
                                  [R_ps[bi], R_x2[s]], [R_x2[s]])
        junk = [sb(f"fjunk{i}", [128, D], BF16) for i in range(2)]; R_junk = [A("fjunk0"), A("fjunk1")]
        for s, tp in subs:
            ss, R_ss = smcol()
            tr.op("act", lambda: S_.activation(out=junk[s % 2][:tp, :], in_=x2[:tp, s, :], func=AF.Square, accum_out=ss[:tp]),
                  reads=[R_x2[s]], writes=[R_junk[s % 2], R_ss])
            rstd_from_ss(ss, R_ss, D, tp)
            tr.op("dve", lambda: V.scalar_tensor_tensor(x2[:tp, s, :], x2[:tp, s, :], ss[:tp], gfin_b[:tp, :], ALU.mult, ALU.mult),
                  [R_x2[s], R_ss, R_cst], [R_x2[s]])
            tr.dma("pool", y_dst[s * 128:s * 128 + tp, :], x2[:tp, s, :], reads=[R_x2[s]])

    def own_tile(nt, ktiles, x_src, y_dst):
        with Scope():
            A = lambda n: Res(n, arena=True)
            outAT = sb("outAT", [128, 8, T], BF16); R_oA = A("outAT")
            outBT = sb("outBT", [128, 8, T], BF16); R_oB = A("outBT")
            with Scope():
                set_psring([0, 1, 2, 3])
                with (nc.named_scope("att") if DBG.get("scopes", 0) else contextlib.nullcontext()):
                    attention(nt, ktiles, outAT, R_oA, outBT, R_oB)
            if stage >= 4:
                with Scope():
                    with (nc.named_scope("back") if DBG.get("scopes", 0) else contextlib.nullcontext()):
                        back(nt, x_src, y_dst, outAT, R_oA, outBT, R_oB)

    emit_consts()
    emit_prep(1)
    import contextlib
    scope = (lambda n: nc.named_scope(n)) if DBG.get("scopes", 0) else (lambda n: contextlib.nullcontext())
    for i in range(npair):
        with scope(f"fo{i}"):
            front("other", T, 2 * i + 1, x_src=xk[i], rope_src=rope_k[i])
        if stage < 2:
            continue
        with scope(f"fw{i}"):
            front("own", T, 2 * i, x_src=xo[i], rope_src=rope_o[i],
                  outs={"ckv": ckv_o[i], "kr": kr_o[i], "dk": dk_o[i], "dv": dv_o[i]})
        if i == 0:
            emit_prep(2)
        if stage < 3:
            continue
        kts = [(kt, 128, "full", 0) for kt in range(8 * i)]
        kts += [(8 * i + j, 128, "diag", j) for j in range(4)]
        kts += [(8 * i + 4 + j, 128, "bias", 0) for j in range(4)]
        own_tile(T, kts, xo[i], yo[i])
    if do_sample:
        if npair == 0:
            emit_prep(2)
        front("cache", T, 16, cache_rows=0)
        front("cache", T, 17, cache_rows=512)
        front("own", 16, 18, x_src=xs, rope_src=rope_s,
              outs={"ckv": ckv_s, "kr": kr_s, "dk": dk_s, "dv": dv_s})
        kts = [(64 + j, 128, "full", 0) for j in range(8)] + [(72, 16, "full", 0)]
        own_tile(16, kts, xs, ys)
    tr.finish()
    while ctxs:
        ctxs.pop().__exit__(None, None, None)
    tr.close()
    return nc, tr


def _rope_table(pos):
    pos = pos.astype(np.float32)
    invM = (1.0 / (np.float32(10000.0) ** (np.arange(32, dtype=np.float32) / np.float32(32)))).astype(np.float32)
    invD = (1.0 / (np.float32(500000.0) ** (np.arange(8, dtype=np.float32) / np.float32(8)))).astype(np.float32)
    aM = (pos[:, None] * invM[None, :]).astype(np.float32)
    aD = (pos[:, None] * invD[None, :]).astype(np.float32)
    cM, sM = np.cos(aM).astype(np.float32), np.sin(aM).astype(np.float32)
    cD, sD = np.cos(aD).astype(np.float32), np.sin(aD).astype(np.float32)
    return np.concatenate([np.tile(cM, (1, 8)), np.tile(sM, (1, 8)), np.tile(cD, (1, 8)), np.tile(sD, (1, 8))],
                          axis=1).astype(np.float32)


_PROG = {}


def kernel(x_prompt, x_sample, cache_mla_ckv, cache_mla_krope, cache_diff_k, cache_diff_v,
           norm_mix, w_in, mla_q_norm, mla_w_uq, mla_kv_norm, mla_w_ukv,
           diff_lq1, diff_lk1, diff_lq2, diff_lk2, diff_subln,
           w_branch_a, w_branch_b, w_out, norm_ffn, w_ffn_in, w_ffn_out, norm_final,
           _npair=8, _do_sample=True, _stage=9):
    f = lambda a: np.ascontiguousarray(np.asarray(a, dtype=np.float32))
    key = (_npair, _do_sample, _stage)
    if key not in _PROG:
        _PROG[key] = build_program(_npair, _do_sample, _stage)[0]
    nc = _PROG[key]
    x_prompt = f(x_prompt); x_sample = f(x_sample)
    shared = {
        "w_in": f(w_in)[0], "w_uq": f(mla_w_uq)[0], "w_ukv": f(mla_w_ukv)[0], "w_ba": f(w_branch_a)[0],
        "w_bb": f(w_branch_b)[0], "w_out": f(w_out)[0], "w_fi": f(w_ffn_in)[0], "w_fo": f(w_ffn_out)[0],
        "g_mix": f(norm_mix)[0], "g_q": f(mla_q_norm)[0], "g_kv": f(mla_kv_norm)[0],
        "lq1": f(diff_lq1)[0], "lk1": f(diff_lk1)[0], "lq2": f(diff_lq2)[0], "lk2": f(diff_lk2)[0],
        "g_sub": f(diff_subln)[0], "g_ffn": f(norm_ffn)[0], "g_fin": f(norm_final),
        "ident": np.eye(128, dtype=np.float32).astype(ml_dtypes.bfloat16),
    }
    rope_all = _rope_table(np.arange(8192))
    rope_s = _rope_table(1024 + np.arange(16))
    in_maps = []
    for c in range(8):
        b, e = c // 2, c % 2
        xt = x_prompt[b].reshape(16, T, D)
        rt = rope_all.reshape(16, T, RW)
        m = dict(shared)
        m["xo"] = np.ascontiguousarray(xt[e::2]); m["xk"] = np.ascontiguousarray(xt[1 - e::2])
        m["rope_o"] = np.ascontiguousarray(rt[e::2]); m["rope_k"] = np.ascontiguousarray(rt[1 - e::2])
        m["xs"] = x_sample[c]; m["rope_s"] = rope_s
        m["c_ckv"] = f(cache_mla_ckv)[0, c]; m["c_kr"] = f(cache_mla_krope)[0, c]
        m["c_dk"] = f(cache_diff_k)[0, c].reshape(1024, 1024); m["c_dv"] = f(cache_diff_v)[0, c].reshape(1024, 1024)
        m["obias"] = np.full((128, 1), 0.0 if e == 1 else NEG, np.float32)
        in_maps.append(m)
    res = run_bass_kernel_spmd(nc, in_maps, core_ids=list(range(8)))
    R = res.results
    y_p = np.zeros((4, 8192, D), np.float32); ckv_p = np.zeros((1, 4, 8192, 256), np.float32)
    kr_p = np.zeros((1, 4, 8192, 64), np.float32); dk_p = np.zeros((1, 4, 8192, 8, 128), np.float32)
    dv_p = np.zeros((1, 4, 8192, 8, 128), np.float32)
    y_s = np.zeros((8, 16, D), np.float32); ckv_sm = np.zeros((1, 8, 16, 256), np.float32)
    kr_sm = np.zeros((1, 8, 16, 64), np.float32); dk_sm = np.zeros((1, 8, 16, 8, 128), np.float32)
    dv_sm = np.zeros((1, 8, 16, 8, 128), np.float32)
    for c in range(8):
        b, e = c // 2, c % 2
        r = R[c]
        y_p[b].reshape(16, T, D)[e::2] = r["yo"]
        ckv_p[0, b].reshape(16, T, 256)[e::2] = r["ckv_o"]
        kr_p[0, b].reshape(16, T, 64)[e::2] = r["kr_o"]
        dk_p[0, b].reshape(16, T, 1024)[e::2] = r["dk_o"]
        dv_p[0, b].reshape(16, T, 1024)[e::2] = r["dv_o"]
        y_s[c] = r["ys"]; ckv_sm[0, c] = r["ckv_s"]; kr_sm[0, c] = r["kr_s"]
        dk_sm[0, c] = r["dk_s"].reshape(16, 8, 128); dv_sm[0, c] = r["dv_s"].reshape(16, 8, 128)
    return (y_p, y_s, ckv_p, kr_p, dk_p, dv_p, ckv_sm, kr_sm, dk_sm, dv_sm)
```

```python
import contextlib
import numpy as np
import ml_dtypes
import concourse.bass as bass
import concourse.mybir as mybir
from concourse.bass_utils import run_bass_kernel_spmd

F32 = mybir.dt.float32
BF16 = mybir.dt.bfloat16
AF = mybir.ActivationFunctionType
DBG = {"cut": 9}
ALU = mybir.AluOpType

D = 2048
T = 512
DFF = 5632
EPS = 1e-6
NEG = -30000.0
LAMBDA_INIT = 0.8 - 0.6 * 1.0
SC_MLA = float((128 + 64) ** -0.5)
SC_DIFF = float(64 ** -0.5)
NBLK = 19
NKT = NBLK * 4
RW = 640


class Res:
    __slots__ = ("name", "w", "r", "arena", "multi", "ws", "excl")

    def __init__(self, name, arena=False, multi=False, excl=False):
        self.name = name
        self.w = None
        self.r = {}
        self.arena = arena
        self.multi = multi
        self.ws = {}
        self.excl = excl


class Trk:
    SEM_LIMIT = 30000

    def __init__(self, nc, n_dma_sems=14):
        self.nc = nc
        self.engs = {"pe": nc.tensor, "act": nc.scalar, "dve": nc.vector, "pool": nc.gpsimd, "sp": nc.sync}
        self.sems = {}
        self.owner = {}
        self.cur = {}
        self.gen = {}
        self.waited = {e: {} for e in self.engs}
        self.pending = {e: {} for e in self.engs}
        self.arena_evs = {}
        self.nwaits = 0
        self.nops = 0
        self._stack = []
        for e in ("pe", "act", "dve", "pool"):
            self.gen[e] = 0
            self._new_sem(e)
        self.dq = {}
        for q in ("sp", "pool"):
            lst = []
            for i in range(n_dma_sems):
                k = f"d_{q}{i}"
                self.sems[k] = self._alloc(k)
                self.owner[k] = "dma"
                lst.append([k, 0])
            self.dq[q] = [lst, 0]

    def _alloc(self, name):
        cm = self.nc.semaphore(name)
        h = cm.__enter__()
        self._stack.append(cm)
        return h

    def _new_sem(self, e):
        k = f"s_{e}{self.gen[e]}"
        self.gen[e] += 1
        self.sems[k] = self._alloc(k)
        self.owner[k] = e
        self.cur[e] = [k, 0]

    def close(self):
        for cm in reversed(self._stack):
            cm.__exit__(None, None, None)

    def _need(self, eng, evs):
        for (k, v) in evs:
            if self.waited[eng].get(k, 0) < v:
                self.engs[eng].wait_ge(self.sems[k], v)
                self.waited[eng][k] = v
                self.nwaits += 1

    def _deps(self, eng, reads, writes):
        evs = []
        arena = False
        for r in reads:
            arena |= r.arena
            if r.multi:
                evs.extend(r.ws.items())
            elif r.w is not None and not (eng == "pe" and self.owner[r.w[0]] == "pe"):
                evs.append(r.w)
            if r.excl:
                for k, v in r.r.items():
                    if self.owner[k] != eng:
                        evs.append((k, v))
        for w in writes:
            arena |= w.arena
            if w.multi:
                pass
            elif w.w is not None and not (eng == "pe" and self.owner[w.w[0]] == "pe"):
                evs.append(w.w)
            for k, v in w.r.items():
                if not (eng == "pe" and self.owner[k] == "pe"):
                    evs.append((k, v))
        if arena and self.pending[eng]:
            evs.extend(self.pending[eng].items())
            self.pending[eng] = {}
        self._need(eng, evs)

    def _commit(self, ev, reads, writes):
        k, v = ev
        for r in reads:
            if r.r.get(k, 0) < v:
                r.r[k] = v
            if r.arena:
                self.arena_evs[k] = max(self.arena_evs.get(k, 0), v)
        for w in writes:
            if w.multi:
                w.ws[k] = max(w.ws.get(k, 0), v)
            else:
                w.w = ev
                w.r = {}
            if w.arena:
                self.arena_evs[k] = max(self.arena_evs.get(k, 0), v)

    def barrier(self):
        for e in self.engs:
            p = self.pending[e]
            for k, v in self.arena_evs.items():
                if p.get(k, 0) < v:
                    p[k] = v
        self.arena_evs = {}

    def op(self, eng, fn, reads=(), writes=()):
        self._deps(eng, reads, writes)
        c = self.cur[eng]
        if c[1] >= self.SEM_LIMIT:
            self._new_sem(eng)
            c = self.cur[eng]
        ins = fn()
        c[1] += 1
        ins.then_inc(self.sems[c[0]], 1)
        self.nops += 1
        self._commit((c[0], c[1]), reads, writes)

    def dma(self, q, out, in_, reads=(), writes=()):
        self._deps(q, reads, writes)
        lst, idx = self.dq[q]
        slot = lst[idx]
        self.dq[q][1] = (idx + 1) % len(lst)
        if slot[1] > 0:
            self._need(q, [(slot[0], slot[1])])
        slot[1] += 16
        self.engs[q].dma_start(out=out, in_=in_).then_inc(self.sems[slot[0]], 16)
        self.nops += 1
        self._commit((slot[0], slot[1]), reads, writes)

    def finish(self):
        evs = []
        for q in self.dq:
            for s in self.dq[q][0]:
                if s[1] > 0:
                    evs.append((s[0], s[1]))
        for e in ("pe", "act", "dve", "pool"):
            if self.cur[e][1] > 0:
                evs.append((self.cur[e][0], self.cur[e][1]))
        self._need("pool", evs)
        self._need("sp", evs)


class Ring:
    def __init__(self, items):
        self.items = items
        self.i = 0

    def next(self):
        it = self.items[self.i]
        self.i = (self.i + 1) % len(self.items)
        return it


def build_program(npair=8, do_sample=True, stage=9):
    nc = bass.Bass("TRN2", target_bir_lowering=False)
    tr = Trk(nc)
    V, S_, P_ = nc.vector, nc.scalar, nc.tensor

    def din(name, shape, dt=F32):
        return nc.dram_tensor(name, list(shape), dt, kind="ExternalInput").ap()

    def dout(name, shape, dt=F32):
        return nc.dram_tensor(name, list(shape), dt, kind="ExternalOutput").ap()

    def dscr(name, shape, dt=BF16):
        return nc.dram_tensor(name, list(shape), dt, kind="Internal").ap()

    xo = din("xo", [8, T, D]); xk = din("xk", [8, T, D]); xs = din("xs", [16, D])
    rope_o = din("rope_o", [8, T, RW]); rope_k = din("rope_k", [8, T, RW]); rope_s = din("rope_s", [16, RW])
    c_ckv = din("c_ckv", [1024, 256]); c_kr = din("c_kr", [1024, 64])
    c_dk = din("c_dk", [1024, 1024]); c_dv = din("c_dv", [1024, 1024])
    w_in = din("w_in", [D, 8000]); w_uq = din("w_uq", [512, 1536]); w_ukv = din("w_ukv", [256, 2048])
    w_ba = din("w_ba", [1024, D]); w_bb = din("w_bb", [1024, D]); w_out = din("w_out", [D, D])
    w_fi = din("w_fi", [D, 2 * DFF]); w_fo = din("w_fo", [DFF, D])
    g_mix = din("g_mix", [D]); g_q = din("g_q", [512]); g_kv = din("g_kv", [256])
    lq1 = din("lq1", [64]); lk1 = din("lk1", [64]); lq2 = din("lq2", [64]); lk2 = din("lk2", [64])
    g_sub = din("g_sub", [128]); g_ffn = din("g_ffn", [D]); g_fin = din("g_fin", [D])
    ident_d = din("ident", [128, 128], BF16); obias_d = din("obias", [128, 1])

    yo = dout("yo", [8, T, D]); ckv_o = dout("ckv_o", [8, T, 256]); kr_o = dout("kr_o", [8, T, 64])
    dk_o = dout("dk_o", [8, T, 1024]); dv_o = dout("dv_o", [8, T, 1024])
    ys = dout("ys", [16, D]); ckv_s = dout("ckv_s", [16, 256]); kr_s = dout("kr_s", [16, 64])
    dk_s = dout("dk_s", [16, 1024]); dv_s = dout("dv_s", [16, 1024])

    WA = [dscr(f"WA{i}", [128, 16, 512]) for i in range(8)]
    WG = [dscr(f"WG{i}", [128, 16, 512]) for i in range(8)]
    WUQ = dscr("WUQ", [128, 4, 1536])
    WUKV = dscr("WUKV", [128, 2, 2048])
    WBA = [dscr(f"WBA{i}", [128, 8, 512]) for i in range(4)]
    WBB = [dscr(f"WBB{i}", [128, 8, 512]) for i in range(4)]
    WO = [dscr(f"WO{i}", [128, 16, 512]) for i in range(4)]
    WFG = [dscr(f"WFG{i}", [128, 16, 512]) for i in range(11)]
    WFU = [dscr(f"WFU{i}", [128, 16, 512]) for i in range(11)]
    WFO = [dscr(f"WFO{i}", [128, 4, 2048]) for i in range(11)]
    KN = dscr("KN", [8, 128, NBLK * T]); KR = dscr("KR", [64, NBLK * T]); DK = dscr("DK", [8, 128, NBLK * T])
    VM = dscr("VM", [8, 128, NKT, 129]); DV = dscr("DV", [8, 128, NKT, 129])
    R_w = {}

    def wres(name):
        if name not in R_w:
            R_w[name] = Res(name, multi=name in ("KN", "KR", "DK", "VM", "DV", "WUQ", "WUKV"))
        return R_w[name]

    def prep(dst, src, name):
        tr.dma("pool", dst, src, writes=[wres(name)])

    def kc_view(w, c0, n):
        return w[:, c0:c0 + n].rearrange("(kc p) n -> p kc n", p=128)

    def emit_prep(part):
        if part == 2:
            return emit_prep2()
        cols = [(0, 512), (512, 320), (832, 512), (1344, 512), (1856, 512), (2368, 512), (2880, 512), (3392, 512)]
        for i, (c0, n) in enumerate(cols):
            prep(WA[i], kc_view(w_in, c0, 512), f"WA{i}")
        for kc in range(4):
            src = w_uq[kc * 128:(kc + 1) * 128, :].rearrange("p (h d) -> p h d", d=192)
            prep(WUQ[:, kc, 0:1024].rearrange("p (h d) -> p h d", d=128), src[:, :, 0:128], "WUQ")
            prep(WUQ[:, kc, 1024:1536].rearrange("p (h d) -> p h d", d=64), src[:, :, 128:192], "WUQ")
        for kc in range(2):
            src = w_ukv[kc * 128:(kc + 1) * 128, :].rearrange("p (h d) -> p h d", d=256)
            prep(WUKV[:, kc, 0:1024].rearrange("p (h d) -> p h d", d=128), src[:, :, 0:128], "WUKV")
            prep(WUKV[:, kc, 1024:2048].rearrange("p (h d) -> p h d", d=128), src[:, :, 128:256], "WUKV")
    def emit_prep2():
        for i in range(8):
            prep(WG[i], kc_view(w_in, 3904 + i * 512, 512), f"WG{i}")
        for i in range(4):
            prep(WBA[i], kc_view(w_ba, i * 512, 512), f"WBA{i}")
            prep(WBB[i], kc_view(w_bb, i * 512, 512), f"WBB{i}")
        for i in range(4):
            prep(WO[i], kc_view(w_out, i * 512, 512), f"WO{i}")
        for i in range(11):
            prep(WFG[i], kc_view(w_fi, i * 512, 512), f"WFG{i}")
            prep(WFU[i], kc_view(w_fi, DFF + i * 512, 512), f"WFU{i}")
            prep(WFO[i], w_fo[i * 512:(i + 1) * 512, :].rearrange("(kc p) n -> p kc n", p=128), f"WFO{i}")

    ctxs = []

    uid = [0]

    def sb(name, shape, dt):
        uid[0] += 1
        cm = nc.sbuf_tensor(f"sb{uid[0]}_{name}", list(shape), dt)
        t = cm.__enter__()
        ctxs.append(cm)
        return t

    class Scope:
        def __enter__(self):
            self.mark = len(ctxs)
            tr.barrier()
            return self

        def __exit__(self, *a):
            while len(ctxs) > self.mark:
                ctxs.pop().__exit__(None, None, None)
            tr.barrier()
            return False

    ident = sb("ident", [128, 128], BF16); R_ident = Res("ident")
    zcol = sb("zcol", [128, 1], F32); obias = sb("obias", [128, 1], F32); R_cst = Res("cst")
    gmix_c = sb("gmix_c", [128, 16], F32); gffn_c = sb("gffn_c", [128, 16], F32); gq_c = sb("gq_c", [128, 4], F32)
    gkv_b = sb("gkv_b", [128, 256], F32); gsub_b = sb("gsub_b", [128, 128], F32); gfin_b = sb("gfin_b", [128, D], F32)
    lam4 = sb("lam4", [128, 4, 64], F32); lamt = sb("lamt", [128, 4], F32); nlam = sb("nlam", [128, 1], F32)
    ones_bf = sb("ones_bf", [128, 8], BF16)
    hT = [sb(f"hT{i}", [128, 16, T], BF16) for i in range(2)]
    R_hT = [Res(f"hT{i}") for i in range(2)]
    hring = Ring([0, 1])
    QnT = sb("QnT", [128, 8, T], BF16); QrT = sb("QrT", [64, 8, T], BF16); Q1p = sb("Q1p", [128, 8, T], BF16); Q2p = sb("Q2p", [128, 8, T], BF16)
    R_QnT, R_QrT, R_Q12T = Res("QnT"), Res("QrT"), Res("Q12T")
    NW = 3
    wslot = [sb(f"wslot{i}", [128, 8192], BF16) for i in range(NW)]
    R_wslot = [Res(f"wslot{i}") for i in range(NW)]
    wring = Ring(list(range(NW)))
    small = sb("small", [128, 64], F32)
    R_small = [Res(f"small{i}") for i in range(64)]
    smring = Ring(list(range(64)))

    psb = []
    for i in range(8):
        cm = nc.psum_tensor(f"ps{i}", [128, 512], F32)
        psb.append(cm.__enter__())
        ctxs.append(cm)
    R_ps = [Res(f"ps{i}", excl=True) for i in range(8)]
    psring = Ring(list(range(8)))

    def set_psring(banks):
        psring.items = list(banks)
        psring.i = 0

    flip = [0]

    def evac_eng():
        flip[0] ^= 1
        return "act" if flip[0] else "dve"

    def copy(eng, out, in_, reads, writes):
        if eng == "act":
            tr.op("act", lambda: S_.activation(out=out, in_=in_, func=AF.Copy), reads, writes)
        else:
            tr.op("dve", lambda: V.tensor_copy(out, in_), reads, writes)

    def wload(dram_ap, rname, nelem):
        i = wring.next()
        view = wslot[i][:, 0:nelem]
        src = dram_ap if len(dram_ap.shape) == 2 else dram_ap.rearrange("p a b -> p (a b)")
        tr.dma("sp", view, src, reads=[wres(rname)], writes=[R_wslot[i]])
        return view, R_wslot[i]

    def wstream(items, depth=2):
        q = []
        it = iter(items)
        for _ in range(depth):
            x = next(it, None)
            if x is not None:
                q.append(wload(*x))
        while q:
            cur = q.pop(0)
            x = next(it, None)
            if x is not None:
                q.append(wload(*x))
            yield cur

    def smcol():
        i = smring.next()
        return small[:, i:i + 1], R_small[i]

    def rstd_from_ss(col, rcol, n, tp):
        tr.op("act", lambda: S_.activation(out=col[:tp], in_=col[:tp], func=AF.Ln, scale=1.0 / n, bias=EPS),
              reads=[rcol], writes=[rcol])
        tr.op("act", lambda: S_.activation(out=col[:tp], in_=col[:tp], func=AF.Exp, scale=-0.5),
              reads=[rcol], writes=[rcol])

    def emit_consts():
        tr.dma("sp", ident[:], ident_d, writes=[R_ident])
        tr.dma("sp", obias[:], obias_d, writes=[R_cst])
        with nc.allow_non_contiguous_dma(reason="tiny one-off per-partition gain columns"):
            for j in range(4):
                tr.dma("sp", gmix_c[:, 4 * j:4 * j + 4], g_mix[512 * j:512 * j + 512].rearrange("(c p) -> p c", p=128), writes=[R_cst])
                tr.dma("sp", gffn_c[:, 4 * j:4 * j + 4], g_ffn[512 * j:512 * j + 512].rearrange("(c p) -> p c", p=128), writes=[R_cst])
            tr.dma("sp", gq_c[:], g_q.rearrange("(c p) -> p c", p=128), writes=[R_cst])
        tr.dma("sp", gkv_b[:], g_kv.partition_broadcast(128), writes=[R_cst])
        tr.dma("sp", gsub_b[:], g_sub.partition_broadcast(128), writes=[R_cst])
        tr.dma("sp", gfin_b[:], g_fin.partition_broadcast(128), writes=[R_cst])
        for j, v in enumerate((lq1, lk1, lq2, lk2)):
            tr.dma("sp", lam4[:, j, :], v.partition_broadcast(128), writes=[R_cst])
        tr.op("dve", lambda: V.memset(zcol[:], 0.0), writes=[R_cst])
        tr.op("pool", lambda: nc.gpsimd.memset(Q1p[64:128, :, :], 0.0), writes=[R_Q12T])
        tr.op("pool", lambda: nc.gpsimd.memset(Q2p[0:64, :, :], 0.0), writes=[R_Q12T])
        tr.op("dve", lambda: V.memset(ones_bf[:], 1.0), writes=[R_cst])
        tr.op("dve", lambda: V.tensor_tensor(lam4[:, 0, :], lam4[:, 0, :], lam4[:, 1, :], ALU.mult), [R_cst], [R_cst])
        tr.op("dve", lambda: V.tensor_tensor(lam4[:, 2, :], lam4[:, 2, :], lam4[:, 3, :], ALU.mult), [R_cst], [R_cst])
        tr.op("dve", lambda: V.reduce_sum(lamt[:, 0:1], lam4[:, 0, :], mybir.AxisListType.X), [R_cst], [R_cst])
        tr.op("dve", lambda: V.reduce_sum(lamt[:, 1:2], lam4[:, 2, :], mybir.AxisListType.X), [R_cst], [R_cst])
        tr.op("act", lambda: S_.activation(out=lamt[:, 2:4], in_=lamt[:, 0:2], func=AF.Exp), [R_cst], [R_cst])
        tr.op("dve", lambda: V.scalar_tensor_tensor(nlam[:], lamt[:, 3:4], -LAMBDA_INIT, lamt[:, 2:3],
                                                    ALU.add, ALU.subtract), [R_cst], [R_cst])

    def front(kind, nt, blk, x_src=None, rope_src=None, outs=None, cache_rows=None):
        subs = [(s, min(128, nt - s * 128)) for s in range((nt + 127) // 128)]
        ns = len(subs)
        own = kind == "own"
        A = lambda n: Res(n, arena=True)
        g3 = lambda ap, w: ap.rearrange("p (g w) -> p g w", w=w)
        with Scope():
            set_psring(range(8))
            if kind != "cache":
                hi = hring.next()
                h_T, R_h = hT[hi], R_hT[hi]
                with Scope():
                    xt = [sb(f"xt{i}", [128, D], F32) for i in range(2)]; R_xt = [A("xt0"), A("xt1")]
                    xb4 = sb("xb4", [128, ns, D], BF16); R_xb4 = [A(f"xb4_{s}") for s in range(ns)]
                    for s, tp in subs:
                        b = s % 2
                        tr.dma("sp", xt[b][:tp, :], x_src[s * 128:s * 128 + tp, :], writes=[R_xt[b]])
                        ss, R_ss = smcol()
                        tr.op("act", lambda: S_.activation(out=xb4[:tp, s, :], in_=xt[b][:tp, :], func=AF.Square,
                                                           accum_out=ss[:tp]),
                              reads=[R_xt[b]], writes=[R_xb4[s], R_ss])
                        rstd_from_ss(ss, R_ss, D, tp)
                        tr.op("dve", lambda: V.tensor_scalar(xb4[:tp, s, :], xt[b][:tp, :], ss[:tp], None, ALU.mult),
                              reads=[R_xt[b], R_ss], writes=[R_xb4[s]])
                    for dc2 in range(8):
                        bi = psring.next()
                        pbf = psb[bi][:].bitcast(BF16)

                        def trs(dc2=dc2, pbf=pbf):
                            ins = None
                            for u in range(2):
                                dc = dc2 * 2 + u
                                for s, tp in subs:
                                    ins = P_.transpose(pbf[:, u * 512 + s * 128:u * 512 + s * 128 + tp],
                                                       xb4[:tp, s, dc * 128:(dc + 1) * 128], ident[:tp, :tp])
                            return ins
                        tr.op("pe", trs, reads=R_xb4 + [R_ident], writes=[R_ps[bi]])
                        for u in range(2):
                            dc = dc2 * 2 + u
                            if dc2 % 2 == 0:
                                tr.op("act", lambda: S_.activation(out=h_T[:, dc, 0:nt], in_=pbf[:, u * 512:u * 512 + nt],
                                                                   func=AF.Copy, scale=gmix_c[:, dc:dc + 1]),
                                      reads=[R_ps[bi], R_cst], writes=[R_h])
                            else:
                                tr.op("dve", lambda: V.tensor_scalar(h_T[:, dc, 0:nt], pbf[:, u * 512:u * 512 + nt],
                                                                     gmix_c[:, dc:dc + 1], None, ALU.mult),
                                      reads=[R_ps[bi], R_cst], writes=[R_h])

            if DBG["cut"] <= 1:
                return
            ckvb = sb("ckvb", [128, ns, 256], BF16); R_ckvb = A("ckvb")
            krb = sb("krb", [128, ns, 64], BF16); R_krb = A("krb")
            dkb = sb("dkb", [128, ns, 1024], BF16); R_dkb = A("dkb")
            vst = [sb(f"vst{i}", [128, 8, 129], BF16) for i in range(2)]; R_vst = [A("vst0"), A("vst1")]
            dst = [sb(f"dst{i}", [128, 4, 129], BF16) for i in range(2)]; R_dst = [A("dst0"), A("dst1")]
            kst = [sb(f"kst{i}", [128, T], BF16) for i in range(4)]; R_kst = [A(f"kst{i}") for i in range(4)]
            kring = Ring([0, 1, 2, 3])
            dring = Ring([0, 1])
            ckvT = sb("ckvT", [128, 2, T], BF16); R_ckvT = A("ckvT")
            for i in range(2):
                if DBG.get("nomemset", 0):
                    break
                tr.op("dve", (lambda i=i: V.memset(vst[i][:, :, 128:129], 1.0)), writes=[R_vst[i]])
                tr.op("dve", (lambda i=i: V.memset(dst[i][:, :, 128:129], 1.0)), writes=[R_dst[i]])
            if kind != "cache":
                rt = sb("rt", [128, ns, RW], F32); R_rt = A("rt")
                o_ckv = [sb(f"o_ckv{i}", [128, 256], F32) for i in range(2)]; R_ockv = [A("ockv0"), A("ockv1")]
                o_kr = [sb(f"o_kr{i}", [128, 64], F32) for i in range(2)]; R_okr = [A("okr0"), A("okr1")]
                o_blk = [sb(f"o_blk{i}", [128, 512], F32) for i in range(3)]; R_oblk = [A(f"oblk{i}") for i in range(3)]
                oring = Ring([0, 1, 2])
                tmp = sb("ropetmp", [128, 4, 256], F32); R_tmp = A("ropetmp")
                if own:
                    qlb = sb("qlb", [128, ns, 512], BF16); R_qlb = A("qlb")
                    dqb = sb("dqb", [128, ns, 1024], BF16); R_dqb = A("dqb")
                    qrb = sb("qrb", [128, ns, 512], BF16); R_qrb = A("qrb")
                    qlatT = sb("qlatT", [128, 4, T], BF16); R_qlatT = A("qlatT")
                for s, tp in subs:
                    if DBG.get("nort", 0):
                        break
                    tr.dma("sp", rt[:tp, s, :], rope_src[s * 128:s * 128 + tp, :], writes=[R_rt])

                def rope(src3, dst3, C3, S3, tp, G, hw, rd, wr):
                    n = G * hw
                    t = [tmp[:tp, j, 0:n].rearrange("p (g w) -> p g w", w=hw) for j in range(4)]
                    x1, x2 = src3[:, :, 0:hw], src3[:, :, hw:2 * hw]
                    rm = DBG.get("ropemask", 3)
                    if rm & 1:
                        tr.op("dve", lambda: V.tensor_tensor(t[0], x1, C3, ALU.mult), rd, [R_tmp])
                        tr.op("dve", lambda: V.tensor_tensor(t[1], x2, S3, ALU.mult), rd, [R_tmp])
                        tr.op("dve", lambda: V.tensor_tensor(t[2], x2, C3, ALU.mult), rd, [R_tmp])
                        tr.op("dve", lambda: V.tensor_tensor(t[3], x1, S3, ALU.mult), rd, [R_tmp])
                    if rm & 2:
                        tr.op("dve", lambda: V.tensor_tensor(dst3[:, :, 0:hw], t[0], t[1], ALU.subtract), [R_tmp], wr)
                        tr.op("dve", lambda: V.tensor_tensor(dst3[:, :, hw:2 * hw], t[2], t[3], ALU.add), [R_tmp], wr)

                blocks = [0, 1, 2, 3, 4, 5, 6, 7] if own else [1, 4, 5, 6, 7]
                blocks = blocks[:DBG.get("nblk", 99)]
                ncols = {0: 512, 1: 320}
                ws_ = wstream([(WA[nb_], f"WA{nb_}", 8192) for nb_ in blocks])
                for bix, nb in enumerate(blocks):
                    wv, R_wv = next(ws_)
                    w3 = wv.rearrange("p (kc n) -> p kc n", n=512)
                    ncl = ncols.get(nb, 512)
                    if DBG.get("ncl512", 0):
                        ncl = 512
                    for s, tp in subs:
                        b = s % 2
                        r0 = s * 128
                        bi = psring.next()
                        bank = psb[bi]

                        def mm(s=s, tp=tp, bank=bank, w3=w3, ncl=ncl):
                            ins = None
                            for kc in range(16):
                                ins = P_.matmul(bank[:tp, 0:ncl], h_T[:, kc, s * 128:s * 128 + tp], w3[:, kc, 0:ncl],
                                                start=(kc == 0), stop=(kc == 15))
                            return ins
                        if not DBG.get("nomm", 0):
                            tr.op("pe", mm, reads=[R_h, R_wv], writes=[R_ps[bi]])
                        Rb = R_ps[bi]
                        if DBG.get("noev", 0):
                            continue
                        if nb == 0:
                            ss, R_ss = smcol()
                            tr.op("act", lambda: S_.activation(out=qlb[:tp, s, :], in_=bank[:tp, 0:512], func=AF.Square,
                                                               accum_out=ss[:tp]), [Rb], [R_qlb, R_ss])
                            rstd_from_ss(ss, R_ss, 512, tp)
                            tr.op("act", lambda: S_.activation(out=qlb[:tp, s, :], in_=bank[:tp, 0:512], func=AF.Copy,
                                                               scale=ss[:tp]), [Rb, R_ss], [R_qlb])
                        elif nb == 1:
                            ss, R_ss = smcol()
                            tr.op("act", lambda: S_.activation(out=o_ckv[b][:tp, :], in_=bank[:tp, 0:256], func=AF.Square,
                                                               accum_out=ss[:tp]), [Rb], [R_ockv[b], R_ss])
                            rstd_from_ss(ss, R_ss, 256, tp)
                            tr.op("dve", lambda: V.scalar_tensor_tensor(o_ckv[b][:tp, :], bank[:tp, 0:256], ss[:tp],
                                                                        gkv_b[:tp, :], ALU.mult, ALU.mult),
                                  [Rb, R_ss, R_cst], [R_ockv[b]])
                            tr.op("act", lambda: S_.activation(out=ckvb[:tp, s, :], in_=o_ckv[b][:tp, :], func=AF.Copy),
                                  [R_ockv[b]], [R_ckvb])
                            rope(g3(bank[:tp, 256:320], 64), g3(o_kr[b][:tp, :], 64),
                                 g3(rt[:tp, s, 0:32], 32), g3(rt[:tp, s, 256:288], 32), tp, 1, 32,
                                 [Rb, R_rt], [R_okr[b]])
                            tr.op("act", lambda: S_.activation(out=krb[:tp, s, :], in_=o_kr[b][:tp, :], func=AF.Copy),
                                  [R_okr[b]], [R_krb])
                            if outs is not None:
                                tr.dma("pool", outs["ckv"][r0:r0 + tp, :], o_ckv[b][:tp, :], reads=[R_ockv[b]])
                                tr.dma("pool", outs["kr"][r0:r0 + tp, :], o_kr[b][:tp, :], reads=[R_okr[b]])
                        elif nb in (2, 3):
                            c0 = (nb - 2) * 512
                            tr.op("act", lambda: S_.activation(out=dqb[:tp, s, c0:c0 + 512], in_=bank[:tp, 0:512],
                                                               func=AF.Copy), [Rb], [R_dqb])
                            rope(g3(bank[:tp, 0:512], 64), g3(dqb[:tp, s, c0:c0 + 512], 64),
                                 g3(rt[:tp, s, 512:576], 8), g3(rt[:tp, s, 576:640], 8), tp, 8, 8,
                                 [Rb, R_rt], [R_dqb])
                        elif nb in (4, 5):
                            c0 = (nb - 4) * 512
                            oi = oring.next()
                            tr.op("act", lambda: S_.activation(out=o_blk[oi][:tp, :], in_=bank[:tp, 0:512],
                                                               func=AF.Copy), [Rb], [R_oblk[oi]])
                            if not DBG.get("norope", 0):
                                rope(g3(bank[:tp, 0:512], 64), g3(o_blk[oi][:tp, :], 64),
                                     g3(rt[:tp, s, 512:576], 8), g3(rt[:tp, s, 576:640], 8), tp, 8, 8,
                                     [Rb, R_rt] + ([R_oblk[oi]] if DBG.get("ropeser", 0) else []), [R_oblk[oi]])
                            tr.op("act", lambda: S_.activation(out=dkb[:tp, s, c0:c0 + 512], in_=o_blk[oi][:tp, :],
                                                               func=AF.Copy), [R_oblk[oi]], [R_dkb])
                            if outs is not None:
                                tr.dma("pool", outs["dk"][r0:r0 + tp, c0:c0 + 512], o_blk[oi][:tp, :], reads=[R_oblk[oi]])
                        else:
                            c0 = (nb - 6) * 512
                            h0 = (nb - 6) * 4
                            di = dring.next()
                            tr.op("dve", lambda: V.tensor_copy(dst[di][:tp, :, 0:128], g3(bank[:tp, 0:512], 128)),
                                  [Rb], [R_dst[di]])
                            kt = blk * 4 + s
                            tr.dma("pool", DV[h0:h0 + 4, 0:tp, kt, :].rearrange("h p c -> p h c"), dst[di][:tp, :, :],
                                   reads=[R_dst[di]], writes=[wres("DV")])
                            if outs is not None:
                                oi = oring.next()
                                tr.op("act", lambda: S_.activation(out=o_blk[oi][:tp, :], in_=bank[:tp, 0:512],
                                                                   func=AF.Copy), [Rb], [R_oblk[oi]])
                                tr.dma("pool", outs["dv"][r0:r0 + tp, c0:c0 + 512], o_blk[oi][:tp, :], reads=[R_oblk[oi]])
            else:
                cst = [sb(f"cst32_{i}", [128, 1024], F32) for i in range(2)]; R_c32 = [A("cst32_0"), A("cst32_1")]
                cring = Ring([0, 1])
                for s, tp in subs:
                    r0 = cache_rows + s * 128
                    kt = blk * 4 + s
                    ci = cring.next()
                    tr.dma("sp", cst[ci][:tp, 0:256], c_ckv[r0:r0 + tp, :], writes=[R_c32[ci]])
                    tr.dma("sp", cst[ci][:tp, 256:320], c_kr[r0:r0 + tp, :], writes=[R_c32[ci]])
                    copy("dve", ckvb[:tp, s, :], cst[ci][:tp, 0:256], [R_c32[ci]], [R_ckvb])
                    copy("dve", krb[:tp, s, :], cst[ci][:tp, 256:320], [R_c32[ci]], [R_krb])
                    ci = cring.next()
                    tr.dma("sp", cst[ci][:tp, :], c_dk[r0:r0 + tp, :], writes=[R_c32[ci]])
                    copy("act", dkb[:tp, s, :], cst[ci][:tp, :], [R_c32[ci]], [R_dkb])
                    ci = cring.next()
                    tr.dma("sp", cst[ci][:tp, :], c_dv[r0:r0 + tp, :], writes=[R_c32[ci]])
                    for half in range(2):
                        di = dring.next()
                        copy("dve", dst[di][:tp, :, 0:128], g3(cst[ci][:tp, half * 512:(half + 1) * 512], 128),
                             [R_c32[ci]], [R_dst[di]])
                        tr.dma("pool", DV[half * 4:half * 4 + 4, 0:tp, kt, :].rearrange("h p c -> p h c"), dst[di][:tp, :, :],
                               reads=[R_dst[di]], writes=[wres("DV")])

            if DBG["cut"] <= 2:
                return
            def tr_group(src_of_s, width, emit_evac, rd):
                bi = psring.next()
                pbf = psb[bi][:].bitcast(BF16)

                def f():
                    ins = None
                    for s, tp in subs:
                        ins = P_.transpose(pbf[0:width, s * 128:s * 128 + tp], src_of_s(s, tp), ident[:tp, :tp])
                    return ins
                tr.op("pe", f, reads=rd + [R_ident], writes=[R_ps[bi]])
                emit_evac(pbf[0:width, 0:nt], R_ps[bi])

            for c in range(2):
                def ev(src, Rb, c=c):
                    copy(evac_eng(), ckvT[:, c, 0:nt], src, [Rb], [R_ckvT])
                tr_group(lambda s, tp, c=c: ckvb[:tp, s, c * 128:(c + 1) * 128], 128, ev, [R_ckvb])

            def ev_kr(src, Rb):
                ki = kring.next()
                copy(evac_eng(), kst[ki][0:64, 0:nt], src, [Rb], [R_kst[ki]])
                tr.dma("pool", KR[:, blk * T:blk * T + nt], kst[ki][0:64, 0:nt], reads=[R_kst[ki]], writes=[wres("KR")])
            tr_group(lambda s, tp: krb[:tp, s, :], 64, ev_kr, [R_krb])
            for h in range(8):
                def ev_dk(src, Rb, h=h):
                    ki = kring.next()
                    copy(evac_eng(), kst[ki][:, 0:nt], src, [Rb], [R_kst[ki]])
                    tr.dma("pool", DK[h, :, blk * T:blk * T + nt], kst[ki][:, 0:nt], reads=[R_kst[ki]], writes=[wres("DK")])
                tr_group(lambda s, tp, h=h: dkb[:tp, s, h * 128:(h + 1) * 128], 128, ev_dk, [R_dkb])
            if own:
                for h in range(8):
                    def ev_dq(src, Rb, h=h):
                        e = evac_eng()
                        copy(e, Q1p[0:64, h, 0:nt], src[0:64, :], [Rb], [R_Q12T])
                        copy(e, Q2p[64:128, h, 0:nt], src[64:128, :], [Rb], [R_Q12T])
                    tr_group(lambda s, tp, h=h: dqb[:tp, s, h * 128:(h + 1) * 128], 128, ev_dq, [R_dqb])
                for c in range(4):
                    def ev_ql(src, Rb, c=c):
                        tr.op("act", lambda: S_.activation(out=qlatT[:, c, 0:nt], in_=src, func=AF.Copy,
                                                           scale=gq_c[:, c:c + 1]), [Rb, R_cst], [R_qlatT])
                    tr_group(lambda s, tp, c=c: qlb[:tp, s, c * 128:(c + 1) * 128], 128, ev_ql, [R_qlb])

            if DBG["cut"] <= 3:
                return
            wv, R_wv = wload(WUKV, "WUKV", 4096)
            wk3 = wv.rearrange("p (kc n) -> p kc n", n=2048)
            for h in range(8):
                bi = psring.next()
                bank = psb[bi]

                def mmk(h=h, bank=bank):
                    ins = None
                    for kc in range(2):
                        ins = P_.matmul(bank[:, 0:nt], wk3[:, kc, h * 128:(h + 1) * 128], ckvT[:, kc, 0:nt],
                                        start=(kc == 0), stop=(kc == 1))
                    return ins
                tr.op("pe", mmk, reads=[R_wv, R_ckvT], writes=[R_ps[bi]])
                ki = kring.next()
                copy(evac_eng(), kst[ki][:, 0:nt], bank[:, 0:nt], [R_ps[bi]], [R_kst[ki]])
                tr.dma("pool", KN[h, :, blk * T:blk * T + nt], kst[ki][:, 0:nt], reads=[R_kst[ki]], writes=[wres("KN")])
            for s, tp in subs:
                b = s % 2
                for half in range(2):
                    bi = psring.next()
                    bank = psb[bi]

                    def mmv(s=s, tp=tp, half=half, bank=bank):
                        ins = None
                        for kc in range(2):
                            ins = P_.matmul(bank[:tp, 0:512], ckvT[:, kc, s * 128:s * 128 + tp],
                                            wk3[:, kc, 1024 + half * 512:1024 + (half + 1) * 512],
                                            start=(kc == 0), stop=(kc == 1))
                        return ins
                    tr.op("pe", mmv, reads=[R_wv, R_ckvT], writes=[R_ps[bi]])
                    copy(evac_eng(), vst[b][:tp, half * 4:half * 4 + 4, 0:128], g3(bank[:tp, 0:512], 128),
                         [R_ps[bi]], [R_vst[b]])
                kt = blk * 4 + s
                tr.dma("pool", VM[:, 0:tp, kt, :].rearrange("h p c -> p h c"), vst[b][:tp, :, :],
                       reads=[R_vst[b]], writes=[wres("VM")])

            if own:
                wv, R_wv = wload(WUQ, "WUQ", 6144)
                wq3 = wv.rearrange("p (kc n) -> p kc n", n=1536)
                for h in range(8):
                    bi = psring.next()
                    bank = psb[bi]

                    def mmq(h=h, bank=bank):
                        ins = None
                        for kc in range(4):
                            ins = P_.matmul(bank[:, 0:nt], wq3[:, kc, h * 128:(h + 1) * 128], qlatT[:, kc, 0:nt],
                                            start=(kc == 0), stop=(kc == 3))
                        return ins
                    tr.op("pe", mmq, reads=[R_wv, R_qlatT], writes=[R_ps[bi]])
                    copy(evac_eng(), QnT[:, h, 0:nt], bank[:, 0:nt], [R_ps[bi]], [R_QnT])
                for s, tp in subs:
                    bi = psring.next()
                    bank = psb[bi]

                    def mmr(s=s, tp=tp, bank=bank):
                        ins = None
                        for kc in range(4):
                            ins = P_.matmul(bank[:tp, 0:512], qlatT[:, kc, s * 128:s * 128 + tp], wq3[:, kc, 1024:1536],
                                            start=(kc == 0), stop=(kc == 3))
                        return ins
                    tr.op("pe", mmr, reads=[R_wv, R_qlatT], writes=[R_ps[bi]])
                    rope(g3(bank[:tp, 0:512], 64), g3(qrb[:tp, s, :], 64),
                         g3(rt[:tp, s, 0:256], 32), g3(rt[:tp, s, 256:512], 32), tp, 8, 32,
                         [R_ps[bi], R_rt], [R_qrb])
                for h in range(8):
                    def ev_qr(src, Rb, h=h):
                        copy(evac_eng(), QrT[:, h, 0:nt], src, [Rb], [R_QrT])
                    tr_group(lambda s, tp, h=h: qrb[:tp, s, h * 64:(h + 1) * 64], 64, ev_qr, [R_qrb])

    def attention(nt, ktiles, outAT, R_oA, outBT, R_oB):
        subs = [(s, min(128, nt - s * 128)) for s in range((nt + 127) // 128)]
        A = lambda n: Res(n, arena=True)
        NB = 3
        kn_s = [sb(f"kn_s{i}", [128, 1024], BF16) for i in range(NB)]
        kr_s = [sb(f"kr_s{i}", [64, 1024], BF16) for i in range(NB)]
        v_s = [sb(f"v_s{i}", [128, 8, 129], BF16) for i in range(NB)]
        R_kn = [A(f"skn{i}") for i in range(NB)]; R_kr = [A(f"skr{i}") for i in range(NB)]; R_v = [A(f"sv{i}") for i in range(NB)]
        sring = Ring(list(range(NB)))
        NPB = 4
        pt = [sb(f"pt{i}", [128, 2, T], BF16) for i in range(NPB)]
        R_pt = [A(f"pt{i}") for i in range(NPB)]
        pring = Ring(list(range(NPB)))
        otok = [sb(f"otok{i}", [128, 4, 128], BF16) for i in range(2)]; R_otok = [A("otok0"), A("otok1")]
        oring = Ring([0, 1])
        o1 = sb("o1", [128, 128], F32); R_o1 = A("o1")
        o2 = sb("o2", [128, 4, 128], F32); R_o2 = A("o2")
        rz = sb("rz", [128, 32], F32); R_rz = A("rz")
        rz2 = sb("rz2", [128, 8], F32); R_rz2 = A("rz2")
        accM = [(2, 0), (2, 129), (2, 258), (3, 0)]
        acc1 = [(4, 0), (4, 129), (4, 258), (5, 0)]
        acc2 = [(6, 0), (6, 129), (6, 258), (5, 129)]
        sbank = Ring([0, 1, 7])
        groups = []
        for kt in ktiles:
            if groups and len(groups[-1]) < 8 and groups[-1][-1][0] + 1 == kt[0] and kt[0] % 8 != 0:
                groups[-1].append(kt)
            else:
                groups.append([kt])

        def acc_banks(accs):
            return sorted(set(b for b, _ in accs[:len(subs)]))

        def zero_acc(accs):
            for b in acc_banks(accs):
                cols = [c for bb, c in accs[:len(subs)] if bb == b]
                c0, c1 = min(cols), max(cols) + 129
                tr.op("dve", lambda: V.memset(psb[b][:, c0:c1], 0.0), writes=[R_ps[b]])

        def zero_branch(branch):
            if branch == "M":
                zero_acc(accM)
            else:
                zero_acc(acc1)
                zero_acc(acc2)

        zero_branch("M")
        zero_branch("D")
        deferred = []
        pend = None
        for h in range(DBG.get("att_heads", 8)):
            for branch in ("M", "D")[:DBG.get("att_br", 2)]:
                accl = [accM] if branch == "M" else [acc1, acc2]
                R_accw = [R_ps[b] for accs in accl for b in acc_banks(accs)]
                nstep = 0
                for grp in groups:
                    kt0 = grp[0][0]
                    nk = len(grp)
                    si = sring.next()
                    key0 = kt0 * 128
                    nkeys = sum(g[1] for g in grp)
                    pl = max(g[1] for g in grp)
                    if branch == "M":
                        tr.dma("sp", kn_s[si][:, 0:nkeys], KN[h, :, key0:key0 + nkeys], reads=[wres("KN")], writes=[R_kn[si]])
                        tr.dma("sp", kr_s[si][:, 0:nkeys], KR[:, key0:key0 + nkeys], reads=[wres("KR")], writes=[R_kr[si]])
                        tr.dma("sp", v_s[si][:pl, 0:nk, :], VM[h, 0:pl, kt0:kt0 + nk, :], reads=[wres("VM")], writes=[R_v[si]])
                    else:
                        tr.dma("sp", kn_s[si][:, 0:nkeys], DK[h, :, key0:key0 + nkeys], reads=[wres("DK")], writes=[R_kn[si]])
                        tr.dma("sp", v_s[si][:pl, 0:nk, :], DV[h, 0:pl, kt0:kt0 + nk, :], reads=[wres("DV")], writes=[R_v[si]])
                    for gi, (kt, kp, mode, j) in enumerate(grp):
                        q0 = 128 * j if mode == "diag" else 0
                        ko = gi * 128
                        pi = pring.next()
                        bias = obias if mode == "bias" else zcol
                        if branch == "M":
                            b0 = sbank.next()

                            def mms(b0=b0, ko=ko, kp=kp, q0=q0, si=si):
                                P_.matmul(psb[b0][:kp, q0:nt], kn_s[si][:, ko:ko + kp], QnT[:, h, q0:nt], start=True, stop=False)
                                return P_.matmul(psb[b0][:kp, q0:nt], kr_s[si][:, ko:ko + kp], QrT[:, h, q0:nt],
                                                 start=False, stop=True)
                            tr.op("pe", mms, reads=[R_kn[si], R_kr[si], R_QnT, R_QrT], writes=[R_ps[b0]])
                            tr.op("act", lambda: S_.activation(out=pt[pi][:kp, 0, q0:nt], in_=psb[b0][:kp, q0:nt], func=AF.Exp,
                                                               scale=SC_MLA, bias=bias[:kp]),
                                  reads=[R_ps[b0], R_cst], writes=[R_pt[pi]])
                            nmap = 1
                        else:
                            for qh in range(2):
                                qa, qb = max(q0, qh * 256), min(nt, (qh + 1) * 256)
                                if qa >= qb:
                                    continue
                                w = qb - qa
                                b0 = sbank.next()

                                def mms(b0=b0, ko=ko, kp=kp, qa=qa, qb=qb, w=w, si=si):
                                    P_.matmul(psb[b0][:kp, 0:w], kn_s[si][:, ko:ko + kp], Q1p[:, h, qa:qb], start=True, stop=True)
                                    return P_.matmul(psb[b0][:kp, 256:256 + w], kn_s[si][:, ko:ko + kp], Q2p[:, h, qa:qb],
                                                     start=True, stop=True, skip_group_check=True)
                                tr.op("pe", mms, reads=[R_kn[si], R_Q12T], writes=[R_ps[b0]])
                                src = psb[b0][:kp, :].rearrange("p (m q) -> p m q", q=256)[:, :, 0:w]
                                tr.op("act", lambda: S_.activation(out=pt[pi][:kp, :, qa:qb], in_=src, func=AF.Exp,
                                                                   scale=SC_DIFF, bias=bias[:kp]),
                                      reads=[R_ps[b0], R_cst], writes=[R_pt[pi]])
                            nmap = 2
                        if mode == "diag":
                            tr.op("dve", lambda: V.memset(pt[pi][64:128, 0:nmap, q0:q0 + 64], 0.0), writes=[R_pt[pi]])

                        def pv(pi=pi, kp=kp, j=j, mode=mode, gi=gi, accl=accl, si=si, R_accw=R_accw):
                            def f():
                                ins = None
                                for m, accs in enumerate(accl):
                                    for s, tp in subs:
                                        if mode == "diag" and s < j:
                                            continue
                                        bk, c0 = accs[s]
                                        ins = P_.matmul(psb[bk][:tp, c0:c0 + 129], pt[pi][:kp, m, s * 128:s * 128 + tp],
                                                        v_s[si][:kp, gi, :], start=False, stop=False, skip_group_check=True)
                                return ins
                            tr.op("pe", f, reads=[R_pt[pi], R_v[si]], writes=R_accw)
                        if pend is not None:
                            pend()
                        pend = pv
                        nstep += 1
                        while deferred and deferred[0][0] <= nstep:
                            deferred.pop(0)[1]()
                while deferred:
                    deferred.pop(0)[1]()
                oi = oring.next()

                def ev1(h=h, branch=branch, oi=oi):
                    for s, tp in subs:
                        r = rz[:, s * 8:s * 8 + 8]
                        if branch == "M":
                            bk, c0 = accM[s]
                            tr.op("dve", lambda: V.reciprocal(r[:tp, 0:1], psb[bk][:tp, c0 + 128:c0 + 129]), [R_ps[bk]], [R_rz])
                            tr.op("dve", lambda: V.tensor_scalar(otok[oi][:tp, s, :], psb[bk][:tp, c0:c0 + 128], r[:tp, 0:1], None, ALU.mult),
                                  [R_ps[bk], R_rz], [R_otok[oi]])
                        else:
                            bk1, c1 = acc1[s]
                            bk2, c2 = acc2[s]
                            tr.op("dve", lambda: V.reciprocal(r[:tp, 1:2], psb[bk1][:tp, c1 + 128:c1 + 129]), [R_ps[bk1]], [R_rz])
                            tr.op("dve", lambda: V.reciprocal(r[:tp, 2:3], psb[bk2][:tp, c2 + 128:c2 + 129]), [R_ps[bk2]], [R_rz])
                            tr.op("dve", lambda: V.tensor_tensor(r[:tp, 3:4], r[:tp, 2:3], nlam[:tp, :], ALU.mult), [R_rz, R_cst], [R_rz])
                            tr.op("dve", lambda: V.tensor_scalar(o1[:tp, :], psb[bk1][:tp, c1:c1 + 128], r[:tp, 1:2], None, ALU.mult),
                                  [R_ps[bk1], R_rz], [R_o1])
                            tr.op("dve", lambda: V.scalar_tensor_tensor(o2[:tp, s, :], psb[bk2][:tp, c2:c2 + 128], r[:tp, 3:4],
                                                                        o1[:tp, :], ALU.mult, ALU.add),
                                  [R_ps[bk2], R_rz, R_o1], [R_o2])
                    if branch == "M" and h < 7:
                        zero_branch(branch)

                def ev2(h=h, branch=branch, oi=oi):
                    if branch == "M":
                        return
                    for s, tp in subs:
                        tr.op("act", lambda: S_.activation(out=o1[:tp, :], in_=o2[:tp, s, :], func=AF.Square,
                                                           accum_out=rz2[:tp, s:s + 1]), [R_o2], [R_o1, R_rz2])
                    tp0 = subs[0][1]
                    ns_ = len(subs)
                    tr.op("act", lambda: S_.activation(out=rz2[:tp0, 0:ns_], in_=rz2[:tp0, 0:ns_], func=AF.Ln, scale=1.0 / 128, bias=EPS),
                          [R_rz2], [R_rz2])
                    tr.op("act", lambda: S_.activation(out=rz2[:tp0, 0:ns_], in_=rz2[:tp0, 0:ns_], func=AF.Exp, scale=-0.5),
                          [R_rz2], [R_rz2])
                    tr.op("dve", lambda: V.tensor_scalar(rz2[:tp0, 4:4 + ns_], rz2[:tp0, 0:ns_], 1.0 - LAMBDA_INIT, None, ALU.mult),
                          [R_rz2], [R_rz2])
                    for s, tp in subs:
                        tr.op("dve", lambda: V.scalar_tensor_tensor(otok[oi][:tp, s, :], o2[:tp, s, :], rz2[:tp, 4 + s:5 + s], gsub_b[:tp, :],
                                                                    ALU.mult, ALU.mult), [R_o2, R_rz2, R_cst], [R_otok[oi]])
                    if h < 7:
                        zero_branch(branch)

                def ev3(h=h, branch=branch, oi=oi):
                    b0 = sbank.next()
                    pbf = psb[b0][:].bitcast(BF16)

                    def tro():
                        ins = None
                        for s, tp in subs:
                            ins = P_.transpose(pbf[:, s * 128:s * 128 + tp], otok[oi][:tp, s, :], ident[:tp, :tp])
                        return ins
                    tr.op("pe", tro, reads=[R_otok[oi], R_ident], writes=[R_ps[b0]])
                    if branch == "M":
                        copy("dve", outAT[:, h, 0:nt], pbf[:, 0:nt], [R_ps[b0]], [R_oA])
                    else:
                        copy("dve", outBT[:, h, 0:nt], pbf[:, 0:nt], [R_ps[b0]], [R_oB])
                if not DBG.get("att_noevac", 0):
                    deferred.extend([[2, ev1], [4, ev2], [6, ev3]])
        if pend is not None:
            pend()
        while deferred:
            deferred.pop(0)[1]()

    def back(nt, x_src, y_dst, outAT, R_oA, outBT, R_oB):
        subs = [(s, min(128, nt - s * 128)) for s in range((nt + 127) // 128)]
        ns = len(subs)
        A = lambda n: Res(n, arena=True)
        set_psring(range(8))
        x2 = sb("x2", [128, ns, D], F32); R_x2 = [A(f"x2_{s}") for s in range(ns)]

        def norm_T(gcol, h_T, R_h):
            with Scope():
                norm_T_(gcol, h_T, R_h)

        def norm_T_(gcol, h_T, R_h):
            xb4 = sb("xb4b", [128, ns, D], BF16); R_xb4 = [A(f"xb4b_{s}") for s in range(ns)]
            for s, tp in subs:
                ss, R_ss = smcol()
                tr.op("act", lambda: S_.activation(out=xb4[:tp, s, :], in_=x2[:tp, s, :], func=AF.Square, accum_out=ss[:tp]),
                      reads=[R_x2[s]], writes=[R_xb4[s], R_ss])
                rstd_from_ss(ss, R_ss, D, tp)
                tr.op("dve", lambda: V.tensor_scalar(xb4[:tp, s, :], x2[:tp, s, :], ss[:tp], None, ALU.mult),
                      reads=[R_x2[s], R_ss], writes=[R_xb4[s]])
            for dc2 in range(8):
                bi = psring.next()
                pbf = psb[bi][:].bitcast(BF16)

                def trs(dc2=dc2, pbf=pbf):
                    ins = None
                    for u in range(2):
                        dc = dc2 * 2 + u
                        for s, tp in subs:
                            ins = P_.transpose(pbf[:, u * 512 + s * 128:u * 512 + s * 128 + tp],
                                               xb4[:tp, s, dc * 128:(dc + 1) * 128], ident[:tp, :tp])
                    return ins
                tr.op("pe", trs, reads=R_xb4 + [R_ident], writes=[R_ps[bi]])
                for u in range(2):
                    dc = dc2 * 2 + u
                    if dc2 % 2 == 0:
                        tr.op("act", lambda: S_.activation(out=h_T[:, dc, 0:nt], in_=pbf[:, u * 512:u * 512 + nt], func=AF.Copy,
                                                           scale=gcol[:, dc:dc + 1]), [R_ps[bi], R_cst], [R_h])
                    else:
                        tr.op("dve", lambda: V.tensor_scalar(h_T[:, dc, 0:nt], pbf[:, u * 512:u * 512 + nt],
                                                             gcol[:, dc:dc + 1], None, ALU.mult), [R_ps[bi], R_cst], [R_h])

        for s, tp in subs:
            tr.dma("sp", x2[:tp, s, :], x_src[s * 128:s * 128 + tp, :], writes=[R_x2[s]])
        hi = hring.next()
        h_T, R_h = hT[hi], R_hT[hi]
        norm_T(gmix_c, h_T, R_h)

        with Scope():
            merged = sb("merged", [128, 16, T], BF16); R_mg = A("merged")
            sg = [sb(f"sg{i}", [128, T], F32) for i in range(4)]; R_sg = [A(f"sg{i}") for i in range(4)]
            order = []
            for g in range(4):
                order += [(WG[g], f"WG{g}", 8192, "ga", g), (WG[4 + g], f"WG{4 + g}", 8192, "gb", g),
                          (WBA[g], f"WBA{g}", 4096, "ya", g), (WBB[g], f"WBB{g}", 4096, "yb", g)]
            ws_ = wstream([o[:3] for o in order])
            pend = {}
            for oi, (wd, wn, ne, typ, g) in enumerate(order):
                wv, R_wv = next(ws_)
                nk = 16 if typ in ("ga", "gb") else 8
                w3 = wv.rearrange("p (kc n) -> p kc n", n=512)
                src, R_src = (h_T, R_h) if typ in ("ga", "gb") else ((outAT, R_oA) if typ == "ya" else (outBT, R_oB))
                for cc in range(4):
                    c = g * 4 + cc
                    bi = psring.next()
                    bank = psb[bi]

                    def mm(bank=bank, w3=w3, cc=cc, nk=nk, src=src):
                        ins = None
                        for kc in range(nk):
                            ins = P_.matmul(bank[:, 0:nt], w3[:, kc, cc * 128:(cc + 1) * 128], src[:, kc, 0:nt],
                                            start=(kc == 0), stop=(kc == nk - 1))
                        return ins
                    tr.op("pe", mm, reads=[R_wv, R_src], writes=[R_ps[bi]])
                    if typ == "ga":
                        tr.op("act", lambda: S_.activation(out=sg[cc][:, 0:nt], in_=bank[:, 0:nt], func=AF.Sigmoid),
                              [R_ps[bi]], [R_sg[cc]])
                        pend[("ga", cc)] = None
                    elif typ == "gb":
                        tr.op("act", lambda: S_.activation(out=merged[:, c, 0:nt], in_=bank[:, 0:nt], func=AF.Sigmoid),
                              [R_ps[bi]], [R_mg])
                    elif typ == "ya":
                        tr.op("dve", lambda: V.tensor_tensor(sg[cc][:, 0:nt], sg[cc][:, 0:nt], bank[:, 0:nt], ALU.mult),
                              [R_ps[bi], R_sg[cc]], [R_sg[cc]])
                    else:
                        tr.op("dve", lambda: V.tensor_tensor(merged[:, c, 0:nt], merged[:, c, 0:nt], bank[:, 0:nt], ALU.mult),
                              [R_ps[bi], R_mg], [R_mg])
                        tr.op("dve", lambda: V.tensor_tensor(merged[:, c, 0:nt], merged[:, c, 0:nt], sg[cc][:, 0:nt], ALU.add),
                              [R_mg, R_sg[cc]], [R_mg])
            ws_ = wstream([(WO[nb_], f"WO{nb_}", 8192) for nb_ in range(4)])
            for nb in range(4):
                wv, R_wv = next(ws_)
                w3 = wv.rearrange("p (kc n) -> p kc n", n=512)
                for s, tp in subs:
                    bi = psring.next()
                    bank = psb[bi]

                    def mm(bank=bank, w3=w3, s=s, tp=tp):
                        ins = None
                        for kc in range(16):
                            ins = P_.matmul(bank[:tp, 0:512], merged[:, kc, s * 128:s * 128 + tp], w3[:, kc, :],
                                            start=(kc == 0), stop=(kc == 15))
                        return ins
                    tr.op("pe", mm, reads=[R_wv, R_mg], writes=[R_ps[bi]])
                    tr.op("dve", lambda: V.tensor_tensor(x2[:tp, s, nb * 512:(nb + 1) * 512], x2[:tp, s, nb * 512:(nb + 1) * 512],
                                                         bank[:tp, 0:512], ALU.add), [R_ps[bi], R_x2[s]], [R_x2[s]])
        hi = hring.next()
        h2T, R_h2 = hT[hi], R_hT[hi]
        norm_T(gffn_c, h2T, R_h2)
        with Scope():
            actT = [sb(f"actT{i}", [128, 4, T], BF16) for i in range(2)]; R_act = [A("actT0"), A("actT1")]
            sl = [sb(f"silu{i}", [128, T], F32) for i in range(2)]; R_sl = [A("silu0"), A("silu1")]
            order = []
            for g in range(11):
                order += [(WFG[g], f"WFG{g}", 8192, "g", g), (WFU[g], f"WFU{g}", 8192, "u", g), (WFO[g], f"WFO{g}", 8192, "o", g)]
            ws_ = wstream([o[:3] for o in order])
            for oi, (wd, wn, ne, typ, g) in enumerate(order):
                wv, R_wv = next(ws_)
                ab = g % 2
                if typ in ("g", "u"):
                    w3 = wv.rearrange("p (kc n) -> p kc n", n=512)
                    for cc in range(4):
                        bi = psring.next()
                        bank = psb[bi]

                        def mm(bank=bank, w3=w3, cc=cc):
                            ins = None
                            for kc in range(16):
                                ins = P_.matmul(bank[:, 0:nt], w3[:, kc, cc * 128:(cc + 1) * 128], h2T[:, kc, 0:nt],
                                                start=(kc == 0), stop=(kc == 15))
                            return ins
                        tr.op("pe", mm, reads=[R_wv, R_h2], writes=[R_ps[bi]])
                        if typ == "g":
                            tr.op("act", lambda: S_.activation(out=actT[ab][:, cc, 0:nt], in_=bank[:, 0:nt], func=AF.Silu),
                                  [R_ps[bi]], [R_act[ab]])
                        else:
                            tr.op("dve", lambda: V.tensor_tensor(actT[ab][:, cc, 0:nt], actT[ab][:, cc, 0:nt], bank[:, 0:nt],
                                                                 ALU.mult), [R_ps[bi], R_act[ab]], [R_act[ab]])
                else:
                    w3 = wv.rearrange("p (kc n) -> p kc n", n=2048)
                    for s, tp in subs:
                        for nb in range(4):
                            bi = psring.next()
                            bank = psb[bi]

                            def mm(bank=bank, w3=w3, s=s, tp=tp, nb=nb):
                                ins = None
                                for kc in range(4):
                                    ins = P_.matmul(bank[:tp, 0:512], actT[ab][:, kc, s * 128:s * 128 + tp],
                                                    w3[:, kc, nb * 512:(nb + 1) * 512], start=(kc == 0), stop=(kc == 3))
                                return ins
                            tr.op("pe", mm, reads=[R_wv, R_act[ab]], writes=[R_ps[bi]])
                            tr.op("dve", lambda: V.tensor_tensor(x2[:tp, s, nb * 512:(nb + 1) * 512],
                                                                 x2[:tp, s, nb * 512:(nb + 1) * 512], bank[:tp, 0:512], ALU.add),
                                  [R_ps[bi], R_x2[s]], [R_x2[s]])
        junk = [sb(f"fjunk{i}", [128, D], BF16) for i in range(2)]; R_junk = [A("fjunk0"), A("fjunk1")]
        for s, tp in subs:
            ss, R_ss = smcol()
            tr.op("act", lambda: S_.activation(out=junk[s % 2][:tp, :], in_=x2[:tp, s, :], func=AF.Square, accum_out=ss[:tp]),
                  reads=[R_x2[s]], writes=[R_junk[s % 2], R_ss])
            rstd_from_ss(ss, R_ss, D, tp)
            tr.op("dve", lambda: V.scalar_tensor_tensor(x2[:tp, s, :], x2[:tp, s, :], ss[:tp], gfin_b[:tp, :], ALU.mult, ALU.mult),
                  [R_x2[s], R_ss, R_cst], [R_x2[s]])
            tr.dma("pool", y_dst[s * 128:s * 128 + tp, :], x2[:tp, s, :], reads=[R_x2[s]])

    def own_tile(nt, ktiles, x_src, y_dst):
        with Scope():
            A = lambda n: Res(n, arena=True)
            outAT = sb("outAT", [128, 8, T], BF16); R_oA = A("outAT")
            outBT = sb("outBT", [128, 8, T], BF16); R_oB = A("outBT")
            with Scope():
                set_psring([0, 1, 2, 3])
                with (nc.named_scope("att") if DBG.get("scopes", 0) else contextlib.nullcontext()):
                    attention(nt, ktiles, outAT, R_oA, outBT, R_oB)
            if stage >= 4:
                with Scope():
                    with (nc.named_scope("back") if DBG.get("scopes", 0) else contextlib.nullcontext()):
                        back(nt, x_src, y_dst, outAT, R_oA, outBT, R_oB)

    emit_consts()
    emit_prep(1)
    import contextlib
    scope = (lambda n: nc.named_scope(n)) if DBG.get("scopes", 0) else (lambda n: contextlib.nullcontext())
    for i in range(npair):
        with scope(f"fo{i}"):
            front("other", T, 2 * i + 1, x_src=xk[i], rope_src=rope_k[i])
        if stage < 2:
            continue
        with scope(f"fw{i}"):
            front("own", T, 2 * i, x_src=xo[i], rope_src=rope_o[i],
                  outs={"ckv": ckv_o[i], "kr": kr_o[i], "dk": dk_o[i], "dv": dv_o[i]})
        if i == 0:
            emit_prep(2)
        if stage < 3:
            continue
        kts = [(kt, 128, "full", 0) for kt in range(8 * i)]
        kts += [(8 * i + j, 128, "diag", j) for j in range(4)]
        kts += [(8 * i + 4 + j, 128, "bias", 0) for j in range(4)]
        own_tile(T, kts, xo[i], yo[i])
    if do_sample:
        if npair == 0:
            emit_prep(2)
        front("cache", T, 16, cache_rows=0)
        front("cache", T, 17, cache_rows=512)
        front("own", 16, 18, x_src=xs, rope_src=rope_s,
              outs={"ckv": ckv_s, "kr": kr_s, "dk": dk_s, "dv": dv_s})
        kts = [(64 + j, 128, "full", 0) for j in range(8)] + [(72, 16, "full", 0)]
        own_tile(16, kts, xs, ys)
    tr.finish()
    while ctxs:
        ctxs.pop().__exit__(None, None, None)
    tr.close()
    return nc, tr


def _rope_table(pos):
    pos = pos.astype(np.float32)
    invM = (1.0 / (np.float32(10000.0) ** (np.arange(32, dtype=np.float32) / np.float32(32)))).astype(np.float32)
    invD = (1.0 / (np.float32(500000.0) ** (np.arange(8, dtype=np.float32) / np.float32(8)))).astype(np.float32)
    aM = (pos[:, None] * invM[None, :]).astype(np.float32)
    aD = (pos[:, None] * invD[None, :]).astype(np.float32)
    cM, sM = np.cos(aM).astype(np.float32), np.sin(aM).astype(np.float32)
    cD, sD = np.cos(aD).astype(np.float32), np.sin(aD).astype(np.float32)
    return np.concatenate([np.tile(cM, (1, 8)), np.tile(sM, (1, 8)), np.tile(cD, (1, 8)), np.tile(sD, (1, 8))],
                          axis=1).astype(np.float32)


_PROG = {}


def kernel(x_prompt, x_sample, cache_mla_ckv, cache_mla_krope, cache_diff_k, cache_diff_v,
           norm_mix, w_in, mla_q_norm, mla_w_uq, mla_kv_norm, mla_w_ukv,
           diff_lq1, diff_lk1, diff_lq2, diff_lk2, diff_subln,
           w_branch_a, w_branch_b, w_out, norm_ffn, w_ffn_in, w_ffn_out, norm_final,
           _npair=8, _do_sample=True, _stage=9):
    f = lambda a: np.ascontiguousarray(np.asarray(a, dtype=np.float32))
    key = (_npair, _do_sample, _stage)
    if key not in _PROG:
        _PROG[key] = build_program(_npair, _do_sample, _stage)[0]
    nc = _PROG[key]
    x_prompt = f(x_prompt); x_sample = f(x_sample)
    shared = {
        "w_in": f(w_in)[0], "w_uq": f(mla_w_uq)[0], "w_ukv": f(mla_w_ukv)[0], "w_ba": f(w_branch_a)[0],
        "w_bb": f(w_branch_b)[0], "w_out": f(w_out)[0], "w_fi": f(w_ffn_in)[0], "w_fo": f(w_ffn_out)[0],
        "g_mix": f(norm_mix)[0], "g_q": f(mla_q_norm)[0], "g_kv": f(mla_kv_norm)[0],
        "lq1": f(diff_lq1)[0], "lk1": f(diff_lk1)[0], "lq2": f(diff_lq2)[0], "lk2": f(diff_lk2)[0],
        "g_sub": f(diff_subln)[0], "g_ffn": f(norm_ffn)[0], "g_fin": f(norm_final),
        "ident": np.eye(128, dtype=np.float32).astype(ml_dtypes.bfloat16),
    }
    rope_all = _rope_table(np.arange(8192))
    rope_s = _rope_table(1024 + np.arange(16))
    in_maps = []
    for c in range(8):
        b, e = c // 2, c % 2
        xt = x_prompt[b].reshape(16, T, D)
        rt = rope_all.reshape(16, T, RW)
        m = dict(shared)
        m["xo"] = np.ascontiguousarray(xt[e::2]); m["xk"] = np.ascontiguousarray(xt[1 - e::2])
        m["rope_o"] = np.ascontiguousarray(rt[e::2]); m["rope_k"] = np.ascontiguousarray(rt[1 - e::2])
        m["xs"] = x_sample[c]; m["rope_s"] = rope_s
        m["c_ckv"] = f(cache_mla_ckv)[0, c]; m["c_kr"] = f(cache_mla_krope)[0, c]
        m["c_dk"] = f(cache_diff_k)[0, c].reshape(1024, 1024); m["c_dv"] = f(cache_diff_v)[0, c].reshape(1024, 1024)
        m["obias"] = np.full((128, 1), 0.0 if e == 1 else NEG, np.float32)
        in_maps.append(m)
    res = run_bass_kernel_spmd(nc, in_maps, core_ids=list(range(8)))
    R = res.results
    y_p = np.zeros((4, 8192, D), np.float32); ckv_p = np.zeros((1, 4, 8192, 256), np.float32)
    kr_p = np.zeros((1, 4, 8192, 64), np.float32); dk_p = np.zeros((1, 4, 8192, 8, 128), np.float32)
    dv_p = np.zeros((1, 4, 8192, 8, 128), np.float32)
    y_s = np.zeros((8, 16, D), np.float32); ckv_sm = np.zeros((1, 8, 16, 256), np.float32)
    kr_sm = np.zeros((1, 8, 16, 64), np.float32); dk_sm = np.zeros((1, 8, 16, 8, 128), np.float32)
    dv_sm = np.zeros((1, 8, 16, 8, 128), np.float32)
    for c in range(8):
        b, e = c // 2, c % 2
        r = R[c]
        y_p[b].reshape(16, T, D)[e::2] = r["yo"]
        ckv_p[0, b].reshape(16, T, 256)[e::2] = r["ckv_o"]
        kr_p[0, b].reshape(16, T, 64)[e::2] = r["kr_o"]
        dk_p[0, b].reshape(16, T, 1024)[e::2] = r["dk_o"]
        dv_p[0, b].reshape(16, T, 1024)[e::2] = r["dv_o"]
        y_s[c] = r["ys"]; ckv_sm[0, c] = r["ckv_s"]; kr_sm[0, c] = r["kr_s"]
        dk_sm[0, c] = r["dk_s"].reshape(16, 8, 128); dv_sm[0, c] = r["dv_s"].reshape(16, 8, 128)
    return (y_p, y_s, ckv_p, kr_p, dk_p, dv_p, ckv_sm, kr_sm, dk_sm, dv_sm)
```

```python
import contextlib
import numpy as np
import ml_dtypes
import concourse.bass as bass
import concourse.mybir as mybir
from concourse.bass_utils import run_bass_kernel_spmd

F32 = mybir.dt.float32
BF16 = mybir.dt.bfloat16
AF = mybir.ActivationFunctionType
DBG = {"cut": 9}
ALU = mybir.AluOpType

D = 2048
T = 512
DFF = 5632
EPS = 1e-6
NEG = -30000.0
LAMBDA_INIT = 0.8 - 0.6 * 1.0
SC_MLA = float((128 + 64) ** -0.5)
SC_DIFF = float(64 ** -0.5)
NBLK = 19
NKT = NBLK * 4
RW = 640


class Res:
    __slots__ = ("name", "w", "r", "arena", "multi", "ws", "excl")

    def __init__(self, name, arena=False, multi=False, excl=False):
        self.name = name
        self.w = None
        self.r = {}
        self.arena = arena
        self.multi = multi
        self.ws = {}
        self.excl = excl


class Trk:
    SEM_LIMIT = 30000

    def __init__(self, nc, n_dma_sems=14):
        self.nc = nc
        self.engs = {"pe": nc.tensor, "act": nc.scalar, "dve": nc.vector, "pool": nc.gpsimd, "sp": nc.sync}
        self.sems = {}
        self.owner = {}
        self.cur = {}
        self.gen = {}
        self.waited = {e: {} for e in self.engs}
        self.pending = {e: {} for e in self.engs}
        self.arena_evs = {}
        self.nwaits = 0
        self.nops = 0
        self._stack = []
        for e in ("pe", "act", "dve", "pool"):
            self.gen[e] = 0
            self._new_sem(e)
        self.dq = {}
        for q in ("sp", "pool"):
            lst = []
            for i in range(n_dma_sems):
                k = f"d_{q}{i}"
                self.sems[k] = self._alloc(k)
                self.owner[k] = "dma"
                lst.append([k, 0])
            self.dq[q] = [lst, 0]

    def _alloc(self, name):
        cm = self.nc.semaphore(name)
        h = cm.__enter__()
        self._stack.append(cm)
        return h

    def _new_sem(self, e):
        k = f"s_{e}{self.gen[e]}"
        self.gen[e] += 1
        self.sems[k] = self._alloc(k)
        self.owner[k] = e
        self.cur[e] = [k, 0]

    def close(self):
        for cm in reversed(self._stack):
            cm.__exit__(None, None, None)

    def _need(self, eng, evs):
        for (k, v) in evs:
            if self.waited[eng].get(k, 0) < v:
                self.engs[eng].wait_ge(self.sems[k], v)
                self.waited[eng][k] = v
                self.nwaits += 1

    def _deps(self, eng, reads, writes):
        evs = []
        arena = False
        for r in reads:
            arena |= r.arena
            if r.multi:
                evs.extend(r.ws.items())
            elif r.w is not None and not (eng == "pe" and self.owner[r.w[0]] == "pe"):
                evs.append(r.w)
            if r.excl:
                for k, v in r.r.items():
                    if self.owner[k] != eng:
                        evs.append((k, v))
        for w in writes:
            arena |= w.arena
            if w.multi:
                pass
            elif w.w is not None and not (eng == "pe" and self.owner[w.w[0]] == "pe"):
                evs.append(w.w)
            for k, v in w.r.items():
                if not (eng == "pe" and self.owner[k] == "pe"):
                    evs.append((k, v))
        if arena and self.pending[eng]:
            evs.extend(self.pending[eng].items())
            self.pending[eng] = {}
        self._need(eng, evs)

    def _commit(self, ev, reads, writes):
        k, v = ev
        for r in reads:
            if r.r.get(k, 0) < v:
                r.r[k] = v
            if r.arena:
                self.arena_evs[k] = max(self.arena_evs.get(k, 0), v)
        for w in writes:
            if w.multi:
                w.ws[k] = max(w.ws.get(k, 0), v)
            else:
                w.w = ev
                w.r = {}
            if w.arena:
                self.arena_evs[k] = max(self.arena_evs.get(k, 0), v)

    def barrier(self):
        for e in self.engs:
            p = self.pending[e]
            for k, v in self.arena_evs.items():
                if p.get(k, 0) < v:
                    p[k] = v
        self.arena_evs = {}

    def op(self, eng, fn, reads=(), writes=()):
        self._deps(eng, reads, writes)
        c = self.cur[eng]
        if c[1] >= self.SEM_LIMIT:
            self._new_sem(eng)
            c = self.cur[eng]
        ins = fn()
        c[1] += 1
        ins.then_inc(self.sems[c[0]], 1)
        self.nops += 1
        self._commit((c[0], c[1]), reads, writes)

    def dma(self, q, out, in_, reads=(), writes=()):
        self._deps(q, reads, writes)
        lst, idx = self.dq[q]
        slot = lst[idx]
        self.dq[q][1] = (idx + 1) % len(lst)
        if slot[1] > 0:
            self._need(q, [(slot[0], slot[1])])
        slot[1] += 16
        self.engs[q].dma_start(out=out, in_=in_).then_inc(self.sems[slot[0]], 16)
        self.nops += 1
        self._commit((slot[0], slot[1]), reads, writes)

    def finish(self):
        evs = []
        for q in self.dq:
            for s in self.dq[q][0]:
                if s[1] > 0:
                    evs.append((s[0], s[1]))
        for e in ("pe", "act", "dve", "pool"):
            if self.cur[e][1] > 0:
                evs.append((self.cur[e][0], self.cur[e][1]))
        self._need("pool", evs)
        self._need("sp", evs)


class Ring:
    def __init__(self, items):
        self.items = items
        self.i = 0

    def next(self):
        it = self.items[self.i]
        self.i = (self.i + 1) % len(self.items)
        return it


def build_program(npair=8, do_sample=True, stage=9):
    nc = bass.Bass("TRN2", target_bir_lowering=False)
    tr = Trk(nc)
    V, S_, P_ = nc.vector, nc.scalar, nc.tensor

    def din(name, shape, dt=F32):
        return nc.dram_tensor(name, list(shape), dt, kind="ExternalInput").ap()

    def dout(name, shape, dt=F32):
        return nc.dram_tensor(name, list(shape), dt, kind="ExternalOutput").ap()

    def dscr(name, shape, dt=BF16):
        return nc.dram_tensor(name, list(shape), dt, kind="Internal").ap()

    xo = din("xo", [8, T, D]); xk = din("xk", [8, T, D]); xs = din("xs", [16, D])
    rope_o = din("rope_o", [8, T, RW]); rope_k = din("rope_k", [8, T, RW]); rope_s = din("rope_s", [16, RW])
    c_ckv = din("c_ckv", [1024, 256]); c_kr = din("c_kr", [1024, 64])
    c_dk = din("c_dk", [1024, 1024]); c_dv = din("c_dv", [1024, 1024])
    w_in = din("w_in", [D, 8000]); w_uq = din("w_uq", [512, 1536]); w_ukv = din("w_ukv", [256, 2048])
    w_ba = din("w_ba", [1024, D]); w_bb = din("w_bb", [1024, D]); w_out = din("w_out", [D, D])
    w_fi = din("w_fi", [D, 2 * DFF]); w_fo = din("w_fo", [DFF, D])
    g_mix = din("g_mix", [D]); g_q = din("g_q", [512]); g_kv = din("g_kv", [256])
    lq1 = din("lq1", [64]); lk1 = din("lk1", [64]); lq2 = din("lq2", [64]); lk2 = din("lk2", [64])
    g_sub = din("g_sub", [128]); g_ffn = din("g_ffn", [D]); g_fin = din("g_fin", [D])
    ident_d = din("ident", [128, 128], BF16); obias_d = din("obias", [128, 1])

    yo = dout("yo", [8, T, D]); ckv_o = dout("ckv_o", [8, T, 256]); kr_o = dout("kr_o", [8, T, 64])
    dk_o = dout("dk_o", [8, T, 1024]); dv_o = dout("dv_o", [8, T, 1024])
    ys = dout("ys", [16, D]); ckv_s = dout("ckv_s", [16, 256]); kr_s = dout("kr_s", [16, 64])
    dk_s = dout("dk_s", [16, 1024]); dv_s = dout("dv_s", [16, 1024])

    WA = [dscr(f"WA{i}", [128, 16, 512]) for i in range(8)]
    WG = [dscr(f"WG{i}", [128, 16, 512]) for i in range(8)]
    WUQ = dscr("WUQ", [128, 4, 1536])
    WUKV = dscr("WUKV", [128, 2, 2048])
    WBA = [dscr(f"WBA{i}", [128, 8, 512]) for i in range(4)]
    WBB = [dscr(f"WBB{i}", [128, 8, 512]) for i in range(4)]
    WO = [dscr(f"WO{i}", [128, 16, 512]) for i in range(4)]
    WFG = [dscr(f"WFG{i}", [128, 16, 512]) for i in range(11)]
    WFU = [dscr(f"WFU{i}", [128, 16, 512]) for i in range(11)]
    WFO = [dscr(f"WFO{i}", [128, 4, 2048]) for i in range(11)]
    KN = dscr("KN", [8, 128, NBLK * T]); KR = dscr("KR", [64, NBLK * T]); DK = dscr("DK", [8, 128, NBLK * T])
    VM = dscr("VM", [8, 128, NKT, 129]); DV = dscr("DV", [8, 128, NKT, 129])
    R_w = {}

    def wres(name):
        if name not in R_w:
            R_w[name] = Res(name, multi=name in ("KN", "KR", "DK", "VM", "DV", "WUQ", "WUKV"))
        return R_w[name]

    def prep(dst, src, name):
        tr.dma("pool", dst, src, writes=[wres(name)])

    def kc_view(w, c0, n):
        return w[:, c0:c0 + n].rearrange("(kc p) n -> p kc n", p=128)

    def emit_prep(part):
        if part == 2:
            return emit_prep2()
        cols = [(0, 512), (512, 320), (832, 512), (1344, 512), (1856, 512), (2368, 512), (2880, 512), (3392, 512)]
        for i, (c0, n) in enumerate(cols):
            prep(WA[i], kc_view(w_in, c0, 512), f"WA{i}")
        for kc in range(4):
            src = w_uq[kc * 128:(kc + 1) * 128, :].rearrange("p (h d) -> p h d", d=192)
            prep(WUQ[:, kc, 0:1024].rearrange("p (h d) -> p h d", d=128), src[:, :, 0:128], "WUQ")
            prep(WUQ[:, kc, 1024:1536].rearrange("p (h d) -> p h d", d=64), src[:, :, 128:192], "WUQ")
        for kc in range(2):
            src = w_ukv[kc * 128:(kc + 1) * 128, :].rearrange("p (h d) -> p h d", d=256)
            prep(WUKV[:, kc, 0:1024].rearrange("p (h d) -> p h d", d=128), src[:, :, 0:128], "WUKV")
            prep(WUKV[:, kc, 1024:2048].rearrange("p (h d) -> p h d", d=128), src[:, :, 128:256], "WUKV")
    def emit_prep2():
        for i in range(8):
            prep(WG[i], kc_view(w_in, 3904 + i * 512, 512), f"WG{i}")
        for i in range(4):
            prep(WBA[i], kc_view(w_ba, i * 512, 512), f"WBA{i}")
            prep(WBB[i], kc_view(w_bb, i * 512, 512), f"WBB{i}")
        for i in range(4):
            prep(WO[i], kc_view(w_out, i * 512, 512), f"WO{i}")
        for i in range(11):
            prep(WFG[i], kc_view(w_fi, i * 512, 512), f"WFG{i}")
            prep(WFU[i], kc_view(w_fi, DFF + i * 512, 512), f"WFU{i}")
            prep(WFO[i], w_fo[i * 512:(i + 1) * 512, :].rearrange("(kc p) n -> p kc n", p=128), f"WFO{i}")

    ctxs = []

    uid = [0]

    def sb(name, shape, dt):
        uid[0] += 1
        cm = nc.sbuf_tensor(f"sb{uid[0]}_{name}", list(shape), dt)
        t = cm.__enter__()
        ctxs.append(cm)
        return t

    class Scope:
        def __enter__(self):
            self.mark = len(ctxs)
            tr.barrier()
            return self

        def __exit__(self, *a):
            while len(ctxs) > self.mark:
                ctxs.pop().__exit__(None, None, None)
            tr.barrier()
            return False

    ident = sb("ident", [128, 128], BF16); R_ident = Res("ident")
    zcol = sb("zcol", [128, 1], F32); obias = sb("obias", [128, 1], F32); R_cst = Res("cst")
    gmix_c = sb("gmix_c", [128, 16], F32); gffn_c = sb("gffn_c", [128, 16], F32); gq_c = sb("gq_c", [128, 4], F32)
    gkv_b = sb("gkv_b", [128, 256], F32); gsub_b = sb("gsub_b", [128, 128], F32); gfin_b = sb("gfin_b", [128, D], F32)
    lam4 = sb("lam4", [128, 4, 64], F32); lamt = sb("lamt", [128, 4], F32); nlam = sb("nlam", [128, 1], F32)
    ones_bf = sb("ones_bf", [128, 8], BF16)
    hT = [sb(f"hT{i}", [128, 16, T], BF16) for i in range(2)]
    R_hT = [Res(f"hT{i}") for i in range(2)]
    hring = Ring([0, 1])
    last_h = [0]
    QnT = sb("QnT", [128, 8, T], BF16); QrT = sb("QrT", [64, 8, T], BF16); Q1p = sb("Q1p", [128, 8, T], BF16); Q2p = sb("Q2p", [128, 8, T], BF16)
    R_QnT, R_QrT, R_Q12T = Res("QnT"), Res("QrT"), Res("Q12T")
    NW = 3
    wslot = [sb(f"wslot{i}", [128, 8192], BF16) for i in range(NW)]
    R_wslot = [Res(f"wslot{i}") for i in range(NW)]
    wring = Ring(list(range(NW)))
    small = sb("small", [128, 64], F32)
    R_small = [Res(f"small{i}") for i in range(64)]
    smring = Ring(list(range(64)))

    psb = []
    for i in range(8):
        cm = nc.psum_tensor(f"ps{i}", [128, 512], F32)
        psb.append(cm.__enter__())
        ctxs.append(cm)
    R_ps = [Res(f"ps{i}", excl=True) for i in range(8)]
    psring = Ring(list(range(8)))

    def set_psring(banks):
        psring.items = list(banks)
        psring.i = 0

    flip = [0]

    def evac_eng():
        flip[0] ^= 1
        return "act" if flip[0] else "dve"

    def copy(eng, out, in_, reads, writes):
        if eng == "act":
            tr.op("act", lambda: S_.activation(out=out, in_=in_, func=AF.Copy), reads, writes)
        else:
            tr.op("dve", lambda: V.tensor_copy(out, in_), reads, writes)

    def wload(dram_ap, rname, nelem):
        i = wring.next()
        view = wslot[i][:, 0:nelem]
        src = dram_ap if len(dram_ap.shape) == 2 else dram_ap.rearrange("p a b -> p (a b)")
        tr.dma("sp", view, src, reads=[wres(rname)], writes=[R_wslot[i]])
        return view, R_wslot[i]

    def wstream(items, depth=2):
        q = []
        it = iter(items)
        for _ in range(depth):
            x = next(it, None)
            if x is not None:
                q.append(wload(*x))
        while q:
            cur = q.pop(0)
            x = next(it, None)
            if x is not None:
                q.append(wload(*x))
            yield cur

    def smcol():
        i = smring.next()
        return small[:, i:i + 1], R_small[i]

    def rstd_from_ss(col, rcol, n, tp):
        tr.op("act", lambda: S_.activation(out=col[:tp], in_=col[:tp], func=AF.Ln, scale=1.0 / n, bias=EPS),
              reads=[rcol], writes=[rcol])
        tr.op("act", lambda: S_.activation(out=col[:tp], in_=col[:tp], func=AF.Exp, scale=-0.5),
              reads=[rcol], writes=[rcol])

    def emit_consts():
        tr.dma("sp", ident[:], ident_d, writes=[R_ident])
        tr.dma("sp", obias[:], obias_d, writes=[R_cst])
        with nc.allow_non_contiguous_dma(reason="tiny one-off per-partition gain columns"):
            for j in range(4):
                tr.dma("sp", gmix_c[:, 4 * j:4 * j + 4], g_mix[512 * j:512 * j + 512].rearrange("(c p) -> p c", p=128), writes=[R_cst])
                tr.dma("sp", gffn_c[:, 4 * j:4 * j + 4], g_ffn[512 * j:512 * j + 512].rearrange("(c p) -> p c", p=128), writes=[R_cst])
            tr.dma("sp", gq_c[:], g_q.rearrange("(c p) -> p c", p=128), writes=[R_cst])
        tr.dma("sp", gkv_b[:], g_kv.partition_broadcast(128), writes=[R_cst])
        tr.dma("sp", gsub_b[:], g_sub.partition_broadcast(128), writes=[R_cst])
        tr.dma("sp", gfin_b[:], g_fin.partition_broadcast(128), writes=[R_cst])
        for j, v in enumerate((lq1, lk1, lq2, lk2)):
            tr.dma("sp", lam4[:, j, :], v.partition_broadcast(128), writes=[R_cst])
        tr.op("dve", lambda: V.memset(zcol[:], 0.0), writes=[R_cst])
        tr.op("pool", lambda: nc.gpsimd.memset(Q1p[64:128, :, :], 0.0), writes=[R_Q12T])
        tr.op("pool", lambda: nc.gpsimd.memset(Q2p[0:64, :, :], 0.0), writes=[R_Q12T])
        tr.op("dve", lambda: V.memset(ones_bf[:], 1.0), writes=[R_cst])
        tr.op("dve", lambda: V.tensor_tensor(lam4[:, 0, :], lam4[:, 0, :], lam4[:, 1, :], ALU.mult), [R_cst], [R_cst])
        tr.op("dve", lambda: V.tensor_tensor(lam4[:, 2, :], lam4[:, 2, :], lam4[:, 3, :], ALU.mult), [R_cst], [R_cst])
        tr.op("dve", lambda: V.reduce_sum(lamt[:, 0:1], lam4[:, 0, :], mybir.AxisListType.X), [R_cst], [R_cst])
        tr.op("dve", lambda: V.reduce_sum(lamt[:, 1:2], lam4[:, 2, :], mybir.AxisListType.X), [R_cst], [R_cst])
        tr.op("act", lambda: S_.activation(out=lamt[:, 2:4], in_=lamt[:, 0:2], func=AF.Exp), [R_cst], [R_cst])
        tr.op("dve", lambda: V.scalar_tensor_tensor(nlam[:], lamt[:, 3:4], -LAMBDA_INIT, lamt[:, 2:3],
                                                    ALU.add, ALU.subtract), [R_cst], [R_cst])

    def front(kind, nt, blk, x_src=None, rope_src=None, outs=None, cache_rows=None):
        subs = [(s, min(128, nt - s * 128)) for s in range((nt + 127) // 128)]
        ns = len(subs)
        own = kind == "own"
        A = lambda n: Res(n, arena=True)
        g3 = lambda ap, w: ap.rearrange("p (g w) -> p g w", w=w)
        with Scope():
            set_psring(range(8))
            if kind != "cache":
                hi = hring.next()
                last_h[0] = hi
                h_T, R_h = hT[hi], R_hT[hi]
                with Scope():
                    xt = [sb(f"xt{i}", [128, D], F32) for i in range(2)]; R_xt = [A("xt0"), A("xt1")]
                    xb4 = sb("xb4", [128, ns, D], BF16); R_xb4 = [A(f"xb4_{s}") for s in range(ns)]
                    for s, tp in subs:
                        b = s % 2
                        tr.dma("sp", xt[b][:tp, :], x_src[s * 128:s * 128 + tp, :], writes=[R_xt[b]])
                        ss, R_ss = smcol()
                        tr.op("act", lambda: S_.activation(out=xb4[:tp, s, :], in_=xt[b][:tp, :], func=AF.Square,
                                                           accum_out=ss[:tp]),
                              reads=[R_xt[b]], writes=[R_xb4[s], R_ss])
                        rstd_from_ss(ss, R_ss, D, tp)
                        tr.op("act", lambda: S_.activation(out=xb4[:tp, s, :], in_=xt[b][:tp, :], func=AF.Copy,
                                                           scale=ss[:tp]),
                              reads=[R_xt[b], R_ss], writes=[R_xb4[s]])
                    for dc2 in range(8):
                        bi = psring.next()
                        pbf = psb[bi][:].bitcast(BF16)

                        def trs(dc2=dc2, pbf=pbf):
                            ins = None
                            for u in range(2):
                                dc = dc2 * 2 + u
                                for s, tp in subs:
                                    ins = P_.transpose(pbf[:, u * 512 + s * 128:u * 512 + s * 128 + tp],
                                                       xb4[:tp, s, dc * 128:(dc + 1) * 128], ident[:tp, :tp])
                            return ins
                        tr.op("pe", trs, reads=R_xb4 + [R_ident], writes=[R_ps[bi]])
                        for u in range(2):
                            dc = dc2 * 2 + u
                            if dc2 % 2 == 0:
                                tr.op("act", lambda: S_.activation(out=h_T[:, dc, 0:nt], in_=pbf[:, u * 512:u * 512 + nt],
                                                                   func=AF.Copy, scale=gmix_c[:, dc:dc + 1]),
                                      reads=[R_ps[bi], R_cst], writes=[R_h])
                            else:
                                tr.op("dve", lambda: V.tensor_scalar(h_T[:, dc, 0:nt], pbf[:, u * 512:u * 512 + nt],
                                                                     gmix_c[:, dc:dc + 1], None, ALU.mult),
                                      reads=[R_ps[bi], R_cst], writes=[R_h])

            if DBG["cut"] <= 1:
                return
            ckvb = sb("ckvb", [128, ns, 256], BF16); R_ckvb = A("ckvb")
            krb = sb("krb", [128, ns, 64], BF16); R_krb = A("krb")
            dkb = sb("dkb", [128, ns, 1024], BF16); R_dkb = A("dkb")
            vst = [sb(f"vst{i}", [128, 8, 129], BF16) for i in range(2)]; R_vst = [A("vst0"), A("vst1")]
            dst = [sb(f"dst{i}", [128, 4, 129], BF16) for i in range(2)]; R_dst = [A("dst0"), A("dst1")]
            kst = [sb(f"kst{i}", [128, T], BF16) for i in range(4)]; R_kst = [A(f"kst{i}") for i in range(4)]
            kring = Ring([0, 1, 2, 3])
            dring = Ring([0, 1])
            ckvT = sb("ckvT", [128, 2, T], BF16); R_ckvT = A("ckvT")
            for i in range(2):
                if DBG.get("nomemset", 0):
                    break
                tr.op("dve", (lambda i=i: V.memset(vst[i][:, :, 128:129], 1.0)), writes=[R_vst[i]])
                tr.op("dve", (lambda i=i: V.memset(dst[i][:, :, 128:129], 1.0)), writes=[R_dst[i]])
            if kind != "cache":
                rt = sb("rt", [128, ns, RW], F32); R_rt = A("rt")
                o_ckv = [sb(f"o_ckv{i}", [128, 256], F32) for i in range(2)]; R_ockv = [A("ockv0"), A("ockv1")]
                o_kr = [sb(f"o_kr{i}", [128, 64], F32) for i in range(2)]; R_okr = [A("okr0"), A("okr1")]
                o_blk = [sb(f"o_blk{i}", [128, 512], F32) for i in range(3)]; R_oblk = [A(f"oblk{i}") for i in range(3)]
                oring = Ring([0, 1, 2])
                tmp = sb("ropetmp", [128, 4, 256], F32); R_tmp = A("ropetmp")
                if own:
                    qlb = sb("qlb", [128, ns, 512], BF16); R_qlb = A("qlb")
                    dqb = sb("dqb", [128, ns, 1024], BF16); R_dqb = A("dqb")
                    qrb = sb("qrb", [128, ns, 512], BF16); R_qrb = A("qrb")
                    qlatT = sb("qlatT", [128, 4, T], BF16); R_qlatT = A("qlatT")
                for s, tp in subs:
                    if DBG.get("nort", 0):
                        break
                    tr.dma("sp", rt[:tp, s, :], rope_src[s * 128:s * 128 + tp, :], writes=[R_rt])

                def rope(src3, dst3, C3, S3, tp, G, hw, rd, wr):
                    n = G * hw
                    t = [tmp[:tp, j, 0:n].rearrange("p (g w) -> p g w", w=hw) for j in range(4)]
                    x1, x2 = src3[:, :, 0:hw], src3[:, :, hw:2 * hw]
                    rm = DBG.get("ropemask", 3)
                    if rm & 1:
                        tr.op("dve", lambda: V.tensor_tensor(t[0], x1, C3, ALU.mult), rd, [R_tmp])
                        tr.op("dve", lambda: V.tensor_tensor(t[1], x2, S3, ALU.mult), rd, [R_tmp])
                        tr.op("dve", lambda: V.tensor_tensor(t[2], x2, C3, ALU.mult), rd, [R_tmp])
                        tr.op("dve", lambda: V.tensor_tensor(t[3], x1, S3, ALU.mult), rd, [R_tmp])
                    if rm & 2:
                        tr.op("dve", lambda: V.tensor_tensor(dst3[:, :, 0:hw], t[0], t[1], ALU.subtract), [R_tmp], wr)
                        tr.op("dve", lambda: V.tensor_tensor(dst3[:, :, hw:2 * hw], t[2], t[3], ALU.add), [R_tmp], wr)

                blocks = [0, 1, 2, 3, 4, 5, 6, 7] if own else [1, 4, 5, 6, 7]
                blocks = blocks[:DBG.get("nblk", 99)]
                ncols = {0: 512, 1: 320}
                ws_ = wstream([(WA[nb_], f"WA{nb_}", 8192) for nb_ in blocks])
                for bix, nb in enumerate(blocks):
                    wv, R_wv = next(ws_)
                    w3 = wv.rearrange("p (kc n) -> p kc n", n=512)
                    ncl = ncols.get(nb, 512)
                    if DBG.get("ncl512", 0):
                        ncl = 512
                    for s, tp in subs:
                        b = s % 2
                        r0 = s * 128
                        bi = psring.next()
                        bank = psb[bi]

                        def mm(s=s, tp=tp, bank=bank, w3=w3, ncl=ncl):
                            ins = None
                            for kc in range(16):
                                ins = P_.matmul(bank[:tp, 0:ncl], h_T[:, kc, s * 128:s * 128 + tp], w3[:, kc, 0:ncl],
                                                start=(kc == 0), stop=(kc == 15))
                            return ins
                        if not DBG.get("nomm", 0):
                            tr.op("pe", mm, reads=[R_h, R_wv], writes=[R_ps[bi]])
                        Rb = R_ps[bi]
                        if DBG.get("noev", 0):
                            continue
                        if nb == 0:
                            ss, R_ss = smcol()
                            tr.op("act", lambda: S_.activation(out=qlb[:tp, s, :], in_=bank[:tp, 0:512], func=AF.Square,
                                                               accum_out=ss[:tp]), [Rb], [R_qlb, R_ss])
                            rstd_from_ss(ss, R_ss, 512, tp)
                            tr.op("act", lambda: S_.activation(out=qlb[:tp, s, :], in_=bank[:tp, 0:512], func=AF.Copy,
                                                               scale=ss[:tp]), [Rb, R_ss], [R_qlb])
                        elif nb == 1:
                            ss, R_ss = smcol()
                            tr.op("act", lambda: S_.activation(out=o_ckv[b][:tp, :], in_=bank[:tp, 0:256], func=AF.Square,
                                                               accum_out=ss[:tp]), [Rb], [R_ockv[b], R_ss])
                            rstd_from_ss(ss, R_ss, 256, tp)
                            tr.op("dve", lambda: V.scalar_tensor_tensor(o_ckv[b][:tp, :], bank[:tp, 0:256], ss[:tp],
                                                                        gkv_b[:tp, :], ALU.mult, ALU.mult),
                                  [Rb, R_ss, R_cst], [R_ockv[b]])
                            tr.op("act", lambda: S_.activation(out=ckvb[:tp, s, :], in_=o_ckv[b][:tp, :], func=AF.Copy),
                                  [R_ockv[b]], [R_ckvb])
                            rope(g3(bank[:tp, 256:320], 64), g3(o_kr[b][:tp, :], 64),
                                 g3(rt[:tp, s, 0:32], 32), g3(rt[:tp, s, 256:288], 32), tp, 1, 32,
                                 [Rb, R_rt], [R_okr[b]])
                            tr.op("act", lambda: S_.activation(out=krb[:tp, s, :], in_=o_kr[b][:tp, :], func=AF.Copy),
                                  [R_okr[b]], [R_krb])
                            if outs is not None:
                                tr.dma("pool", outs["ckv"][r0:r0 + tp, :], o_ckv[b][:tp, :], reads=[R_ockv[b]])
                                tr.dma("pool", outs["kr"][r0:r0 + tp, :], o_kr[b][:tp, :], reads=[R_okr[b]])
                        elif nb in (2, 3):
                            c0 = (nb - 2) * 512
                            tr.op("act", lambda: S_.activation(out=dqb[:tp, s, c0:c0 + 512], in_=bank[:tp, 0:512],
                                                               func=AF.Copy), [Rb], [R_dqb])
                            rope(g3(bank[:tp, 0:512], 64), g3(dqb[:tp, s, c0:c0 + 512], 64),
                                 g3(rt[:tp, s, 512:576], 8), g3(rt[:tp, s, 576:640], 8), tp, 8, 8,
                                 [Rb, R_rt], [R_dqb])
                        elif nb in (4, 5):
                            c0 = (nb - 4) * 512
                            oi = oring.next()
                            tr.op("act", lambda: S_.activation(out=o_blk[oi][:tp, :], in_=bank[:tp, 0:512],
                                                               func=AF.Copy), [Rb], [R_oblk[oi]])
                            if not DBG.get("norope", 0):
                                rope(g3(bank[:tp, 0:512], 64), g3(o_blk[oi][:tp, :], 64),
                                     g3(rt[:tp, s, 512:576], 8), g3(rt[:tp, s, 576:640], 8), tp, 8, 8,
                                     [Rb, R_rt] + ([R_oblk[oi]] if DBG.get("ropeser", 0) else []), [R_oblk[oi]])
                            tr.op("act", lambda: S_.activation(out=dkb[:tp, s, c0:c0 + 512], in_=o_blk[oi][:tp, :],
                                                               func=AF.Copy), [R_oblk[oi]], [R_dkb])
                            if outs is not None:
                                tr.dma("pool", outs["dk"][r0:r0 + tp, c0:c0 + 512], o_blk[oi][:tp, :], reads=[R_oblk[oi]])
                        else:
                            c0 = (nb - 6) * 512
                            h0 = (nb - 6) * 4
                            di = dring.next()
                            tr.op("dve", lambda: V.tensor_copy(dst[di][:tp, :, 0:128], g3(bank[:tp, 0:512], 128)),
                                  [Rb], [R_dst[di]])
                            kt = blk * 4 + s
                            tr.dma("pool", DV[h0:h0 + 4, 0:tp, kt, :].rearrange("h p c -> p h c"), dst[di][:tp, :, :],
                                   reads=[R_dst[di]], writes=[wres("DV")])
                            if outs is not None:
                                oi = oring.next()
                                tr.op("act", lambda: S_.activation(out=o_blk[oi][:tp, :], in_=bank[:tp, 0:512],
                                                                   func=AF.Copy), [Rb], [R_oblk[oi]])
                                tr.dma("pool", outs["dv"][r0:r0 + tp, c0:c0 + 512], o_blk[oi][:tp, :], reads=[R_oblk[oi]])
            else:
                cst = [sb(f"cst32_{i}", [128, 1024], F32) for i in range(2)]; R_c32 = [A("cst32_0"), A("cst32_1")]
                cring = Ring([0, 1])
                for s, tp in subs:
                    r0 = cache_rows + s * 128
                    kt = blk * 4 + s
                    ci = cring.next()
                    tr.dma("sp", cst[ci][:tp, 0:256], c_ckv[r0:r0 + tp, :], writes=[R_c32[ci]])
                    tr.dma("sp", cst[ci][:tp, 256:320], c_kr[r0:r0 + tp, :], writes=[R_c32[ci]])
                    copy("dve", ckvb[:tp, s, :], cst[ci][:tp, 0:256], [R_c32[ci]], [R_ckvb])
                    copy("dve", krb[:tp, s, :], cst[ci][:tp, 256:320], [R_c32[ci]], [R_krb])
                    ci = cring.next()
                    tr.dma("sp", cst[ci][:tp, :], c_dk[r0:r0 + tp, :], writes=[R_c32[ci]])
                    copy("act", dkb[:tp, s, :], cst[ci][:tp, :], [R_c32[ci]], [R_dkb])
                    ci = cring.next()
                    tr.dma("sp", cst[ci][:tp, :], c_dv[r0:r0 + tp, :], writes=[R_c32[ci]])
                    for half in range(2):
                        di = dring.next()
                        copy("dve", dst[di][:tp, :, 0:128], g3(cst[ci][:tp, half * 512:(half + 1) * 512], 128),
                             [R_c32[ci]], [R_dst[di]])
                        tr.dma("pool", DV[half * 4:half * 4 + 4, 0:tp, kt, :].rearrange("h p c -> p h c"), dst[di][:tp, :, :],
                               reads=[R_dst[di]], writes=[wres("DV")])

            if DBG["cut"] <= 2:
                return
            def tr_group(src_of_s, width, emit_evac, rd):
                bi = psring.next()
                pbf = psb[bi][:].bitcast(BF16)

                def f():
                    ins = None
                    for s, tp in subs:
                        ins = P_.transpose(pbf[0:width, s * 128:s * 128 + tp], src_of_s(s, tp), ident[:tp, :tp])
                    return ins
                tr.op("pe", f, reads=rd + [R_ident], writes=[R_ps[bi]])
                emit_evac(pbf[0:width, 0:nt], R_ps[bi])

            for c in range(2):
                def ev(src, Rb, c=c):
                    copy(evac_eng(), ckvT[:, c, 0:nt], src, [Rb], [R_ckvT])
                tr_group(lambda s, tp, c=c: ckvb[:tp, s, c * 128:(c + 1) * 128], 128, ev, [R_ckvb])

            def ev_kr(src, Rb):
                ki = kring.next()
                copy(evac_eng(), kst[ki][0:64, 0:nt], src, [Rb], [R_kst[ki]])
                tr.dma("pool", KR[:, blk * T:blk * T + nt], kst[ki][0:64, 0:nt], reads=[R_kst[ki]], writes=[wres("KR")])
            tr_group(lambda s, tp: krb[:tp, s, :], 64, ev_kr, [R_krb])
            for h in range(8):
                def ev_dk(src, Rb, h=h):
                    ki = kring.next()
                    copy(evac_eng(), kst[ki][:, 0:nt], src, [Rb], [R_kst[ki]])
                    tr.dma("pool", DK[h, :, blk * T:blk * T + nt], kst[ki][:, 0:nt], reads=[R_kst[ki]], writes=[wres("DK")])
                tr_group(lambda s, tp, h=h: dkb[:tp, s, h * 128:(h + 1) * 128], 128, ev_dk, [R_dkb])
            if own:
                for h in range(8):
                    def ev_dq(src, Rb, h=h):
                        e = evac_eng()
                        copy(e, Q1p[0:64, h, 0:nt], src[0:64, :], [Rb], [R_Q12T])
                        copy(e, Q2p[64:128, h, 0:nt], src[64:128, :], [Rb], [R_Q12T])
                    tr_group(lambda s, tp, h=h: dqb[:tp, s, h * 128:(h + 1) * 128], 128, ev_dq, [R_dqb])
                for c in range(4):
                    def ev_ql(src, Rb, c=c):
                        tr.op("act", lambda: S_.activation(out=qlatT[:, c, 0:nt], in_=src, func=AF.Copy,
                                                           scale=gq_c[:, c:c + 1]), [Rb, R_cst], [R_qlatT])
                    tr_group(lambda s, tp, c=c: qlb[:tp, s, c * 128:(c + 1) * 128], 128, ev_ql, [R_qlb])

            if DBG["cut"] <= 3:
                return
            wv, R_wv = wload(WUKV, "WUKV", 4096)
            wk3 = wv.rearrange("p (kc n) -> p kc n", n=2048)
            for h in range(8):
                bi = psring.next()
                bank = psb[bi]

                def mmk(h=h, bank=bank):
                    ins = None
                    for kc in range(2):
                        ins = P_.matmul(bank[:, 0:nt], wk3[:, kc, h * 128:(h + 1) * 128], ckvT[:, kc, 0:nt],
                                        start=(kc == 0), stop=(kc == 1))
                    return ins
                tr.op("pe", mmk, reads=[R_wv, R_ckvT], writes=[R_ps[bi]])
                ki = kring.next()
                copy(evac_eng(), kst[ki][:, 0:nt], bank[:, 0:nt], [R_ps[bi]], [R_kst[ki]])
                tr.dma("pool", KN[h, :, blk * T:blk * T + nt], kst[ki][:, 0:nt], reads=[R_kst[ki]], writes=[wres("KN")])
            for s, tp in subs:
                b = s % 2
                for half in range(2):
                    bi = psring.next()
                    bank = psb[bi]

                    def mmv(s=s, tp=tp, half=half, bank=bank):
                        ins = None
                        for kc in range(2):
                            ins = P_.matmul(bank[:tp, 0:512], ckvT[:, kc, s * 128:s * 128 + tp],
                                            wk3[:, kc, 1024 + half * 512:1024 + (half + 1) * 512],
                                            start=(kc == 0), stop=(kc == 1))
                        return ins
                    tr.op("pe", mmv, reads=[R_wv, R_ckvT], writes=[R_ps[bi]])
                    copy(evac_eng(), vst[b][:tp, half * 4:half * 4 + 4, 0:128], g3(bank[:tp, 0:512], 128),
                         [R_ps[bi]], [R_vst[b]])
                kt = blk * 4 + s
                tr.dma("pool", VM[:, 0:tp, kt, :].rearrange("h p c -> p h c"), vst[b][:tp, :, :],
                       reads=[R_vst[b]], writes=[wres("VM")])

            if own:
                wv, R_wv = wload(WUQ, "WUQ", 6144)
                wq3 = wv.rearrange("p (kc n) -> p kc n", n=1536)
                for h in range(8):
                    bi = psring.next()
                    bank = psb[bi]

                    def mmq(h=h, bank=bank):
                        ins = None
                        for kc in range(4):
                            ins = P_.matmul(bank[:, 0:nt], wq3[:, kc, h * 128:(h + 1) * 128], qlatT[:, kc, 0:nt],
                                            start=(kc == 0), stop=(kc == 3))
                        return ins
                    tr.op("pe", mmq, reads=[R_wv, R_qlatT], writes=[R_ps[bi]])
                    copy(evac_eng(), QnT[:, h, 0:nt], bank[:, 0:nt], [R_ps[bi]], [R_QnT])
                for s, tp in subs:
                    bi = psring.next()
                    bank = psb[bi]

                    def mmr(s=s, tp=tp, bank=bank):
                        ins = None
                        for kc in range(4):
                            ins = P_.matmul(bank[:tp, 0:512], qlatT[:, kc, s * 128:s * 128 + tp], wq3[:, kc, 1024:1536],
                                            start=(kc == 0), stop=(kc == 3))
                        return ins
                    tr.op("pe", mmr, reads=[R_wv, R_qlatT], writes=[R_ps[bi]])
                    rope(g3(bank[:tp, 0:512], 64), g3(qrb[:tp, s, :], 64),
                         g3(rt[:tp, s, 0:256], 32), g3(rt[:tp, s, 256:512], 32), tp, 8, 32,
                         [R_ps[bi], R_rt], [R_qrb])
                for h in range(8):
                    def ev_qr(src, Rb, h=h):
                        copy(evac_eng(), QrT[:, h, 0:nt], src, [Rb], [R_QrT])
                    tr_group(lambda s, tp, h=h: qrb[:tp, s, h * 64:(h + 1) * 64], 64, ev_qr, [R_qrb])

    def attention(nt, ktiles, outAT, R_oA, outBT, R_oB):
        subs = [(s, min(128, nt - s * 128)) for s in range((nt + 127) // 128)]
        A = lambda n: Res(n, arena=True)
        NB = 3
        kn_s = [sb(f"kn_s{i}", [128, 1024], BF16) for i in range(NB)]
        kr_s = [sb(f"kr_s{i}", [64, 1024], BF16) for i in range(NB)]
        v_s = [sb(f"v_s{i}", [128, 8, 129], BF16) for i in range(NB)]
        R_kn = [A(f"skn{i}") for i in range(NB)]; R_kr = [A(f"skr{i}") for i in range(NB)]; R_v = [A(f"sv{i}") for i in range(NB)]
        sring = Ring(list(range(NB)))
        NPB = 4
        pt = [sb(f"pt{i}", [128, 2, T], BF16) for i in range(NPB)]
        R_pt = [A(f"pt{i}") for i in range(NPB)]
        pring = Ring(list(range(NPB)))
        otok = [sb(f"otok{i}", [128, 4, 128], BF16) for i in range(2)]; R_otok = [A("otok0"), A("otok1")]
        oring = Ring([0, 1])
        o1 = sb("o1", [128, 128], F32); R_o1 = A("o1")
        o2 = sb("o2", [128, 4, 128], F32); R_o2 = A("o2")
        rz = sb("rz", [128, 32], F32); R_rz = A("rz")
        rz2 = sb("rz2", [128, 8], F32); R_rz2 = A("rz2")
        accM = [(2, 0), (2, 129), (2, 258), (3, 0)]
        acc1 = [(4, 0), (4, 129), (4, 258), (5, 0)]
        acc2 = [(6, 0), (6, 129), (6, 258), (5, 129)]
        sbank = Ring([0, 1, 7])
        groups = []
        for kt in ktiles:
            if groups and len(groups[-1]) < 8 and groups[-1][-1][0] + 1 == kt[0] and kt[0] % 8 != 0:
                groups[-1].append(kt)
            else:
                groups.append([kt])

        def acc_banks(accs):
            return sorted(set(b for b, _ in accs[:len(subs)]))

        def zero_acc(accs):
            for b in acc_banks(accs):
                cols = [c for bb, c in accs[:len(subs)] if bb == b]
                c0, c1 = min(cols), max(cols) + 129
                tr.op("dve", lambda: V.memset(psb[b][:, c0:c1], 0.0), writes=[R_ps[b]])

        def zero_branch(branch):
            if branch == "M":
                zero_acc(accM)
            else:
                zero_acc(acc1)
                zero_acc(acc2)

        zero_branch("M")
        zero_branch("D")
        deferred = []
        pend = None
        for h in range(DBG.get("att_heads", 8)):
            for branch in ("M", "D")[:DBG.get("att_br", 2)]:
                accl = [accM] if branch == "M" else [acc1, acc2]
                R_accw = [R_ps[b] for accs in accl for b in acc_banks(accs)]
                nstep = 0
                for grp in groups:
                    kt0 = grp[0][0]
                    nk = len(grp)
                    si = sring.next()
                    key0 = kt0 * 128
                    nkeys = sum(g[1] for g in grp)
                    pl = max(g[1] for g in grp)
                    if branch == "M":
                        tr.dma("sp", kn_s[si][:, 0:nkeys], KN[h, :, key0:key0 + nkeys], reads=[wres("KN")], writes=[R_kn[si]])
                        tr.dma("sp", kr_s[si][:, 0:nkeys], KR[:, key0:key0 + nkeys], reads=[wres("KR")], writes=[R_kr[si]])
                        tr.dma("sp", v_s[si][:pl, 0:nk, :], VM[h, 0:pl, kt0:kt0 + nk, :], reads=[wres("VM")], writes=[R_v[si]])
                    else:
                        tr.dma("sp", kn_s[si][:, 0:nkeys], DK[h, :, key0:key0 + nkeys], reads=[wres("DK")], writes=[R_kn[si]])
                        tr.dma("sp", v_s[si][:pl, 0:nk, :], DV[h, 0:pl, kt0:kt0 + nk, :], reads=[wres("DV")], writes=[R_v[si]])
                    for gi, (kt, kp, mode, j) in enumerate(grp):
                        q0 = 128 * j if mode == "diag" else 0
                        ko = gi * 128
                        pi = pring.next()
                        bias = obias if mode == "bias" else zcol
                        if branch == "M":
                            b0 = sbank.next()

                            def mms(b0=b0, ko=ko, kp=kp, q0=q0, si=si):
                                P_.matmul(psb[b0][:kp, q0:nt], kn_s[si][:, ko:ko + kp], QnT[:, h, q0:nt], start=True, stop=False)
                                return P_.matmul(psb[b0][:kp, q0:nt], kr_s[si][:, ko:ko + kp], QrT[:, h, q0:nt],
                                                 start=False, stop=True)
                            tr.op("pe", mms, reads=[R_kn[si], R_kr[si], R_QnT, R_QrT], writes=[R_ps[b0]])
                            tr.op("act", lambda: S_.activation(out=pt[pi][:kp, 0, q0:nt], in_=psb[b0][:kp, q0:nt], func=AF.Exp,
                                                               scale=SC_MLA, bias=bias[:kp]),
                                  reads=[R_ps[b0], R_cst], writes=[R_pt[pi]])
                            nmap = 1
                        else:
                            for qh in range(2):
                                qa, qb = max(q0, qh * 256), min(nt, (qh + 1) * 256)
                                if qa >= qb:
                                    continue
                                w = qb - qa
                                b0 = sbank.next()

                                def mms(b0=b0, ko=ko, kp=kp, qa=qa, qb=qb, w=w, si=si):
                                    P_.matmul(psb[b0][:kp, 0:w], kn_s[si][:, ko:ko + kp], Q1p[:, h, qa:qb], start=True, stop=True)
                                    return P_.matmul(psb[b0][:kp, 256:256 + w], kn_s[si][:, ko:ko + kp], Q2p[:, h, qa:qb],
                                                     start=True, stop=True, skip_group_check=True)
                                tr.op("pe", mms, reads=[R_kn[si], R_Q12T], writes=[R_ps[b0]])
                                src = psb[b0][:kp, :].rearrange("p (m q) -> p m q", q=256)[:, :, 0:w]
                                tr.op("act", lambda: S_.activation(out=pt[pi][:kp, :, qa:qb], in_=src, func=AF.Exp,
                                                                   scale=SC_DIFF, bias=bias[:kp]),
                                      reads=[R_ps[b0], R_cst], writes=[R_pt[pi]])
                            nmap = 2
                        if mode == "diag":
                            tr.op("dve", lambda: V.memset(pt[pi][64:128, 0:nmap, q0:q0 + 64], 0.0), writes=[R_pt[pi]])

                        def pv(pi=pi, kp=kp, j=j, mode=mode, gi=gi, accl=accl, si=si, R_accw=R_accw):
                            def f():
                                ins = None
                                for m, accs in enumerate(accl):
                                    for s, tp in subs:
                                        if mode == "diag" and s < j:
                                            continue
                                        bk, c0 = accs[s]
                                        ins = P_.matmul(psb[bk][:tp, c0:c0 + 129], pt[pi][:kp, m, s * 128:s * 128 + tp],
                                                        v_s[si][:kp, gi, :], start=False, stop=False, skip_group_check=True)
                                return ins
                            tr.op("pe", f, reads=[R_pt[pi], R_v[si]], writes=R_accw)
                        if pend is not None:
                            pend()
                        pend = pv
                        nstep += 1
                        while deferred and deferred[0][0] <= nstep:
                            deferred.pop(0)[1]()
                while deferred:
                    deferred.pop(0)[1]()
                oi = oring.next()

                def ev1(h=h, branch=branch, oi=oi):
                    for s, tp in subs:
                        r = rz[:, s * 8:s * 8 + 8]
                        if branch == "M":
                            bk, c0 = accM[s]
                            tr.op("dve", lambda: V.reciprocal(r[:tp, 0:1], psb[bk][:tp, c0 + 128:c0 + 129]), [R_ps[bk]], [R_rz])
                            tr.op("dve", lambda: V.tensor_scalar(otok[oi][:tp, s, :], psb[bk][:tp, c0:c0 + 128], r[:tp, 0:1], None, ALU.mult),
                                  [R_ps[bk], R_rz], [R_otok[oi]])
                        else:
                            bk1, c1 = acc1[s]
                            bk2, c2 = acc2[s]
                            tr.op("dve", lambda: V.reciprocal(r[:tp, 1:2], psb[bk1][:tp, c1 + 128:c1 + 129]), [R_ps[bk1]], [R_rz])
                            tr.op("dve", lambda: V.reciprocal(r[:tp, 2:3], psb[bk2][:tp, c2 + 128:c2 + 129]), [R_ps[bk2]], [R_rz])
                            tr.op("dve", lambda: V.tensor_tensor(r[:tp, 3:4], r[:tp, 2:3], nlam[:tp, :], ALU.mult), [R_rz, R_cst], [R_rz])
                            tr.op("dve", lambda: V.tensor_scalar(o1[:tp, :], psb[bk1][:tp, c1:c1 + 128], r[:tp, 1:2], None, ALU.mult),
                                  [R_ps[bk1], R_rz], [R_o1])
                            tr.op("dve", lambda: V.scalar_tensor_tensor(o2[:tp, s, :], psb[bk2][:tp, c2:c2 + 128], r[:tp, 3:4],
                                                                        o1[:tp, :], ALU.mult, ALU.add),
                                  [R_ps[bk2], R_rz, R_o1], [R_o2])
                    if branch == "M" and h < 7:
                        zero_branch(branch)

                def ev2(h=h, branch=branch, oi=oi):
                    if branch == "M":
                        return
                    for s, tp in subs:
                        tr.op("act", lambda: S_.activation(out=o1[:tp, :], in_=o2[:tp, s, :], func=AF.Square,
                                                           accum_out=rz2[:tp, s:s + 1]), [R_o2], [R_o1, R_rz2])
                    tp0 = subs[0][1]
                    ns_ = len(subs)
                    tr.op("act", lambda: S_.activation(out=rz2[:tp0, 0:ns_], in_=rz2[:tp0, 0:ns_], func=AF.Ln, scale=1.0 / 128, bias=EPS),
                          [R_rz2], [R_rz2])
                    tr.op("act", lambda: S_.activation(out=rz2[:tp0, 0:ns_], in_=rz2[:tp0, 0:ns_], func=AF.Exp, scale=-0.5),
                          [R_rz2], [R_rz2])
                    tr.op("dve", lambda: V.tensor_scalar(rz2[:tp0, 4:4 + ns_], rz2[:tp0, 0:ns_], 1.0 - LAMBDA_INIT, None, ALU.mult),
                          [R_rz2], [R_rz2])
                    for s, tp in subs:
                        tr.op("dve", lambda: V.scalar_tensor_tensor(otok[oi][:tp, s, :], o2[:tp, s, :], rz2[:tp, 4 + s:5 + s], gsub_b[:tp, :],
                                                                    ALU.mult, ALU.mult), [R_o2, R_rz2, R_cst], [R_otok[oi]])
                    if h < 7:
                        zero_branch(branch)

                def ev3(h=h, branch=branch, oi=oi):
                    b0 = sbank.next()
                    pbf = psb[b0][:].bitcast(BF16)

                    def tro():
                        ins = None
                        for s, tp in subs:
                            ins = P_.transpose(pbf[:, s * 128:s * 128 + tp], otok[oi][:tp, s, :], ident[:tp, :tp])
                        return ins
                    tr.op("pe", tro, reads=[R_otok[oi], R_ident], writes=[R_ps[b0]])
                    if branch == "M":
                        copy("dve", outAT[:, h, 0:nt], pbf[:, 0:nt], [R_ps[b0]], [R_oA])
                    else:
                        copy("dve", outBT[:, h, 0:nt], pbf[:, 0:nt], [R_ps[b0]], [R_oB])
                if not DBG.get("att_noevac", 0):
                    deferred.extend([[2, ev1], [4, ev2], [6, ev3]])
        if pend is not None:
            pend()
        while deferred:
            deferred.pop(0)[1]()

    def back(nt, x_src, y_dst, outAT, R_oA, outBT, R_oB, own_h):
        subs = [(s, min(128, nt - s * 128)) for s in range((nt + 127) // 128)]
        ns = len(subs)
        A = lambda n: Res(n, arena=True)
        set_psring(range(8))
        x2 = sb("x2", [128, ns, D], F32); R_x2 = [A(f"x2_{s}") for s in range(ns)]

        def norm_T(gcol, h_T, R_h):
            with Scope():
                norm_T_(gcol, h_T, R_h)

        def norm_T_(gcol, h_T, R_h):
            xb4 = sb("xb4b", [128, ns, D], BF16); R_xb4 = [A(f"xb4b_{s}") for s in range(ns)]
            for s, tp in subs:
                ss, R_ss = smcol()
                tr.op("act", lambda: S_.activation(out=xb4[:tp, s, :], in_=x2[:tp, s, :], func=AF.Square, accum_out=ss[:tp]),
                      reads=[R_x2[s]], writes=[R_xb4[s], R_ss])
                rstd_from_ss(ss, R_ss, D, tp)
                tr.op("act", lambda: S_.activation(out=xb4[:tp, s, :], in_=x2[:tp, s, :], func=AF.Copy, scale=ss[:tp]),
                      reads=[R_x2[s], R_ss], writes=[R_xb4[s]])
            for dc2 in range(8):
                bi = psring.next()
                pbf = psb[bi][:].bitcast(BF16)

                def trs(dc2=dc2, pbf=pbf):
                    ins = None
                    for u in range(2):
                        dc = dc2 * 2 + u
                        for s, tp in subs:
                            ins = P_.transpose(pbf[:, u * 512 + s * 128:u * 512 + s * 128 + tp],
                                               xb4[:tp, s, dc * 128:(dc + 1) * 128], ident[:tp, :tp])
                    return ins
                tr.op("pe", trs, reads=R_xb4 + [R_ident], writes=[R_ps[bi]])
                for u in range(2):
                    dc = dc2 * 2 + u
                    if dc2 % 2 == 0:
                        tr.op("act", lambda: S_.activation(out=h_T[:, dc, 0:nt], in_=pbf[:, u * 512:u * 512 + nt], func=AF.Copy,
                                                           scale=gcol[:, dc:dc + 1]), [R_ps[bi], R_cst], [R_h])
                    else:
                        tr.op("dve", lambda: V.tensor_scalar(h_T[:, dc, 0:nt], pbf[:, u * 512:u * 512 + nt],
                                                             gcol[:, dc:dc + 1], None, ALU.mult), [R_ps[bi], R_cst], [R_h])

        for s, tp in subs:
            tr.dma("sp", x2[:tp, s, :], x_src[s * 128:s * 128 + tp, :], writes=[R_x2[s]])
        if DBG.get("regate", 0):
            hi = hring.next()
            h_T, R_h = hT[hi], R_hT[hi]
            norm_T(gmix_c, h_T, R_h)
        else:
            h_T, R_h = hT[own_h], R_hT[own_h]

        with Scope():
            merged = sb("merged", [128, 16, T], BF16); R_mg = A("merged")
            sg = [sb(f"sg{i}", [128, T], F32) for i in range(4)]; R_sg = [A(f"sg{i}") for i in range(4)]
            order = []
            for g in range(4):
                order += [(WG[g], f"WG{g}", 8192, "ga", g), (WG[4 + g], f"WG{4 + g}", 8192, "gb", g),
                          (WBA[g], f"WBA{g}", 4096, "ya", g), (WBB[g], f"WBB{g}", 4096, "yb", g)]
            ws_ = wstream([o[:3] for o in order])
            pend = {}
            for oi, (wd, wn, ne, typ, g) in enumerate(order):
                wv, R_wv = next(ws_)
                nk = 16 if typ in ("ga", "gb") else 8
                w3 = wv.rearrange("p (kc n) -> p kc n", n=512)
                src, R_src = (h_T, R_h) if typ in ("ga", "gb") else ((outAT, R_oA) if typ == "ya" else (outBT, R_oB))
                for cc in range(4):
                    c = g * 4 + cc
                    bi = psring.next()
                    bank = psb[bi]

                    def mm(bank=bank, w3=w3, cc=cc, nk=nk, src=src):
                        ins = None
                        for kc in range(nk):
                            ins = P_.matmul(bank[:, 0:nt], w3[:, kc, cc * 128:(cc + 1) * 128], src[:, kc, 0:nt],
                                            start=(kc == 0), stop=(kc == nk - 1))
                        return ins
                    tr.op("pe", mm, reads=[R_wv, R_src], writes=[R_ps[bi]])
                    if typ == "ga":
                        tr.op("act", lambda: S_.activation(out=sg[cc][:, 0:nt], in_=bank[:, 0:nt], func=AF.Sigmoid),
                              [R_ps[bi]], [R_sg[cc]])
                        pend[("ga", cc)] = None
                    elif typ == "gb":
                        tr.op("act", lambda: S_.activation(out=merged[:, c, 0:nt], in_=bank[:, 0:nt], func=AF.Sigmoid),
                              [R_ps[bi]], [R_mg])
                    elif typ == "ya":
                        tr.op("dve", lambda: V.tensor_tensor(sg[cc][:, 0:nt], sg[cc][:, 0:nt], bank[:, 0:nt], ALU.mult),
                              [R_ps[bi], R_sg[cc]], [R_sg[cc]])
                    else:
                        tr.op("dve", lambda: V.tensor_tensor(merged[:, c, 0:nt], merged[:, c, 0:nt], bank[:, 0:nt], ALU.mult),
                              [R_ps[bi], R_mg], [R_mg])
                        tr.op("dve", lambda: V.tensor_tensor(merged[:, c, 0:nt], merged[:, c, 0:nt], sg[cc][:, 0:nt], ALU.add),
                              [R_mg, R_sg[cc]], [R_mg])
            ws_ = wstream([(WO[nb_], f"WO{nb_}", 8192) for nb_ in range(4)])
            for nb in range(4):
                wv, R_wv = next(ws_)
                w3 = wv.rearrange("p (kc n) -> p kc n", n=512)
                for s, tp in subs:
                    bi = psring.next()
                    bank = psb[bi]

                    def mm(bank=bank, w3=w3, s=s, tp=tp):
                        ins = None
                        for kc in range(16):
                            ins = P_.matmul(bank[:tp, 0:512], merged[:, kc, s * 128:s * 128 + tp], w3[:, kc, :],
                                            start=(kc == 0), stop=(kc == 15))
                        return ins
                    tr.op("pe", mm, reads=[R_wv, R_mg], writes=[R_ps[bi]])
                    tr.op("dve", lambda: V.tensor_tensor(x2[:tp, s, nb * 512:(nb + 1) * 512], x2[:tp, s, nb * 512:(nb + 1) * 512],
                                                         bank[:tp, 0:512], ALU.add), [R_ps[bi], R_x2[s]], [R_x2[s]])
        hi = hring.next()
        h2T, R_h2 = hT[hi], R_hT[hi]
        norm_T(gffn_c, h2T, R_h2)
        with Scope():
            actT = [sb(f"actT{i}", [128, 4, T], BF16) for i in range(2)]; R_act = [A("actT0"), A("actT1")]
            sl = [sb(f"silu{i}", [128, T], F32) for i in range(2)]; R_sl = [A("silu0"), A("silu1")]
            order = []
            for g in range(11):
                order += [(WFG[g], f"WFG{g}", 8192, "g", g), (WFU[g], f"WFU{g}", 8192, "u", g), (WFO[g], f"WFO{g}", 8192, "o", g)]
            ws_ = wstream([o[:3] for o in order])
            for oi, (wd, wn, ne, typ, g) in enumerate(order):
                wv, R_wv = next(ws_)
                ab = g % 2
                if typ in ("g", "u"):
                    w3 = wv.rearrange("p (kc n) -> p kc n", n=512)
                    for cc in range(4):
                        bi = psring.next()
                        bank = psb[bi]

                        def mm(bank=bank, w3=w3, cc=cc):
                            ins = None
                            for kc in range(16):
                                ins = P_.matmul(bank[:, 0:nt], w3[:, kc, cc * 128:(cc + 1) * 128], h2T[:, kc, 0:nt],
                                                start=(kc == 0), stop=(kc == 15))
                            return ins
                        tr.op("pe", mm, reads=[R_wv, R_h2], writes=[R_ps[bi]])
                        if typ == "g":
                            tr.op("act", lambda: S_.activation(out=actT[ab][:, cc, 0:nt], in_=bank[:, 0:nt], func=AF.Silu),
                                  [R_ps[bi]], [R_act[ab]])
                        else:
                            tr.op("dve", lambda: V.tensor_tensor(actT[ab][:, cc, 0:nt], actT[ab][:, cc, 0:nt], bank[:, 0:nt],
                                                                 ALU.mult), [R_ps[bi], R_act[ab]], [R_act[ab]])
                else:
                    w3 = wv.rearrange("p (kc n) -> p kc n", n=2048)
                    for s, tp in subs:
                        for nb in range(4):
                            bi = psring.next()
                            bank = psb[bi]

                            def mm(bank=bank, w3=w3, s=s, tp=tp, nb=nb):
                                ins = None
                                for kc in range(4):
                                    ins = P_.matmul(bank[:tp, 0:512], actT[ab][:, kc, s * 128:s * 128 + tp],
                                                    w3[:, kc, nb * 512:(nb + 1) * 512], start=(kc == 0), stop=(kc == 3))
                                return ins
                            tr.op("pe", mm, reads=[R_wv, R_act[ab]], writes=[R_ps[bi]])
                            tr.op("dve", lambda: V.tensor_tensor(x2[:tp, s, nb * 512:(nb + 1) * 512],
                                                                 x2[:tp, s, nb * 512:(nb + 1) * 512], bank[:tp, 0:512], ALU.add),
                                  [R_ps[bi], R_x2[s]], [R_x2[s]])
        junk = [sb(f"fjunk{i}", [128, D], BF16) for i in range(2)]; R_junk = [A("fjunk0"), A("fjunk1")]
        for s, tp in subs:
            ss, R_ss = smcol()
            tr.op("act", lambda: S_.activation(out=junk[s % 2][:tp, :], in_=x2[:tp, s, :], func=AF.Square, accum_out=ss[:tp]),
                  reads=[R_x2[s]], writes=[R_junk[s % 2], R_ss])
            rstd_from_ss(ss, R_ss, D, tp)
            tr.op("dve", lambda: V.scalar_tensor_tensor(x2[:tp, s, :], x2[:tp, s, :], ss[:tp], gfin_b[:tp, :], ALU.mult, ALU.mult),
                  [R_x2[s], R_ss, R_cst], [R_x2[s]])
            tr.dma("pool", y_dst[s * 128:s * 128 + tp, :], x2[:tp, s, :], reads=[R_x2[s]])

    def own_tile(nt, ktiles, x_src, y_dst, own_h):
        with Scope():
            A = lambda n: Res(n, arena=True)
            outAT = sb("outAT", [128, 8, T], BF16); R_oA = A("outAT")
            outBT = sb("outBT", [128, 8, T], BF16); R_oB = A("outBT")
            with Scope():
                set_psring([0, 1, 2, 3])
                with (nc.named_scope("att") if DBG.get("scopes", 0) else contextlib.nullcontext()):
                    attention(nt, ktiles, outAT, R_oA, outBT, R_oB)
            if stage >= 4:
                with Scope():
                    with (nc.named_scope("back") if DBG.get("scopes", 0) else contextlib.nullcontext()):
                        back(nt, x_src, y_dst, outAT, R_oA, outBT, R_oB, own_h)

    emit_consts()
    emit_prep(1)
    import contextlib
    scope = (lambda n: nc.named_scope(n)) if DBG.get("scopes", 0) else (lambda n: contextlib.nullcontext())
    for i in range(npair):
        with scope(f"fo{i}"):
            front("other", T, 2 * i + 1, x_src=xk[i], rope_src=rope_k[i])
        if stage < 2:
            continue
        with scope(f"fw{i}"):
            front("own", T, 2 * i, x_src=xo[i], rope_src=rope_o[i],
                  outs={"ckv": ckv_o[i], "kr": kr_o[i], "dk": dk_o[i], "dv": dv_o[i]})
        own_h_i = last_h[0]
        if i == 0:
            emit_prep(2)
        if stage < 3:
            continue
        kts = [(kt, 128, "full", 0) for kt in range(8 * i)]
        kts += [(8 * i + j, 128, "diag", j) for j in range(4)]
        kts += [(8 * i + 4 + j, 128, "bias", 0) for j in range(4)]
        own_tile(T, kts, xo[i], yo[i], own_h_i)
    if do_sample:
        if npair == 0:
            emit_prep(2)
        front("cache", T, 16, cache_rows=0)
        front("cache", T, 17, cache_rows=512)
        front("own", 16, 18, x_src=xs, rope_src=rope_s,
              outs={"ckv": ckv_s, "kr": kr_s, "dk": dk_s, "dv": dv_s})
        kts = [(64 + j, 128, "full", 0) for j in range(8)] + [(72, 16, "full", 0)]
        own_tile(16, kts, xs, ys, last_h[0])
    tr.finish()
    while ctxs:
        ctxs.pop().__exit__(None, None, None)
    tr.close()
    return nc, tr


def _rope_table(pos):
    pos = pos.astype(np.float32)
    invM = (1.0 / (np.float32(10000.0) ** (np.arange(32, dtype=np.float32) / np.float32(32)))).astype(np.float32)
    invD = (1.0 / (np.float32(500000.0) ** (np.arange(8, dtype=np.float32) / np.float32(8)))).astype(np.float32)
    aM = (pos[:, None] * invM[None, :]).astype(np.float32)
    aD = (pos[:, None] * invD[None, :]).astype(np.float32)
    cM, sM = np.cos(aM).astype(np.float32), np.sin(aM).astype(np.float32)
    cD, sD = np.cos(aD).astype(np.float32), np.sin(aD).astype(np.float32)
    return np.concatenate([np.tile(cM, (1, 8)), np.tile(sM, (1, 8)), np.tile(cD, (1, 8)), np.tile(sD, (1, 8))],
                          axis=1).astype(np.float32)


_PROG = {}


def kernel(x_prompt, x_sample, cache_mla_ckv, cache_mla_krope, cache_diff_k, cache_diff_v,
           norm_mix, w_in, mla_q_norm, mla_w_uq, mla_kv_norm, mla_w_ukv,
           diff_lq1, diff_lk1, diff_lq2, diff_lk2, diff_subln,
           w_branch_a, w_branch_b, w_out, norm_ffn, w_ffn_in, w_ffn_out, norm_final,
           _npair=8, _do_sample=True, _stage=9):
    f = lambda a: np.ascontiguousarray(np.asarray(a, dtype=np.float32))
    key = (_npair, _do_sample, _stage)
    if key not in _PROG:
        _PROG[key] = build_program(_npair, _do_sample, _stage)[0]
    nc = _PROG[key]
    x_prompt = f(x_prompt); x_sample = f(x_sample)
    shared = {
        "w_in": f(w_in)[0], "w_uq": f(mla_w_uq)[0], "w_ukv": f(mla_w_ukv)[0], "w_ba": f(w_branch_a)[0],
        "w_bb": f(w_branch_b)[0], "w_out": f(w_out)[0], "w_fi": f(w_ffn_in)[0], "w_fo": f(w_ffn_out)[0],
        "g_mix": f(norm_mix)[0], "g_q": f(mla_q_norm)[0], "g_kv": f(mla_kv_norm)[0],
        "lq1": f(diff_lq1)[0], "lk1": f(diff_lk1)[0], "lq2": f(diff_lq2)[0], "lk2": f(diff_lk2)[0],
        "g_sub": f(diff_subln)[0], "g_ffn": f(norm_ffn)[0], "g_fin": f(norm_final),
        "ident": np.eye(128, dtype=np.float32).astype(ml_dtypes.bfloat16),
    }
    rope_all = _rope_table(np.arange(8192))
    rope_s = _rope_table(1024 + np.arange(16))
    in_maps = []
    for c in range(8):
        b, e = c // 2, c % 2
        xt = x_prompt[b].reshape(16, T, D)
        rt = rope_all.reshape(16, T, RW)
        m = dict(shared)
        m["xo"] = np.ascontiguousarray(xt[e::2]); m["xk"] = np.ascontiguousarray(xt[1 - e::2])
        m["rope_o"] = np.ascontiguousarray(rt[e::2]); m["rope_k"] = np.ascontiguousarray(rt[1 - e::2])
        m["xs"] = x_sample[c]; m["rope_s"] = rope_s
        m["c_ckv"] = f(cache_mla_ckv)[0, c]; m["c_kr"] = f(cache_mla_krope)[0, c]
        m["c_dk"] = f(cache_diff_k)[0, c].reshape(1024, 1024); m["c_dv"] = f(cache_diff_v)[0, c].reshape(1024, 1024)
        m["obias"] = np.full((128, 1), 0.0 if e == 1 else NEG, np.float32)
        in_maps.append(m)
    res = run_bass_kernel_spmd(nc, in_maps, core_ids=list(range(8)))
    R = res.results
    y_p = np.zeros((4, 8192, D), np.float32); ckv_p = np.zeros((1, 4, 8192, 256), np.float32)
    kr_p = np.zeros((1, 4, 8192, 64), np.float32); dk_p = np.zeros((1, 4, 8192, 8, 128), np.float32)
    dv_p = np.zeros((1, 4, 8192, 8, 128), np.float32)
    y_s = np.zeros((8, 16, D), np.float32); ckv_sm = np.zeros((1, 8, 16, 256), np.float32)
    kr_sm = np.zeros((1, 8, 16, 64), np.float32); dk_sm = np.zeros((1, 8, 16, 8, 128), np.float32)
    dv_sm = np.zeros((1, 8, 16, 8, 128), np.float32)
    for c in range(8):
        b, e = c // 2, c % 2
        r = R[c]
        y_p[b].reshape(16, T, D)[e::2] = r["yo"]
        ckv_p[0, b].reshape(16, T, 256)[e::2] = r["ckv_o"]
        kr_p[0, b].reshape(16, T, 64)[e::2] = r["kr_o"]
        dk_p[0, b].reshape(16, T, 1024)[e::2] = r["dk_o"]
        dv_p[0, b].reshape(16, T, 1024)[e::2] = r["dv_o"]
        y_s[c] = r["ys"]; ckv_sm[0, c] = r["ckv_s"]; kr_sm[0, c] = r["kr_s"]
        dk_sm[0, c] = r["dk_s"].reshape(16, 8, 128); dv_sm[0, c] = r["dv_s"].reshape(16, 8, 128)
    return (y_p, y_s, ckv_p, kr_p, dk_p, dv_p, ckv_sm, kr_sm, dk_sm, dv_sm)
```
